# Optimizing a Trainium2 kernel written in Bass

```python
import jax, jax.numpy as jnp
from jax import lax
import numpy as np

D_MODEL = 2048
BATCH = 2
SEQ = 8192
DEPTH = 4

GRID_W = 64
CTX_LEN = 256
N_MIXERS = 3
EXPAND = 2
D_BRANCH = EXPAND * D_MODEL
FNET_GROUPS = 16
FNET_GROUP_DIM = D_BRANCH // FNET_GROUPS
ATTN_HEAD_DIM = 64
ATTN_Q_HEADS = D_BRANCH // ATTN_HEAD_DIM
ATTN_KV_HEADS = 8
ATTN_GROUP = ATTN_Q_HEADS // ATTN_KV_HEADS
WINDOW = 128
ATTN_BLOCK = 128
ROPE_BASE = 10000.0
GMLP_CHUNK = 128
GMLP_GROUPS = 16
GMLP_GROUP_DIM = D_BRANCH // GMLP_GROUPS
EPS = 1e-6
NEG_INF = -1e30

kernel_name = 'hybrid_fnet_swa_gmlp_prefix_dit'


def _rmsnorm(x, g):
    x32 = x.astype(jnp.float32)
    y = x32 * lax.rsqrt(jnp.mean(x32 * x32, axis=-1, keepdims=True) + EPS)
    return (y * g.astype(jnp.float32)).astype(x.dtype)


def _layernorm(x, g, b):
    x32 = x.astype(jnp.float32)
    mu = jnp.mean(x32, axis=-1, keepdims=True)
    var = jnp.mean(jnp.square(x32 - mu), axis=-1, keepdims=True)
    y = (x32 - mu) * lax.rsqrt(var + EPS)
    return (y * g.astype(jnp.float32) + b.astype(jnp.float32)).astype(x.dtype)


def _rope_half(xp, pos):
    nf = xp.shape[-1] // 2
    inv = ROPE_BASE ** (-jnp.arange(nf, dtype=jnp.float32) / nf)
    ang = pos[:, None] * inv[None, :]
    cos = jnp.cos(ang)[None, :, None, :]
    sin = jnp.sin(ang)[None, :, None, :]
    x1, x2 = xp[..., :nf], xp[..., nf:]
    return jnp.concatenate([x1 * cos - x2 * sin, x1 * sin + x2 * cos], axis=-1)


def _axial_rope(x, rows, cols):
    half = x.shape[-1] // 2
    x32 = x.astype(jnp.float32)
    out = jnp.concatenate([_rope_half(x32[..., :half], rows), _rope_half(x32[..., half:], cols)], axis=-1)
    return out.astype(x.dtype)


def _fourier_branch(h, w_in, w_mix):
    b, n, _ = h.shape
    u, z = jnp.split(h @ w_in, 2, axis=-1)
    ug = u.reshape(b, n, FNET_GROUPS, FNET_GROUP_DIM).astype(jnp.float32)
    f = jnp.fft.fft2(ug, axes=(1, 3), norm='ortho').real.astype(h.dtype)
    f = jnp.einsum('bngc,gcd->bngd', f, w_mix).reshape(b, n, D_BRANCH)
    return f * jax.nn.silu(z)


def _gmlp_branch(h, w_in, w_s, b_s, ln_g, ln_b):
    b, n, _ = h.shape
    uv, z = jnp.split(h @ w_in, [2 * D_BRANCH], axis=-1)
    u, v = jnp.split(jax.nn.gelu(uv), 2, axis=-1)
    v = _layernorm(v, ln_g, ln_b)
    vc = v.reshape(b, n // GMLP_CHUNK, GMLP_CHUNK, GMLP_GROUPS, GMLP_GROUP_DIM)
    s = jnp.einsum('gst,bktgc->bksgc', w_s, vc) + b_s.T[:, :, None]
    return u * s.reshape(b, n, D_BRANCH) * jax.nn.silu(z)


def _attend_with_sink(q, k, v, sink, mask=None):
    s = jnp.einsum('bqhgd,bkhd->bhgqk', q, k).astype(jnp.float32) * (ATTN_HEAD_DIM ** -0.5)
    if mask is not None:
        s = jnp.where(mask, s, NEG_INF)
    sk = jnp.broadcast_to(sink.astype(jnp.float32)[None, :, :, None, None], s.shape[:-1] + (1,))
    p = jax.nn.softmax(jnp.concatenate([s, sk], axis=-1), axis=-1)[..., :-1]
    return jnp.einsum('bhgqk,bkhd->bqhgd', p.astype(v.dtype), v)


def _attention_branch(h, hc, w_in, sink, rows, cols, need_ctx):
    b, n, _ = h.shape
    lc = hc.shape[1]
    kvw = ATTN_KV_HEADS * ATTN_HEAD_DIM
    q, k, v, z = jnp.split(h @ w_in, [D_BRANCH, D_BRANCH + kvw, D_BRANCH + 2 * kvw], axis=-1)
    kc, vc = jnp.split(hc @ w_in[:, D_BRANCH:D_BRANCH + 2 * kvw], 2, axis=-1)
    kc = kc.reshape(b, lc, ATTN_KV_HEADS, ATTN_HEAD_DIM)
    vc = vc.reshape(b, lc, ATTN_KV_HEADS, ATTN_HEAD_DIM)
    sink = sink.reshape(ATTN_KV_HEADS, ATTN_GROUP)
    q = _axial_rope(q.reshape(b, n, ATTN_Q_HEADS, ATTN_HEAD_DIM), rows, cols)
    k = _axial_rope(k.reshape(b, n, ATTN_KV_HEADS, ATTN_HEAD_DIM), rows, cols)
    v = v.reshape(b, n, ATTN_KV_HEADS, ATTN_HEAD_DIM)
    nb = n // ATTN_BLOCK
    qb = q.reshape(b, nb, ATTN_BLOCK, ATTN_KV_HEADS, ATTN_GROUP, ATTN_HEAD_DIM)

    def band(t):
        tp = jnp.pad(t, ((0, 0), (ATTN_BLOCK, ATTN_BLOCK), (0, 0), (0, 0)))
        tp = tp.reshape(b, nb + 2, ATTN_BLOCK, ATTN_KV_HEADS, ATTN_HEAD_DIM)
        return jnp.concatenate([tp[:, :-2], tp[:, 1:-1], tp[:, 2:]], axis=2)

    kw, vw = band(k), band(v)
    q_off = jnp.arange(ATTN_BLOCK)
    k_off = jnp.arange(3 * ATTN_BLOCK) - ATTN_BLOCK
    ctx_mask = jnp.ones((ATTN_BLOCK, lc), dtype=bool)

    def block(args):
        qi, ki, vi, blk = args
        qpos = blk * ATTN_BLOCK + q_off
        kpos = blk * ATTN_BLOCK + k_off
        valid = (jnp.abs(qpos[:, None] - kpos[None, :]) <= WINDOW) & ((kpos >= 0) & (kpos < n))[None, :]
        mask = jnp.concatenate([valid, ctx_mask], axis=1)
        keys = jnp.concatenate([ki, kc], axis=1)
        vals = jnp.concatenate([vi, vc], axis=1)
        return _attend_with_sink(qi, keys, vals, sink, mask)

    o = lax.map(block, (jnp.moveaxis(qb, 1, 0), jnp.moveaxis(kw, 1, 0), jnp.moveaxis(vw, 1, 0), jnp.arange(nb)))
    y = jnp.moveaxis(o, 0, 1).reshape(b, n, D_BRANCH) * jax.nn.silu(z)
    if not need_ctx:
        return y, None
    qc = (hc @ w_in[:, :D_BRANCH]).reshape(b, lc, ATTN_KV_HEADS, ATTN_GROUP, ATTN_HEAD_DIM)
    zc = hc @ w_in[:, D_BRANCH + 2 * kvw:]
    oc = _attend_with_sink(qc, kc, vc, sink).reshape(b, lc, D_BRANCH)
    return y, oc * jax.nn.silu(zc)


def setup_inputs(seed: int = 0) -> dict:
    key = jax.random.key(seed)
    ks = jax.random.split(key, 18)
    n_of = [len(range(m, DEPTH, N_MIXERS)) for m in range(N_MIXERS)]
    kvw2 = 2 * ATTN_KV_HEADS * ATTN_HEAD_DIM

    def nrm(k, shape, s):
        return jax.random.normal(k, shape, jnp.float32) * s

    return {
        'x': nrm(ks[0], (BATCH, SEQ, D_MODEL), 1.0),
        'c': nrm(ks[1], (BATCH, D_MODEL), 1.0),
        'ctx': nrm(ks[2], (BATCH, CTX_LEN, D_MODEL), 1.0),
        'c_ctx': nrm(ks[3], (D_MODEL,), 1.0),
        'norm_g': 1.0 + nrm(ks[4], (DEPTH, D_MODEL), 0.02),
        'ada_w': nrm(ks[5], (DEPTH, D_MODEL, 3 * D_MODEL), 0.5 * D_MODEL ** -0.5),
        'ada_b': nrm(ks[6], (DEPTH, 3 * D_MODEL), 0.02),
        'w_out': nrm(ks[7], (DEPTH, D_BRANCH, D_MODEL), D_BRANCH ** -0.5),
        'fnet_w_in': nrm(ks[8], (n_of[0], D_MODEL, 2 * D_BRANCH), D_MODEL ** -0.5),
        'fnet_w_mix': nrm(ks[9], (n_of[0], FNET_GROUPS, FNET_GROUP_DIM, FNET_GROUP_DIM), FNET_GROUP_DIM ** -0.5),
        'attn_w_in': nrm(ks[10], (n_of[1], D_MODEL, 2 * D_BRANCH + kvw2), D_MODEL ** -0.5),
        'attn_sink': nrm(ks[11], (n_of[1], ATTN_Q_HEADS), 0.5),
        'gmlp_w_in': nrm(ks[12], (n_of[2], D_MODEL, 3 * D_BRANCH), D_MODEL ** -0.5),
        'gmlp_w_s': nrm(ks[13], (n_of[2], GMLP_GROUPS, GMLP_CHUNK, GMLP_CHUNK), GMLP_CHUNK ** -0.5),
        'gmlp_b_s': 1.0 + nrm(ks[14], (n_of[2], GMLP_GROUPS, GMLP_CHUNK), 0.02),
        'gmlp_ln_g': 1.0 + nrm(ks[15], (n_of[2], D_BRANCH), 0.02),
        'gmlp_ln_b': nrm(ks[16], (n_of[2], D_BRANCH), 0.02),
        'final_g': 1.0 + nrm(ks[17], (D_MODEL,), 0.02),
    }


def reference(x, c, ctx, c_ctx, norm_g, ada_w, ada_b, w_out, fnet_w_in, fnet_w_mix, attn_w_in, attn_sink,
              gmlp_w_in, gmlp_w_s, gmlp_b_s, gmlp_ln_g, gmlp_ln_b, final_g):
    n = x.shape[1]
    ROWS = n // GRID_W
    rows = jnp.repeat(jnp.arange(ROWS, dtype=jnp.float32), GRID_W)
    cols = jnp.tile(jnp.arange(GRID_W, dtype=jnp.float32), ROWS)
    xc = ctx
    for i in range(DEPTH):
        kind, j = i % N_MIXERS, i // N_MIXERS
        need_ctx = i < DEPTH - 1
        shift, scale, gate = jnp.split((jax.nn.silu(c) @ ada_w[i] + ada_b[i])[:, None, :], 3, axis=-1)
        h = _rmsnorm(x, norm_g[i]) * (1 + scale) + shift
        hc = None
        if need_ctx or kind == 1:
            shift_c, scale_c, gate_c = jnp.split(jax.nn.silu(c_ctx) @ ada_w[i] + ada_b[i], 3)
            hc = _rmsnorm(xc, norm_g[i]) * (1 + scale_c) + shift_c
        if kind == 0:
            y = _fourier_branch(h, fnet_w_in[j], fnet_w_mix[j])
            yc = _fourier_branch(hc, fnet_w_in[j], fnet_w_mix[j]) if need_ctx else None
        elif kind == 1:
            y, yc = _attention_branch(h, hc, attn_w_in[j], attn_sink[j], rows, cols, need_ctx)
        else:
            y = _gmlp_branch(h, gmlp_w_in[j], gmlp_w_s[j], gmlp_b_s[j], gmlp_ln_g[j], gmlp_ln_b[j])
            yc = (_gmlp_branch(hc, gmlp_w_in[j], gmlp_w_s[j], gmlp_b_s[j], gmlp_ln_g[j], gmlp_ln_b[j])
                  if need_ctx else None)
        x = x + gate * (y @ w_out[i])
        if need_ctx:
            xc = xc + gate_c * (yc @ w_out[i])
    return _rmsnorm(x, final_g)
```

```python
import contextlib
import math
import numpy as np
import ml_dtypes
import concourse.bass as bass
import concourse.mybir as mybir
from concourse.bass_utils import run_bass_kernel_spmd

F32 = mybir.dt.float32
BF16 = mybir.dt.bfloat16
AF = mybir.ActivationFunctionType
ALU = mybir.AluOpType
NPBF = ml_dtypes.bfloat16

PE, ACT, DVE, POOL, SP = "pe", "act", "dve", "pool", "sp"
ENGS = (PE, ACT, DVE, POOL, SP)

D_MODEL = 2048
D_BRANCH = 4096
EPS = 1e-6
ARENA_F32 = 46 * 1024


class Prog:
    def __init__(self, nc, stack):
        self.nc = nc
        self.stack = stack
        self.q = {e: [] for e in ENGS}
        self.nsem = 0
        self.esem = {e: self.sem("e_" + e) for e in (PE, ACT, DVE, POOL)}
        self.ecnt = {e: 0 for e in (PE, ACT, DVE, POOL)}
        self.dsems = []
        self.dcnt = {}

    def sem(self, name):
        self.nsem += 1
        return self.stack.enter_context(self.nc.semaphore(f"{name}_{self.nsem}"))

    def dsem(self, name="d"):
        s = self.sem(name)
        self.dsems.append(s)
        self.dcnt[id(s)] = 0
        return s

    def op(self, eng, fn, deps=(), signal=True):
        tok = None
        inc = None
        if signal:
            self.ecnt[eng] += 1
            tok = (self.esem[eng], self.ecnt[eng])
            inc = (self.esem[eng], 1)
        self.q[eng].append((tuple(d for d in deps if d is not None), fn, inc))
        return tok

    def dma(self, eng, sem, out, in_, deps=()):
        self.dcnt[id(sem)] += 16
        tok = (sem, self.dcnt[id(sem)])
        self.q[eng].append((tuple(d for d in deps if d is not None),
                            (lambda e, out=out, in_=in_: e.dma_start(out=out, in_=in_)), (sem, 16)))
        return tok

    def coll(self, sem, kind, src, dst, groups, deps=()):
        self.dcnt[id(sem)] += 1
        tok = (sem, self.dcnt[id(sem)])
        self.q[POOL].append((tuple(d for d in deps if d is not None),
                             (lambda e: e.collective_compute(kind, ALU.bypass, replica_groups=groups,
                                                             ins=[src.opt()], outs=[dst.opt()])), (sem, 1)))
        return tok

    def wait(self, eng, deps):
        self.q[eng].append((tuple(d for d in deps if d is not None), None, None))

    def all_tokens(self):
        toks = [(self.esem[e], self.ecnt[e]) for e in self.ecnt if self.ecnt[e] > 0]
        toks += [(s, self.dcnt[id(s)]) for s in self.dsems if self.dcnt[id(s)] > 0]
        return toks

    def barrier(self):
        toks = self.all_tokens()
        for e in ENGS:
            self.wait(e, toks)

    def emit(self):
        nc = self.nc
        with nc.Block() as block:
            def replay(name):
                def run(e):
                    seen = {}
                    for deps, fn, inc in self.q[name]:
                        for (s, v) in deps:
                            k = id(s)
                            if seen.get(k, 0) < v:
                                e.wait_ge(s, v)
                                seen[k] = v
                        if fn is None:
                            continue
                        ins = fn(e)
                        if inc is not None:
                            ins.then_inc(inc[0], inc[1])
                return run
            block.tensor(replay(PE))
            block.scalar(replay(ACT))
            block.vector(replay(DVE))
            block.gpsimd(replay(POOL))
            block.sync(replay(SP))


class Buf:
    def __init__(self, kb, ap):
        self.kb = kb
        self.ap = ap
        self.sem = kb.get_dsem()
        self.ready = None
        self.readers = []

    def load(self, src, eng=POOL, deps=(), dst=None):
        P = self.kb.P
        t = P.dma(eng, self.sem, self.ap if dst is None else dst, src, deps=list(self.readers) + list(deps))
        self.ready = t
        self.readers = []
        return t

    def wrote(self, tok):
        self.ready = tok
        self.readers = []

    def read(self, tok):
        if tok is not None:
            self.readers.append(tok)
            if len(self.readers) > 24:
                self.readers = self.readers[-24:]

    def store(self, dst, deps=(), src=None, eng=SP):
        P = self.kb.P
        t = P.dma(eng, self.sem, dst, self.ap if src is None else src, deps=[self.ready] + list(deps))
        self.readers.append(t)
        return t


class KB:
    def __init__(self, nc, stack, ext_in=(), ext_out=()):
        self.nc = nc
        self.P = Prog(nc, stack)
        self.arena = stack.enter_context(nc.sbuf_tensor("arena", [128, ARENA_F32], F32))
        self.banks = [stack.enter_context(nc.psum_tensor(f"bank{i}", [128, 512], F32)) for i in range(8)]
        self.bank_free = [None] * 8
        self.bank_i = 0
        self.off = 0
        self.dram = {}
        self.ext_in = set(ext_in)
        self.ext_out = set(ext_out)
        self.out_toks = []

    def get_dsem(self):
        if not hasattr(self, "sem_pool"):
            self.sem_pool = []
            self.sem_next = 0
        if self.sem_next >= len(self.sem_pool):
            self.sem_pool.append(self.P.dsem("b"))
        s = self.sem_pool[self.sem_next]
        self.sem_next += 1
        return s

    def D(self, name, shape=None, dt=None):
        if name in self.dram:
            return self.dram[name]
        kind = "ExternalInput" if name in self.ext_in else ("ExternalOutput" if name in self.ext_out else "Internal")
        t = self.nc.dram_tensor(name, list(shape), dt, kind=kind).ap()
        self.dram[name] = t
        return t

    def alloc(self, shape, dt):
        n = int(np.prod(shape))
        nf32 = (n * (2 if dt == BF16 else 4) + 3) // 4
        nf32 = (nf32 + 7) // 8 * 8
        assert self.off + nf32 <= ARENA_F32, (self.off, nf32, shape)
        ap = self.arena[:, self.off:self.off + nf32]
        self.off += nf32
        if dt == BF16:
            ap = ap.bitcast(BF16)
        ap = ap[:, 0:n]
        if len(shape) == 2:
            ap = ap.rearrange("p (a b) -> p a b", a=shape[0])
        elif len(shape) == 3:
            ap = ap.rearrange("p (a b c) -> p a b c", a=shape[0], b=shape[1])
        elif len(shape) == 4:
            ap = ap.rearrange("p (a b c d) -> p a b c d", a=shape[0], b=shape[1], c=shape[2])
        return ap

    def buf(self, shape, dt):
        return Buf(self, self.alloc(shape, dt))

    def new_phase(self):
        self.P.barrier()
        self.off = 0
        self.sem_next = 0
        self.bank_free = [None] * 8

    def bank(self):
        i = self.bank_i
        self.bank_i = (i + 1) % 8
        return i

    def gemm(self, K, M, N, Lsrc, Rsrc, resident, epi, l_dt=BF16, r_dt=BF16, sblk=512, kp=128, deps=()):
        P = self.P
        KC = K // kp
        assert K % kp == 0

        def view(src):
            return src.rearrange("(c p) x -> p c x", p=kp)

        RESX = M if resident == 'L' else N
        STRX = N if resident == 'L' else M
        res = self.buf([KC, RESX], BF16)
        res_ap = res.ap if kp == 128 else res.ap[0:kp]
        rsrc = Lsrc if resident == 'L' else Rsrc
        ssrc = Rsrc if resident == 'L' else Lsrc
        nres_toks = []
        for x0 in range(0, RESX, 1024):
            x1 = min(RESX, x0 + 1024)
            nres_toks.append(res.load(view(rsrc(x0, x1)), deps=deps, dst=res_ap[:, :, x0:x1]))
            res.readers = []
        sblk = min(sblk, STRX)
        nblk = (STRX + sblk - 1) // sblk
        sb = [self.buf([KC, sblk], BF16) for _ in range(min(2, nblk))]

        def issue(s):
            b = sb[s % len(sb)]
            x0 = s * sblk
            x1 = min(STRX, x0 + sblk)
            dst = (b.ap if kp == 128 else b.ap[0:kp])[:, :, 0:x1 - x0]
            b.load(view(ssrc(x0, x1)), deps=deps, dst=dst)

        issue(0)
        for s in range(nblk):
            if s + 1 < nblk:
                issue(s + 1)
            b = sb[s % len(sb)]
            bap = b.ap if kp == 128 else b.ap[0:kp]
            x0 = s * sblk
            x1 = min(STRX, x0 + sblk)
            if resident == 'L':
                tiles = [(m0, min(M, m0 + 128), n0, min(x1, n0 + 512))
                         for m0 in range(0, M, 128) for n0 in range(x0, x1, 512)]
            else:
                tiles = [(m0, min(x1, m0 + 128), n0, min(N, n0 + 512))
                         for m0 in range(x0, x1, 128) for n0 in range(0, N, 512)]
            for (m0, m1, n0, n1) in tiles:
                bi = self.bank()
                ps = self.banks[bi][0:m1 - m0, 0:n1 - n0]
                tok = None
                for c in range(KC):
                    if resident == 'L':
                        lt = res_ap[:, c, m0:m1]
                        rt = bap[:, c, n0 - x0:n1 - x0]
                    else:
                        lt = bap[:, c, m0 - x0:m1 - x0]
                        rt = res_ap[:, c, n0:n1]
                    d = [self.bank_free[bi], b.ready] + nres_toks if c == 0 else []
                    last = (c == KC - 1)
                    tok = P.op(PE, (lambda e, ps=ps, lt=lt, rt=rt, c=c, last=last:
                                    e.matmul(ps, lt, rt, start=(c == 0), stop=last)),
                               deps=d, signal=last)
                b.read(tok)
                res.read(tok)
                self.bank_free[bi] = epi(m0, m1 - m0, n0, n1 - n0, ps, tok)

    def stagers(self, n, shape, dt):
        return [self.buf(shape, dt) for _ in range(n)]

    def epi_act(self, dst_fn, func=AF.Copy, dt=BF16, nst=4, alt=True, bias_fn=None):
        P = self.P
        st = self.stagers(nst, [512], dt)
        cnt = [0]

        def epi(m0, msz, n0, nsz, ps, tok):
            b = st[cnt[0] % nst]
            use_dve = alt and func == AF.Copy and bias_fn is None and (cnt[0] % 2 == 1)
            cnt[0] += 1
            o = b.ap[0:msz, 0:nsz]
            deps = [tok] + list(b.readers)
            if use_dve:
                t = P.op(DVE, lambda e: e.tensor_copy(out=o, in_=ps), deps=deps)
            elif bias_fn is not None:
                bcol = bias_fn(m0, msz)
                t = P.op(ACT, lambda e: e.activation(out=o, in_=ps, func=func, bias=bcol), deps=deps)
            else:
                t = P.op(ACT, lambda e: e.activation(out=o, in_=ps, func=func), deps=deps)
            b.wrote(t)
            b.store(dst_fn(m0, msz, n0, nsz), src=o)
            return t
        return epi

    def load_const(self, src, shape, dt, eng=POOL):
        b = self.buf(shape, dt)
        b.load(src, eng=eng)
        return b

    def silu_cols(self, cc, scT):
        P = self.P
        a = self.load_const(cc.rearrange("(c p) x -> p c x", p=128), [16, 2], F32)
        o = self.buf([16, 2], BF16)
        t = P.op(ACT, lambda e: e.activation(out=o.ap, in_=a.ap, func=AF.Silu), deps=[a.ready])
        o.wrote(t)
        o.store(scT.rearrange("(c p) x -> p c x", p=128))

    def mod(self, ada_w, ada_b48, scT, modD):
        bias = self.load_const(ada_b48, [48], F32)
        epi = self.epi_act(lambda m0, ms, n0, ns: modD[m0:m0 + ms, n0:n0 + ns], func=AF.Identity, dt=F32,
                           bias_fn=lambda m0, ms: bias.ap[0:ms, m0 // 128:m0 // 128 + 1])
        self.gemm(2048, 6144, 2, lambda a, b: ada_w[:, a:b], lambda a, b: scT[:, a:b], 'R', epi, deps=[bias.ready])

    def mod_cols(self, modD, g16, col):
        P = self.P
        m = self.load_const(modD.rearrange("(j p) x -> p j x", p=128), [48, 2], F32)
        g = self.load_const(g16, [16], F32)
        A = self.buf([16], F32)
        t = P.op(DVE, lambda e: e.tensor_scalar(A.ap, m.ap[:, 16:32, col], 1.0, 1.0, ALU.add, ALU.mult),
                 deps=[m.ready])
        t = P.op(DVE, lambda e: e.tensor_tensor(A.ap, A.ap, g.ap, ALU.mult), deps=[t, g.ready])
        A.wrote(t)
        sh = self.buf([16], F32)
        t2 = P.op(DVE, lambda e: e.tensor_copy(out=sh.ap, in_=m.ap[:, 0:16, col]), deps=[m.ready])
        sh.wrote(t2)
        gt = self.buf([16], F32)
        t3 = P.op(DVE, lambda e: e.tensor_copy(out=gt.ap, in_=m.ap[:, 32:48, col]), deps=[m.ready])
        gt.wrote(t3)
        return A, sh, gt

    def norm(self, xT, Tn, A, sh, hT=None, outF=None, tile=512):
        P = self.P
        ones = self.buf([128], BF16)
        t1 = P.op(DVE, lambda e: e.memset(ones.ap, 1.0))
        ones.wrote(t1)
        epsb = self.buf([1], F32)
        t1 = P.op(DVE, lambda e: e.memset(epsb.ap, EPS))
        epsb.wrote(t1)
        tile = min(tile, Tn)
        xb = [self.buf([16, tile], F32) for _ in range(2)]
        sq = self.buf([16, tile], BF16)
        rs = self.buf([tile], F32)
        tmp = [self.buf([tile], F32) for _ in range(2)]
        odt = F32 if outF is not None else BF16
        ob = [self.buf([16, tile], odt) for _ in range(1 if outF is not None else 2)]
        dst = outF if outF is not None else hT
        assert Tn % tile == 0
        nt = Tn // tile
        xv = xT.rearrange("(c p) t -> p c t", p=128)
        dv = dst.rearrange("(c p) t -> p c t", p=128)
        xb[0].load(xv[:, :, 0:tile])
        for i in range(nt):
            if i + 1 < nt:
                xb[(i + 1) % 2].load(xv[:, :, (i + 1) * tile:(i + 2) * tile])
            x = xb[i % 2]
            o = ob[i % len(ob)]
            t = P.op(ACT, lambda e, x=x: e.activation(out=sq.ap, in_=x.ap, func=AF.Square),
                     deps=[x.ready] + sq.readers)
            sq.wrote(t)
            bi = self.bank()
            ps = self.banks[bi][:, 0:tile]
            for c in range(16):
                tk = P.op(PE, lambda e, c=c, ps=ps: e.matmul(ps, ones.ap, sq.ap[:, c, :], start=(c == 0), stop=(c == 15)),
                          deps=[sq.ready, ones.ready, self.bank_free[bi]] if c == 0 else [], signal=(c == 15))
            sq.read(tk)
            t0_ = P.op(ACT, lambda e, ps=ps: e.activation(out=rs.ap, in_=ps, func=AF.Sqrt, scale=1.0 / 2048.0, bias=epsb.ap),
                       deps=[tk, epsb.ready] + rs.readers)
            t = P.op(DVE, lambda e: e.reciprocal(rs.ap, rs.ap), deps=[t0_])
            rs.wrote(t)
            self.bank_free[bi] = t
            last = []
            for c in range(16):
                tb = tmp[c % 2]
                t = P.op(DVE, lambda e, c=c, x=x, tb=tb: e.scalar_tensor_tensor(
                    out=tb.ap, in0=x.ap[:, c, :], scalar=A.ap[:, c:c + 1], in1=rs.ap, op0=ALU.mult, op1=ALU.mult),
                    deps=[rs.ready, A.ready, x.ready] + tb.readers)
                tb.wrote(t)
                if sh is not None:
                    t2 = P.op(ACT, lambda e, c=c, tb=tb, o=o: e.activation(
                        out=o.ap[:, c, :], in_=tb.ap, func=AF.Identity, bias=sh.ap[:, c:c + 1]),
                        deps=[t, sh.ready] + (o.readers if c == 0 else []))
                else:
                    t2 = P.op(ACT, lambda e, c=c, tb=tb, o=o: e.activation(out=o.ap[:, c, :], in_=tb.ap, func=AF.Copy),
                              deps=[t] + (o.readers if c == 0 else []))
                tb.read(t2)
                last.append(t2)
            x.read(last[-1])
            x.read(t)
            rs.read(t)
            o.wrote(last[-1])
            o.readers = []
            o.store(dv[:, :, i * tile:(i + 1) * tile])

    def ew_mul(self, a, b, out, rows, cols, dt_out=BF16):
        P = self.P
        R = rows // 128
        ct = min(cols, 512)
        rb = min(R, 8)
        av = a.rearrange("(c p) t -> p c t", p=128)
        bv = b.rearrange("(c p) t -> p c t", p=128)
        ov = out.rearrange("(c p) t -> p c t", p=128)
        ab = [self.buf([rb, ct], BF16) for _ in range(2)]
        bb = [self.buf([rb, ct], BF16) for _ in range(2)]
        ob = [self.buf([rb, ct], dt_out) for _ in range(2)]
        i = 0
        for r0 in range(0, R, rb):
            for c0 in range(0, cols, ct):
                A_, B_, O_ = ab[i % 2], bb[i % 2], ob[i % 2]
                A_.load(av[:, r0:r0 + rb, c0:c0 + ct])
                B_.load(bv[:, r0:r0 + rb, c0:c0 + ct], eng=SP)
                t = P.op(DVE if i % 2 == 0 else POOL, lambda e, A_=A_, B_=B_, O_=O_: e.tensor_tensor(O_.ap, A_.ap, B_.ap, ALU.mult),
                         deps=[A_.ready, B_.ready] + O_.readers)
                A_.read(t)
                B_.read(t)
                O_.wrote(t)
                O_.store(ov[:, r0:r0 + rb, c0:c0 + ct])
                i += 1

    def epi_resid(self, xT_in, xT_out, gate, nst=3):
        P = self.P
        xs = [self.buf([512], F32) for _ in range(nst)]
        os_ = [self.buf([512], F32) for _ in range(nst)]
        cnt = [0]

        def epi(m0, msz, n0, nsz, ps, tok):
            xb = xs[cnt[0] % nst]
            ob = os_[cnt[0] % nst]
            cnt[0] += 1
            xb.load(xT_in[m0:m0 + msz, n0:n0 + nsz], dst=xb.ap[0:msz, 0:nsz], eng=SP)
            j = m0 // 128
            t = P.op(DVE, lambda e: e.scalar_tensor_tensor(out=ob.ap[0:msz, 0:nsz], in0=ps, scalar=gate.ap[0:msz, j:j + 1],
                                                           in1=xb.ap[0:msz, 0:nsz], op0=ALU.mult, op1=ALU.add),
                     deps=[tok, xb.ready, gate.ready] + ob.readers)
            xb.read(t)
            ob.wrote(t)
            ob.store(xT_out[m0:m0 + msz, n0:n0 + nsz], src=ob.ap[0:msz, 0:nsz])
            return t
        return epi

    def outproj(self, w_out, yT, xT_in, xT_out, gate, Tn):
        step = min(Tn, 1024)
        for n0 in range(0, Tn, step):
            off0 = self.off
            epi = self.epi_resid(xT_in[:, n0:n0 + step], xT_out[:, n0:n0 + step], gate)
            self.gemm(4096, 2048, step, lambda a, b: w_out[:, a:b], lambda a, b, n0=n0: yT[:, n0 + a:n0 + b], 'R', epi)
            if n0 + step < Tn:
                self.P.barrier()
                self.off = off0
                self.bank_free = [None] * 8


def run_launch(build_fn, in_maps, out_names):
    nc = bass.Bass("TRN2", target_bir_lowering=False)
    with contextlib.ExitStack() as st:
        kb = KB(nc, st, ext_in=list(in_maps[0].keys()), ext_out=out_names)
        kb.in_shapes = {k: (v.shape, v.dtype) for k, v in in_maps[0].items()}
        build_fn(kb)
        toks = kb.P.all_tokens()
        kb.P.wait(SP, toks)
        kb.P.wait(POOL, toks)
        kb.P.emit()
    res = run_bass_kernel_spmd(nc, in_maps, core_ids=list(range(8)))
    return res.results


def IN(kb, name):
    shape, dt = kb.in_shapes[name]
    return kb.D(name, list(shape), BF16 if dt == NPBF else F32)


def col48(v):
    return np.ascontiguousarray(v.reshape(-1, 128).T)


def build_MOD(kb):
    cc = IN(kb, "cc")
    scT = kb.D("scT", [2048, 2], BF16)
    kb.silu_cols(cc, scT)
    kb.new_phase()
    modD = kb.D("modD", [6144, 2], F32)
    kb.mod(IN(kb, "ada_w"), IN(kb, "ada_b48"), scT, modD)


def front(kb, Tn, has_ctx, x_name="xT", h_name="hT"):
    modD = IN(kb, "modD")
    A, sh, gt = kb.mod_cols(modD, IN(kb, "g16"), 0)
    hT = kb.D(h_name, [2048, Tn], BF16)
    kb.norm(IN(kb, x_name), Tn, A, sh, hT=hT, tile=(384 if Tn == 2304 else 512))
    if has_ctx:
        kb.new_phase()
        A, sh, gt = kb.mod_cols(modD, IN(kb, "g16"), 1)
        hcT = kb.D("hcT", [2048, 256], BF16)
        kb.norm(IN(kb, "xcT"), 256, A, sh, hT=hcT, tile=256)
    kb.new_phase()


def build_A(has_ctx):
    def f(kb):
        front(kb, 2048, has_ctx)
        hT = kb.D("hT")
        w = IN(kb, "w_in")
        U = kb.D("U", [2048, 4096], BF16)
        SZT = kb.D("SZT", [4096, 2048], BF16)
        epi = kb.epi_act(lambda m0, ms, n0, ns: U[m0:m0 + ms, n0:n0 + ns])
        kb.gemm(2048, 2048, 4096, lambda a, b: hT[:, a:b], lambda a, b: w[:, a:b], 'L', epi)
        kb.new_phase()
        epi = kb.epi_act(lambda m0, ms, n0, ns: SZT[m0:m0 + ms, n0:n0 + ns], func=AF.Silu)
        kb.gemm(2048, 4096, 2048, lambda a, b: w[:, 4096 + a:4096 + b], lambda a, b: hT[:, a:b], 'R', epi)
        if has_ctx:
            kb.new_phase()
            hcT = kb.D("hcT")
            UC = kb.D("UC", [256, 4096], BF16)
            SZCT = kb.D("SZCT", [4096, 256], BF16)
            epi = kb.epi_act(lambda m0, ms, n0, ns: UC[m0:m0 + ms, n0:n0 + ns])
            kb.gemm(2048, 256, 4096, lambda a, b: hcT[:, a:b], lambda a, b: w[:, a:b], 'L', epi)
            kb.new_phase()
            epi = kb.epi_act(lambda m0, ms, n0, ns: SZCT[m0:m0 + ms, n0:n0 + ns], func=AF.Silu)
            kb.gemm(2048, 4096, 256, lambda a, b: w[:, 4096 + a:4096 + b], lambda a, b: hcT[:, a:b], 'R', epi)
    return f


def dft_consts():
    n2 = np.arange(128)
    ang = 2 * np.pi * np.outer(n2, n2) / 128.0
    sA = 1.0 / math.sqrt(128.0)
    CSa = np.concatenate([np.cos(ang), -np.sin(ang)], 1) * sA
    c = np.arange(256)
    angc = 2 * np.pi * np.outer(c, c) / 256.0
    Cc = np.cos(angc) / 16.0
    Sc = np.sin(angc) / 16.0
    CS256 = np.concatenate([np.cos(angc), -np.sin(angc)], 1) / 16.0
    n1 = np.arange(64)
    k1 = np.arange(64)
    TW = np.zeros((64, 256, 128), np.float32)
    for j in range(64):
        for e in range(2):
            k2 = 2 * j + e
            th = 2 * np.pi * (n1[:, None] * k2 / 8192.0 + np.outer(n1, k1) / 64.0)
            TW[j, e * 64:(e + 1) * 64, e * 64:(e + 1) * 64] = np.cos(th) / 8.0
            TW[j, 128 + e * 64:128 + (e + 1) * 64, e * 64:(e + 1) * 64] = np.sin(th) / 8.0
    bf = lambda a: np.ascontiguousarray(a.astype(np.float32)).astype(NPBF)
    return dict(CSa=bf(CSa), Cc=bf(Cc), Sc=bf(Sc), Scn=bf(-Sc), CS256=bf(CS256), TW=bf(TW))


def build_B(ngroups, has_ctx):
    def f(kb):
        UQ = IN(kb, "UQ")
        CSa = IN(kb, "CSa")
        A1 = kb.D("A1", [256, 65536], BF16)
        epi = kb.epi_act(lambda m0, ms, n0, ns: A1[m0:m0 + ms, n0:n0 + ns])
        kb.gemm(128, 256, 65536, lambda a, b: CSa[:, a:b], lambda a, b: UQ[:, a:b], 'L', epi)
        kb.new_phase()
        Wm = IN(kb, "Wm")
        MM = kb.D("MM", [3, ngroups * 256, 256], BF16)
        for g in range(ngroups):
            for t, nm in enumerate(("Cc", "Sc", "Scn")):
                Cm = IN(kb, nm)
                epi = kb.epi_act(lambda m0, ms, n0, ns, t=t, g=g: MM[t, g * 256 + m0:g * 256 + m0 + ms, n0:n0 + ns], nst=2)
                kb.gemm(256, 256, 256, lambda a, b: Cm[:, a:b], lambda a, b, g=g: Wm[g * 256:(g + 1) * 256, a:b], 'R', epi)
                kb.new_phase()
        if has_ctx:
            UC = IN(kb, "UC")
            CS256 = IN(kb, "CS256")
            AC = kb.D("AC", [4096, 512], BF16)
            epi = kb.epi_act(lambda m0, ms, n0, ns: AC[m0:m0 + ms, n0:n0 + ns])
            kb.gemm(256, 4096, 512, lambda a, b: UC[:, a:b], lambda a, b: CS256[:, a:b], 'R', epi)
    return f


def build_C(has_ctx):
    def f(kb):
        LB = IN(kb, "LB")
        RB = IN(kb, "RB")
        B1 = kb.D("B1", [4, 512, 8192], BF16)
        for g in range(4):
            epi = kb.epi_act(lambda m0, ms, n0, ns, g=g: B1[g, m0:m0 + ms, n0:n0 + ns])
            kb.gemm(512, 512, 8192, lambda a, b, g=g: RB[g, :, a:b], lambda a, b, g=g: LB[g, :, a:b], 'L', epi)
            kb.new_phase()
        if has_ctx:
            LCc = IN(kb, "LCc")
            RCc = IN(kb, "RCc")
            FCT = kb.D("FCT", [4096, 256], BF16)
            for g in range(16):
                epi = kb.epi_act(lambda m0, ms, n0, ns, g=g: FCT[g * 256 + m0:g * 256 + m0 + ms, n0:n0 + ns], nst=2)
                kb.gemm(512, 256, 256, lambda a, b, g=g: LCc[g, :, a:b], lambda a, b, g=g: RCc[g, :, a:b], 'R', epi)
                kb.new_phase()
    return f


def build_Dd(kb):
    LC = IN(kb, "LC")
    TW = IN(kb, "TW")
    FQ = kb.D("FQ", [64, 128, 1024], BF16)
    for j in range(64):
        epi = kb.epi_act(lambda m0, ms, n0, ns, j=j: FQ[j, m0:m0 + ms, n0:n0 + ns], nst=2)
        kb.gemm(256, 128, 1024, lambda a, b, j=j: TW[j, :, a:b], lambda a, b, j=j: LC[j, :, a:b], 'L', epi, sblk=1024)
        kb.new_phase()


def build_E(has_ctx, final):
    def f(kb):
        FT = IN(kb, "FT")
        SZT = IN(kb, "SZT")
        YT = kb.D("YT", [4096, 2048], BF16)
        kb.ew_mul(FT, SZT, YT, 4096, 2048)
        kb.new_phase()
        modD = IN(kb, "modD")
        A, sh, gt = kb.mod_cols(modD, IN(kb, "g16"), 0)
        xo = kb.D("xoT", [2048, 2048], F32)
        kb.outproj(IN(kb, "w_out"), YT, IN(kb, "xT"), xo, gt, 2048)
        if has_ctx:
            kb.new_phase()
            YCT = kb.D("YCT", [4096, 256], BF16)
            kb.ew_mul(IN(kb, "FCT"), IN(kb, "SZCT"), YCT, 4096, 256)
            kb.new_phase()
            A, sh, gtc = kb.mod_cols(modD, IN(kb, "g16"), 1)
            xco = kb.D("xcoT", [2048, 256], F32)
            kb.outproj(IN(kb, "w_out"), YCT, IN(kb, "xcT"), xco, gtc, 256)
        if final:
            kb.new_phase()
            fg = kb.load_const(IN(kb, "fg16"), [16], F32)
            outT = kb.D("outT", [2048, 2048], F32)
            kb.norm(xo, 2048, fg, None, outF=outT)
    return f


def T(a):
    return np.ascontiguousarray(np.asarray(a).T)


_CONSTS = {}


def consts():
    if not _CONSTS:
        _CONSTS.update(dft_consts())
    return _CONSTS


def run_mods(c, c_ctx, ada_w, ada_b):
    in_maps = []
    for core in range(8):
        b, i = core // 4, core % 4
        in_maps.append({"cc": np.ascontiguousarray(np.stack([c[b], c_ctx], 1)).astype(np.float32),
                        "ada_w": np.ascontiguousarray(ada_w[i]), "ada_b48": col48(ada_b[i])})
    res = run_launch(build_MOD, in_maps, ["modD"])
    return [[np.asarray(res[b * 4 + i]["modD"]) for b in range(2)] for i in range(4)]


def fnet_layer(xT, xcT, mods_i, g16, w_in, w_mix, w_out_i, has_ctx, final_g16=None):
    C = consts()
    in_maps = []
    for core in range(8):
        b = core // 4
        m = {"modD": mods_i[b], "g16": g16, "xT": xT[core], "w_in": w_in}
        if has_ctx:
            m["xcT"] = xcT[b]
        in_maps.append(m)
    outs = ["U", "SZT"] + (["UC", "SZCT"] if has_ctx else [])
    rA = run_launch(build_A(has_ctx), in_maps, outs)
    in_maps = []
    for core in range(8):
        b, q = core // 4, core % 4
        Ufull = np.concatenate([np.asarray(rA[b * 4 + s]["U"]) for s in range(4)], 0)
        UQ = np.ascontiguousarray(Ufull[:, q * 1024:(q + 1) * 1024]).reshape(128, 65536)
        m = {"UQ": UQ, "CSa": C["CSa"], "Cc": C["Cc"], "Sc": C["Sc"], "Scn": C["Scn"]}
        if has_ctx:
            m["Wm"] = np.ascontiguousarray(w_mix.reshape(16 * 256, 256))
            m["UC"] = np.asarray(rA[core]["UC"])
            m["CS256"] = C["CS256"]
        else:
            m["Wm"] = np.ascontiguousarray(w_mix[q * 4:(q + 1) * 4].reshape(4 * 256, 256))
        in_maps.append(m)
    ng = 16 if has_ctx else 4
    rB = run_launch(build_B(ng, has_ctx), in_maps, ["A1", "MM"] + (["AC"] if has_ctx else []))
    in_maps = []
    for core in range(8):
        q = core % 4
        A1 = np.asarray(rB[core]["A1"]).reshape(2, 128, 64, 4, 256)
        LB = np.ascontiguousarray(A1.transpose(3, 0, 4, 1, 2)).reshape(4, 512, 8192)
        MM = np.asarray(rB[core]["MM"]).reshape(3, ng, 256, 256)
        g0 = q * 4 if has_ctx else 0
        RB = np.zeros((4, 512, 512), NPBF)
        for gl in range(4):
            M1, M2, M2n = MM[0, g0 + gl], MM[1, g0 + gl], MM[2, g0 + gl]
            RB[gl, 0:256, 0:256] = M1
            RB[gl, 0:256, 256:512] = M2n
            RB[gl, 256:512, 0:256] = M2
            RB[gl, 256:512, 256:512] = M1
        m = {"LB": LB, "RB": RB}
        if has_ctx:
            AC = np.asarray(rB[core]["AC"]).reshape(16, 256, 2, 256)
            m["RCc"] = np.ascontiguousarray(AC.transpose(0, 2, 1, 3)).reshape(16, 512, 256)
            m["LCc"] = np.ascontiguousarray(np.concatenate([MM[0], MM[1]], 1))
        in_maps.append(m)
    rC = run_launch(build_C(has_ctx), in_maps, ["B1"] + (["FCT"] if has_ctx else []))
    in_maps = []
    for core in range(8):
        B1 = np.asarray(rC[core]["B1"]).reshape(4, 2, 256, 64, 2, 64)
        LC = np.ascontiguousarray(B1.transpose(3, 1, 4, 5, 0, 2)).reshape(64, 256, 1024)
        in_maps.append({"LC": LC, "TW": C["TW"]})
    rD = run_launch(build_Dd, in_maps, ["FQ"])
    Fq = []
    for core in range(8):
        FQ = np.asarray(rD[core]["FQ"]).reshape(64, 2, 64, 1024)
        Fq.append(np.ascontiguousarray(FQ.transpose(2, 0, 1, 3)).reshape(8192, 1024))
    in_maps = []
    for core in range(8):
        b, q = core // 4, core % 4
        FT = np.ascontiguousarray(np.concatenate([Fq[b * 4 + s][q * 2048:(q + 1) * 2048] for s in range(4)], 1).T)
        m = {"FT": FT, "SZT": np.asarray(rA[core]["SZT"]), "modD": mods_i[b], "g16": g16, "w_out": w_out_i,
             "xT": xT[core]}
        if has_ctx:
            m.update({"FCT": np.asarray(rC[core]["FCT"]), "SZCT": np.asarray(rA[core]["SZCT"]), "xcT": xcT[b]})
        if final_g16 is not None:
            m["fg16"] = final_g16
        in_maps.append(m)
    outs = ["xoT"] + (["xcoT"] if has_ctx else []) + (["outT"] if final_g16 is not None else [])
    rE = run_launch(build_E(has_ctx, final_g16 is not None), in_maps, outs)
    if final_g16 is not None:
        return [np.asarray(rE[k]["outT"]) for k in range(8)]
    new_x = [np.asarray(rE[k]["xoT"]) for k in range(8)]
    new_xc = [np.asarray(rE[b * 4]["xcoT"]) for b in range(2)] if has_ctx else xcT
    return new_x, new_xc


def build_F1(kb):
    front(kb, 2304, True)
    hT = kb.D("hT")
    hcT = kb.D("hcT")
    w = IN(kb, "w_in")
    QT0 = kb.D("QT0", [4096, 2048], BF16)
    KT0 = kb.D("KT0", [512, 2304], BF16)
    V = kb.D("V", [2304, 512], BF16)
    SZT = kb.D("SZT", [4096, 2048], BF16)
    KCT = kb.D("KCT", [512, 256], BF16)
    VC = kb.D("VC", [256, 512], BF16)
    epi = kb.epi_act(lambda m0, ms, n0, ns: QT0[m0:m0 + ms, n0:n0 + ns])
    kb.gemm(2048, 4096, 2048, lambda a, b: w[:, a:b], lambda a, b: hT[:, 128 + a:128 + b], 'R', epi)
    kb.new_phase()
    epi = kb.epi_act(lambda m0, ms, n0, ns: SZT[m0:m0 + ms, n0:n0 + ns], func=AF.Silu)
    kb.gemm(2048, 4096, 2048, lambda a, b: w[:, 5120 + a:5120 + b], lambda a, b: hT[:, 128 + a:128 + b], 'R', epi)
    kb.new_phase()
    epi = kb.epi_act(lambda m0, ms, n0, ns: KT0[m0:m0 + ms, n0:n0 + ns])
    kb.gemm(2048, 512, 2304, lambda a, b: w[:, 4096 + a:4096 + b], lambda a, b: hT[:, a:b], 'R', epi)
    kb.new_phase()
    epi = kb.epi_act(lambda m0, ms, n0, ns: V[m0:m0 + ms, n0:n0 + ns])
    kb.gemm(2048, 2304, 512, lambda a, b: hT[:, a:b], lambda a, b: w[:, 4608 + a:4608 + b], 'L', epi)
    kb.new_phase()
    epi = kb.epi_act(lambda m0, ms, n0, ns: KCT[m0:m0 + ms, n0:n0 + ns])
    kb.gemm(2048, 512, 256, lambda a, b: w[:, 4096 + a:4096 + b], lambda a, b: hcT[:, a:b], 'R', epi)
    kb.new_phase()
    epi = kb.epi_act(lambda m0, ms, n0, ns: VC[m0:m0 + ms, n0:n0 + ns])
    kb.gemm(2048, 256, 512, lambda a, b: hcT[:, a:b], lambda a, b: w[:, 4608 + a:4608 + b], 'L', epi)


def rope_tables(pos):
    rows = (pos // 64).astype(np.float64)
    cols = (pos % 64).astype(np.float64)
    inv = 10000.0 ** (-np.arange(16) / 16.0)
    cosT = np.zeros((64, len(pos)), np.float32)
    ssin = np.zeros((64, len(pos)), np.float32)
    for d in range(64):
        half, wv = d // 32, d % 32
        f, part = wv % 16, wv // 16
        ang = (rows if half == 0 else cols) * inv[f]
        cosT[d] = np.cos(ang)
        ssin[d] = np.sin(ang) * (-1.0 if part == 0 else 1.0)
    return np.ascontiguousarray(np.tile(cosT, (2, 1))), np.ascontiguousarray(np.tile(ssin, (2, 1)))


def rope_perm_rows(nheads):
    idx = np.arange(nheads * 64).reshape(nheads, 64)
    d = np.arange(64)
    wv = d % 32
    partner = np.where(wv // 16 == 0, d + 16, d - 16)
    return idx[:, partner].reshape(-1)


def build_G1a(kb):
    P = kb.P
    for (nm, rows, Tn) in (("Q", 4096, 2048), ("K", 512, 2304)):
        a = IN(kb, nm + "T0")
        bp = IN(kb, nm + "T0p")
        cosD = IN(kb, "cos" + nm)
        sinD = IN(kb, "ssin" + nm)
        out = kb.D(nm + "Tr", [rows, Tn], BF16)
        cs = kb.load_const(cosD, [Tn], F32)
        sn = kb.load_const(sinD, [Tn], F32)
        ab = [kb.buf([Tn], BF16) for _ in range(2)]
        bb = [kb.buf([Tn], BF16) for _ in range(2)]
        t1 = [kb.buf([Tn], F32) for _ in range(2)]
        t2 = [kb.buf([Tn], F32) for _ in range(2)]
        ob = [kb.buf([Tn], BF16) for _ in range(2)]
        for ch in range(rows // 128):
            i = ch % 2
            A_, B_, T1, T2, O_ = ab[i], bb[i], t1[i], t2[i], ob[i]
            A_.load(a[ch * 128:(ch + 1) * 128, :])
            B_.load(bp[ch * 128:(ch + 1) * 128, :], eng=SP)
            ta = P.op(DVE, lambda e, A_=A_, T1=T1, cs=cs: e.tensor_tensor(T1.ap, A_.ap, cs.ap, ALU.mult),
                      deps=[A_.ready, cs.ready] + T1.readers)
            T1.wrote(ta)
            A_.read(ta)
            tb = P.op(POOL, lambda e, B_=B_, T2=T2, sn=sn: e.tensor_tensor(T2.ap, B_.ap, sn.ap, ALU.mult),
                      deps=[B_.ready, sn.ready] + T2.readers)
            T2.wrote(tb)
            B_.read(tb)
            tc = P.op(DVE, lambda e, T1=T1, T2=T2, O_=O_: e.tensor_tensor(O_.ap, T1.ap, T2.ap, ALU.add),
                      deps=[ta, tb] + O_.readers)
            T1.read(tc)
            T2.read(tc)
            O_.wrote(tc)
            O_.store(out[ch * 128:(ch + 1) * 128, :])
        kb.new_phase()


def attn_core(kb, QTr, KTr, V, KCT, VC, SZT, YT, maskD, sinkD):
    P = kb.P
    Qv = QTr.rearrange("(h g d) t -> h d g t", g=8, d=64)
    Sv = SZT.rearrange("(h g d) t -> h d g t", g=8, d=64)
    Yv = YT.rearrange("(h g d) t -> h d g t", g=8, d=64)
    masks = kb.load_const(maskD.rearrange("k p n -> p k n"), [4, 512], BF16)
    srow = kb.buf([8192], F32)
    srow.load(sinkD, dst=srow.ap[0:1])
    esrow = kb.buf([8192], BF16)
    t = P.op(ACT, lambda e: e.activation(out=esrow.ap[0:1], in_=srow.ap[0:1], func=AF.Exp), deps=[srow.ready])
    esrow.wrote(t)
    ones = kb.buf([64], BF16)
    t = P.op(DVE, lambda e: e.memset(ones.ap, 1.0))
    ones.wrote(t)
    Qh = [kb.buf([8, 2048], BF16) for _ in range(2)]
    Kh = [kb.buf([2560], BF16) for _ in range(2)]
    Vh = [kb.buf([20, 64], BF16) for _ in range(2)]
    Pt = [kb.buf([5, 512], BF16) for _ in range(2)]
    rD = [kb.buf([512], F32) for _ in range(2)]
    yt = [kb.buf([512], F32) for _ in range(2)]
    szb = [kb.buf([8, 128], BF16) for _ in range(2)]
    yb = [kb.buf([8, 128], BF16) for _ in range(2)]
    it = 0
    for hk in range(8):
        Q_, K_, V_ = Qh[hk % 2], Kh[hk % 2], Vh[hk % 2]
        Q_.load(Qv[hk], dst=Q_.ap[0:64])
        K_.load(KTr[hk * 64:(hk + 1) * 64, :], dst=K_.ap[0:64, 0:2304], eng=SP)
        tk2 = K_.ready
        K_.readers = []
        t2 = P.dma(SP, K_.sem, K_.ap[0:64, 2304:2560], KCT[hk * 64:(hk + 1) * 64, :])
        K_.ready = t2
        kdeps = [tk2, t2]
        V_.load(V[:, hk * 64:(hk + 1) * 64].rearrange("(b p) d -> p b d", p=128), dst=V_.ap[:, 0:18, :], eng=SP)
        tv1 = V_.ready
        V_.readers = []
        tv2 = P.dma(SP, V_.sem, V_.ap[:, 18:20, :], VC[:, hk * 64:(hk + 1) * 64].rearrange("(b p) d -> p b d", p=128))
        V_.ready = tv2
        vdeps = [tv1, tv2]
        for b in range(16):
            SZ_, Y_ = szb[it % 2], yb[it % 2]
            it += 1
            SZ_.load(Sv[hk][:, :, b * 128:(b + 1) * 128], dst=SZ_.ap[0:64], eng=SP)
            ylast = None
            for half in range(2):
                P_ = Pt[half]
                rhs = Q_.ap[0:64, 4 * half:4 * half + 4, b * 128:(b + 1) * 128]
                blks = [b, b + 1, b + 2, 18, 19]
                pt_toks = []
                for j, blk in enumerate(blks):
                    bi = kb.bank()
                    ps = kb.banks[bi][:, 0:512]
                    kcols = K_.ap[0:64, blk * 128:(blk + 1) * 128]
                    tm = P.op(PE, lambda e, ps=ps, kcols=kcols, rhs=rhs: e.matmul(ps, kcols, rhs, start=True, stop=True),
                              deps=[kb.bank_free[bi], Q_.ready] + kdeps)
                    te = P.op(ACT, lambda e, ps=ps, P_=P_, j=j: e.activation(out=P_.ap[:, j, :], in_=ps, func=AF.Exp, scale=0.125),
                              deps=[tm] + (P_.readers if j == 0 else []))
                    kb.bank_free[bi] = te
                    if j in (0, 2):
                        mi = (0 if j == 0 else 1)
                        if j == 0 and b == 0:
                            mi = 2
                        if j == 2 and b == 15:
                            mi = 3
                        te = P.op(DVE, lambda e, P_=P_, j=j, mi=mi: e.tensor_tensor(P_.ap[:, j, :], P_.ap[:, j, :], masks.ap[:, mi, :], ALU.mult),
                                  deps=[te, masks.ready])
                    pt_toks.append(te)
                Q_.read(tm)
                K_.read(tm)
                P_.wrote(pt_toks[-1])
                bo, bd = kb.bank(), kb.bank()
                pso = kb.banks[bo][0:64, 0:512]
                psd = kb.banks[bd][0:64, 0:512]
                for j, blk in enumerate(blks):
                    to = P.op(PE, lambda e, pso=pso, V_=V_, blk=blk, P_=P_, j=j: e.matmul(pso, V_.ap[:, blk, :], P_.ap[:, j, :], start=(j == 0), stop=(j == 4)),
                              deps=(pt_toks + vdeps + [kb.bank_free[bo]]) if j == 0 else [], signal=(j == 4))
                for j in range(5):
                    td = P.op(PE, lambda e, psd=psd, P_=P_, j=j: e.matmul(psd, ones.ap, P_.ap[:, j, :], start=(j == 0), stop=False),
                              deps=[kb.bank_free[bd], ones.ready] if j == 0 else [], signal=False)
                h0 = hk * 8 + 4 * half
                td = P.op(PE, lambda e, psd=psd, h0=h0: e.matmul(psd, ones.ap[0:1, :], esrow.ap[0:1, h0 * 128:h0 * 128 + 512], start=False, stop=True),
                          deps=[esrow.ready])
                P_.read(td)
                V_.read(td)
                R_, T_ = rD[half], yt[half]
                tr = P.op(DVE, lambda e, psd=psd, R_=R_: e.reciprocal(R_.ap[0:64], psd), deps=[td] + R_.readers)
                R_.wrote(tr)
                kb.bank_free[bd] = tr
                ty = P.op(DVE, lambda e, pso=pso, R_=R_, T_=T_: e.tensor_tensor(T_.ap[0:64], pso, R_.ap[0:64], ALU.mult),
                          deps=[to, tr] + T_.readers)
                T_.wrote(ty)
                R_.read(ty)
                kb.bank_free[bo] = ty
                tz = P.op(POOL, lambda e, T_=T_, Y_=Y_, SZ_=SZ_, half=half: e.tensor_tensor(
                    Y_.ap[0:64, 4 * half:4 * half + 4, :], T_.ap[0:64].rearrange("p (g q) -> p g q", g=4),
                    SZ_.ap[0:64, 4 * half:4 * half + 4, :], ALU.mult),
                    deps=[ty, SZ_.ready] + (Y_.readers if half == 0 else []))
                T_.read(tz)
                ylast = tz
            SZ_.read(ylast)
            Y_.wrote(ylast)
            Y_.store(Yv[hk][:, :, b * 128:(b + 1) * 128], src=Y_.ap[0:64])


def build_G1b(kb):
    YT = kb.D("YT", [4096, 2048], BF16)
    attn_core(kb, IN(kb, "QTr"), IN(kb, "KTr"), IN(kb, "V"), IN(kb, "KCT"), IN(kb, "VC"), IN(kb, "SZT"), YT,
              IN(kb, "maskx"), IN(kb, "sinkrep"))
    kb.new_phase()
    A, sh, gt = kb.mod_cols(IN(kb, "modD"), IN(kb, "g16"), 0)
    xo = kb.D("xoT", [2048, 2048], F32)
    kb.outproj(IN(kb, "w_out"), YT, IN(kb, "xT"), xo, gt, 2048)


def attn_layer(xT, xcT, mods_i, g16, w_in, sink, w_out_i):
    in_maps = []
    for core in range(8):
        b, q = core // 4, core % 4
        left = xT[core - 1][:, -128:] if q > 0 else np.zeros((2048, 128), np.float32)
        right = xT[core + 1][:, :128] if q < 3 else np.zeros((2048, 128), np.float32)
        xh = np.ascontiguousarray(np.concatenate([left, xT[core], right], 1))
        in_maps.append({"modD": mods_i[b], "g16": g16, "xT": xh, "xcT": xcT[b], "w_in": w_in})
    rF = run_launch(build_F1, in_maps, ["QT0", "KT0", "V", "SZT", "KCT", "VC"])
    pq, pk = rope_perm_rows(64), rope_perm_rows(8)
    in_maps = []
    for core in range(8):
        q = core % 4
        cq, sq = rope_tables(q * 2048 + np.arange(2048))
        ck, sk = rope_tables(np.abs(q * 2048 - 128 + np.arange(2304)))
        QT0 = np.asarray(rF[core]["QT0"])
        KT0 = np.asarray(rF[core]["KT0"])
        in_maps.append({"QT0": QT0, "QT0p": np.ascontiguousarray(QT0[pq]), "KT0": KT0, "KT0p": np.ascontiguousarray(KT0[pk]),
                        "cosQ": cq, "ssinQ": sq, "cosK": ck, "ssinK": sk})
    rG = run_launch(build_G1a, in_maps, ["QTr", "KTr"])
    kj = np.arange(128)[:, None]
    qi = np.arange(128)[None, :]
    mprev = np.tile((kj >= qi).astype(np.float32), (1, 4))
    mnext = np.tile((kj <= qi).astype(np.float32), (1, 4))
    zero = np.zeros_like(mprev)
    in_maps = []
    for core in range(8):
        b, q = core // 4, core % 4
        maskx = np.stack([mprev, mnext, zero if q == 0 else mprev, zero if q == 3 else mnext], 0).astype(NPBF)
        in_maps.append({"QTr": np.asarray(rG[core]["QTr"]), "KTr": np.asarray(rG[core]["KTr"]),
                        "V": np.asarray(rF[core]["V"]), "KCT": np.asarray(rF[core]["KCT"]), "VC": np.asarray(rF[core]["VC"]),
                        "SZT": np.asarray(rF[core]["SZT"]), "maskx": maskx,
                        "sinkrep": np.ascontiguousarray(np.repeat(sink.astype(np.float32), 128)[None, :]),
                        "modD": mods_i[b], "g16": g16, "w_out": w_out_i, "xT": xT[core]})
    rH = run_launch(build_G1b, in_maps, ["xoT"])
    return [np.asarray(rH[k]["xoT"]) for k in range(8)]


AXX = mybir.AxisListType.X


def build_H2(kb):
    P = kb.P
    front(kb, 2048, False)
    hT = kb.D("hT")
    w = IN(kb, "w_in")
    U = kb.D("U", [2048, 4096], BF16)
    Vf = kb.D("Vf", [2048, 4096], F32)
    SZ = kb.D("SZ", [2048, 4096], BF16)
    e_u = kb.epi_act(lambda m0, ms, n0, ns: U[m0:m0 + ms, n0:n0 + ns], func=AF.Gelu_apprx_tanh, nst=3)
    e_v = kb.epi_act(lambda m0, ms, n0, ns: Vf[m0:m0 + ms, n0 - 4096:n0 - 4096 + ns], func=AF.Gelu_apprx_tanh, dt=F32, nst=3)
    e_z = kb.epi_act(lambda m0, ms, n0, ns: SZ[m0:m0 + ms, n0 - 8192:n0 - 8192 + ns], func=AF.Silu, nst=3)

    def epi(m0, ms, n0, ns, ps, tok):
        return (e_u if n0 < 4096 else (e_v if n0 < 8192 else e_z))(m0, ms, n0, ns, ps, tok)
    kb.gemm(2048, 2048, 12288, lambda a, b: hT[:, a:b], lambda a, b: w[:, a:b], 'L', epi)
    kb.new_phase()
    VN = kb.D("VN", [2048, 4096], BF16)
    G = kb.load_const(IN(kb, "lnG"), [4096], F32)
    Bt = kb.load_const(IN(kb, "lnB"), [4096], F32, eng=SP)
    epsb = kb.buf([1], F32)
    t = P.op(DVE, lambda e: e.memset(epsb.ap, EPS))
    epsb.wrote(t)
    vb = [kb.buf([4096], F32) for _ in range(2)]
    sq = kb.buf([4096], F32)
    vh = kb.buf([4096], F32)
    ob = [kb.buf([4096], BF16) for _ in range(2)]
    st = [kb.buf([8], F32) for _ in range(2)]
    vb[0].load(Vf[0:128, :])
    for i in range(16):
        if i + 1 < 16:
            vb[(i + 1) % 2].load(Vf[(i + 1) * 128:(i + 2) * 128, :])
        v, o, s = vb[i % 2], ob[i % 2], st[i % 2]
        t_sq = P.op(ACT, lambda e, v=v: e.activation(out=sq.ap, in_=v.ap, func=AF.Square), deps=[v.ready] + sq.readers)
        sq.wrote(t_sq)
        t1 = P.op(DVE, lambda e, v=v, s=s: e.reduce_sum(out=s.ap[:, 0:1], in_=v.ap, axis=AXX), deps=[v.ready] + s.readers)
        t2 = P.op(DVE, lambda e, s=s: e.reduce_sum(out=s.ap[:, 1:2], in_=sq.ap, axis=AXX), deps=[t_sq, t1])
        sq.read(t2)
        t3 = P.op(DVE, lambda e, s=s: e.tensor_scalar(s.ap[:, 2:3], s.ap[:, 0:1], 1.0 / 4096.0, None, ALU.mult), deps=[t2])
        t4 = P.op(DVE, lambda e, s=s: e.tensor_tensor(s.ap[:, 3:4], s.ap[:, 2:3], s.ap[:, 2:3], ALU.mult), deps=[t3])
        t5 = P.op(DVE, lambda e, s=s: e.scalar_tensor_tensor(out=s.ap[:, 4:5], in0=s.ap[:, 1:2], scalar=1.0 / 4096.0,
                                                             in1=s.ap[:, 3:4], op0=ALU.mult, op1=ALU.subtract), deps=[t4])
        t6 = P.op(ACT, lambda e, s=s: e.activation(out=s.ap[:, 5:6], in_=s.ap[:, 4:5], func=AF.Sqrt, bias=epsb.ap), deps=[t5, epsb.ready])
        t7 = P.op(DVE, lambda e, s=s: e.reciprocal(s.ap[:, 5:6], s.ap[:, 5:6]), deps=[t6])
        t8 = P.op(DVE, lambda e, s=s: e.scalar_tensor_tensor(out=s.ap[:, 6:7], in0=s.ap[:, 2:3], scalar=-1.0,
                                                             in1=s.ap[:, 5:6], op0=ALU.mult, op1=ALU.mult), deps=[t7])
        t9 = P.op(ACT, lambda e, v=v, s=s: e.activation(out=vh.ap, in_=v.ap, func=AF.Identity, scale=s.ap[:, 5:6], bias=s.ap[:, 6:7]),
                  deps=[t8] + vh.readers)
        vh.wrote(t9)
        v.read(t9)
        v.read(t2)
        t10 = P.op(DVE, lambda e: e.tensor_tensor(vh.ap, vh.ap, G.ap, ALU.mult), deps=[t9, G.ready])
        t11 = P.op(DVE, lambda e, o=o: e.tensor_tensor(o.ap, vh.ap, Bt.ap, ALU.add), deps=[t10, Bt.ready] + o.readers)
        vh.read(t11)
        s.read(t11)
        o.wrote(t11)
        o.store(VN[i * 128:(i + 1) * 128, :])
    kb.new_phase()
    Y = kb.D("Y", [2048, 4096], BF16)
    WsT = IN(kb, "WsT")
    bs = kb.load_const(IN(kb, "bsT"), [16], F32)

    def view(D_, g, kk):
        return D_.rearrange("(k t) (g c) -> g t k c", t=128, c=256)[g][:, kk:kk + 2, :]
    wt = [kb.buf([128], BF16) for _ in range(2)]
    rb = [kb.buf([2, 256], BF16) for _ in range(2)]
    ub = [kb.buf([2, 256], BF16) for _ in range(2)]
    zb = [kb.buf([2, 256], BF16) for _ in range(2)]
    sb_ = [kb.buf([512], F32) for _ in range(2)]
    yb = [kb.buf([2, 256], BF16) for _ in range(2)]
    it = 0
    for g in range(16):
        W_ = wt[g % 2]
        W_.load(WsT[g])
        for kk in range(0, 16, 2):
            i = it % 2
            it += 1
            R_, U_, Z_, S_, Y_ = rb[i], ub[i], zb[i], sb_[i], yb[i]
            R_.load(view(VN, g, kk))
            U_.load(view(U, g, kk), eng=SP)
            Z_.load(view(SZ, g, kk), eng=SP)
            bi = kb.bank()
            ps = kb.banks[bi][:, 0:512]
            tm = P.op(PE, lambda e, ps=ps, W_=W_, R_=R_: e.matmul(ps, W_.ap, R_.ap.rearrange("p k c -> p (k c)"), start=True, stop=True),
                      deps=[W_.ready, R_.ready, kb.bank_free[bi]])
            W_.read(tm)
            R_.read(tm)
            ts = P.op(ACT, lambda e, ps=ps, S_=S_, g=g: e.activation(out=S_.ap, in_=ps, func=AF.Identity, bias=bs.ap[:, g:g + 1]),
                      deps=[tm, bs.ready] + S_.readers)
            kb.bank_free[bi] = ts
            S_.wrote(ts)
            ta = P.op(DVE, lambda e, S_=S_, U_=U_: e.tensor_tensor(S_.ap, S_.ap, U_.ap.rearrange("p k c -> p (k c)"), ALU.mult),
                      deps=[ts, U_.ready])
            U_.read(ta)
            tb = P.op(DVE, lambda e, S_=S_, Z_=Z_, Y_=Y_: e.tensor_tensor(Y_.ap.rearrange("p k c -> p (k c)"), S_.ap,
                                                                         Z_.ap.rearrange("p k c -> p (k c)"), ALU.mult),
                      deps=[ta, Z_.ready] + Y_.readers)
            Z_.read(tb)
            S_.read(tb)
            Y_.wrote(tb)
            Y_.store(view(Y, g, kk))


def build_I(final):
    def f(kb):
        A, sh, gt = kb.mod_cols(IN(kb, "modD"), IN(kb, "g16"), 0)
        xo = kb.D("xoT", [2048, 2048], F32)
        kb.outproj(IN(kb, "w_out"), IN(kb, "YT"), IN(kb, "xT"), xo, gt, 2048)
    return f


def gmlp_layer(xT, mods_i, g16, w_in, w_s, b_s, ln_g, ln_b, w_out_i):
    in_maps = []
    for core in range(8):
        b = core // 4
        in_maps.append({"modD": mods_i[b], "g16": g16, "xT": xT[core], "w_in": w_in,
                        "lnG": np.ascontiguousarray(np.broadcast_to(ln_g[None, :], (128, 4096))).astype(np.float32),
                        "lnB": np.ascontiguousarray(np.broadcast_to(ln_b[None, :], (128, 4096))).astype(np.float32),
                        "WsT": np.ascontiguousarray(w_s.transpose(0, 2, 1)), "bsT": T(b_s)})
    rH = run_launch(build_H2, in_maps, ["Y"])
    in_maps = []
    for core in range(8):
        b = core // 4
        in_maps.append({"YT": T(rH[core]["Y"]), "modD": mods_i[b], "g16": g16, "w_out": w_out_i, "xT": xT[core]})
    rI = run_launch(build_I(False), in_maps, ["xoT"])
    return [np.asarray(rI[k]["xoT"]) for k in range(8)]


def kernel(x, c, ctx, c_ctx, norm_g, ada_w, ada_b, w_out, fnet_w_in, fnet_w_mix, attn_w_in, attn_sink,
           gmlp_w_in, gmlp_w_s, gmlp_b_s, gmlp_ln_g, gmlp_ln_b, final_g):
    f32 = lambda a: np.asarray(a, np.float32)
    x, c, ctx, c_ctx, norm_g, ada_w, ada_b, w_out = map(f32, (x, c, ctx, c_ctx, norm_g, ada_w, ada_b, w_out))
    fnet_w_in, fnet_w_mix, attn_w_in, attn_sink = map(f32, (fnet_w_in, fnet_w_mix, attn_w_in, attn_sink))
    gmlp_w_in, gmlp_w_s, gmlp_b_s, gmlp_ln_g, gmlp_ln_b, final_g = map(
        f32, (gmlp_w_in, gmlp_w_s, gmlp_b_s, gmlp_ln_g, gmlp_ln_b, final_g))
    mods = run_mods(c, c_ctx, ada_w, ada_b)
    xT = [T(x[k // 4, (k % 4) * 2048:(k % 4 + 1) * 2048]) for k in range(8)]
    xcT = [T(ctx[b]) for b in range(2)]
    xT, xcT = fnet_layer(xT, xcT, mods[0], col48(norm_g[0]), fnet_w_in[0], fnet_w_mix[0], w_out[0], True)
    xT = attn_layer(xT, xcT, mods[1], col48(norm_g[1]), attn_w_in[0], attn_sink[0], w_out[1])
    xT = gmlp_layer(xT, mods[2], col48(norm_g[2]), gmlp_w_in[0], gmlp_w_s[0], gmlp_b_s[0], gmlp_ln_g[0], gmlp_ln_b[0], w_out[2])
    oT = fnet_layer(xT, None, mods[3], col48(norm_g[3]), fnet_w_in[1], fnet_w_mix[1], w_out[3], False,
                    final_g16=col48(final_g))
    out = np.empty((2, 8192, 2048), np.float32)
    for k in range(8):
        out[k // 4, (k % 4) * 2048:(k % 4 + 1) * 2048] = oT[k].T
    return out
```

```python
import contextlib
import math
import numpy as np
import ml_dtypes
import concourse.bass as bass
import concourse.mybir as mybir
from concourse.bass_utils import run_bass_kernel_spmd

F32 = mybir.dt.float32
BF16 = mybir.dt.bfloat16
AF = mybir.ActivationFunctionType
ALU = mybir.AluOpType
NPBF = ml_dtypes.bfloat16

PE, ACT, DVE, POOL, SP = "pe", "act", "dve", "pool", "sp"
ENGS = (PE, ACT, DVE, POOL, SP)

D_MODEL = 2048
D_BRANCH = 4096
EPS = 1e-6
ARENA_F32 = 46 * 1024


class Prog:
    def __init__(self, nc, stack):
        self.nc = nc
        self.stack = stack
        self.q = {e: [] for e in ENGS}
        self.nsem = 0
        self.esem = {e: self.sem("e_" + e) for e in (PE, ACT, DVE, POOL)}
        self.ecnt = {e: 0 for e in (PE, ACT, DVE, POOL)}
        self.dsems = []
        self.dcnt = {}

    def sem(self, name):
        self.nsem += 1
        return self.stack.enter_context(self.nc.semaphore(f"{name}_{self.nsem}"))

    def dsem(self, name="d"):
        s = self.sem(name)
        self.dsems.append(s)
        self.dcnt[id(s)] = 0
        return s

    def op(self, eng, fn, deps=(), signal=True):
        tok = None
        inc = None
        if signal:
            self.ecnt[eng] += 1
            tok = (self.esem[eng], self.ecnt[eng])
            inc = (self.esem[eng], 1)
        self.q[eng].append((tuple(d for d in deps if d is not None), fn, inc))
        return tok

    def dma(self, eng, sem, out, in_, deps=()):
        self.dcnt[id(sem)] += 16
        tok = (sem, self.dcnt[id(sem)])
        self.q[eng].append((tuple(d for d in deps if d is not None),
                            (lambda e, out=out, in_=in_: e.dma_start(out=out, in_=in_)), (sem, 16)))
        return tok

    def coll(self, sem, kind, src, dst, groups, deps=()):
        self.dcnt[id(sem)] += 1
        tok = (sem, self.dcnt[id(sem)])
        self.q[POOL].append((tuple(d for d in deps if d is not None),
                             (lambda e: e.collective_compute(kind, ALU.bypass, replica_groups=groups,
                                                             ins=[src.opt()], outs=[dst.opt()])), (sem, 1)))
        return tok

    def wait(self, eng, deps):
        self.q[eng].append((tuple(d for d in deps if d is not None), None, None))

    def all_tokens(self):
        toks = [(self.esem[e], self.ecnt[e]) for e in self.ecnt if self.ecnt[e] > 0]
        toks += [(s, self.dcnt[id(s)]) for s in self.dsems if self.dcnt[id(s)] > 0]
        return toks

    def barrier(self):
        toks = self.all_tokens()
        for e in ENGS:
            self.wait(e, toks)

    def emit(self):
        nc = self.nc
        with nc.Block() as block:
            def replay(name):
                def run(e):
                    seen = {}
                    for deps, fn, inc in self.q[name]:
                        for (s, v) in deps:
                            k = id(s)
                            if seen.get(k, 0) < v:
                                e.wait_ge(s, v)
                                seen[k] = v
                        if fn is None:
                            continue
                        ins = fn(e)
                        if inc is not None:
                            ins.then_inc(inc[0], inc[1])
                return run
            block.tensor(replay(PE))
            block.scalar(replay(ACT))
            block.vector(replay(DVE))
            block.gpsimd(replay(POOL))
            block.sync(replay(SP))


class Buf:
    def __init__(self, kb, ap):
        self.kb = kb
        self.ap = ap
        self.sem = kb.get_dsem()
        self.ready = None
        self.readers = []

    def load(self, src, eng=POOL, deps=(), dst=None):
        P = self.kb.P
        t = P.dma(eng, self.sem, self.ap if dst is None else dst, src, deps=list(self.readers) + list(deps))
        self.ready = t
        self.readers = []
        return t

    def wrote(self, tok):
        self.ready = tok
        self.readers = []

    def read(self, tok):
        if tok is not None:
            self.readers.append(tok)
            if len(self.readers) > 24:
                self.readers = self.readers[-24:]

    def store(self, dst, deps=(), src=None, eng=SP):
        P = self.kb.P
        t = P.dma(eng, self.sem, dst, self.ap if src is None else src, deps=[self.ready] + list(deps))
        self.readers.append(t)
        return t


class KB:
    def __init__(self, nc, stack, ext_in=(), ext_out=()):
        self.nc = nc
        self.P = Prog(nc, stack)
        self.arena = stack.enter_context(nc.sbuf_tensor("arena", [128, ARENA_F32], F32))
        self.banks = [stack.enter_context(nc.psum_tensor(f"bank{i}", [128, 512], F32)) for i in range(8)]
        self.bank_free = [None] * 8
        self.bank_i = 0
        self.off = 0
        self.dram = {}
        self.ext_in = set(ext_in)
        self.ext_out = set(ext_out)
        self.out_toks = []

    def get_dsem(self):
        if not hasattr(self, "sem_pool"):
            self.sem_pool = []
            self.sem_next = 0
        if self.sem_next >= len(self.sem_pool):
            self.sem_pool.append(self.P.dsem("b"))
        s = self.sem_pool[self.sem_next]
        self.sem_next += 1
        return s

    def D(self, name, shape=None, dt=None):
        if name in self.dram:
            return self.dram[name]
        kind = "ExternalInput" if name in self.ext_in else ("ExternalOutput" if name in self.ext_out else "Internal")
        t = self.nc.dram_tensor(name, list(shape), dt, kind=kind).ap()
        self.dram[name] = t
        return t

    def alloc(self, shape, dt):
        n = int(np.prod(shape))
        nf32 = (n * (2 if dt == BF16 else 4) + 3) // 4
        nf32 = (nf32 + 7) // 8 * 8
        assert self.off + nf32 <= ARENA_F32, (self.off, nf32, shape)
        ap = self.arena[:, self.off:self.off + nf32]
        self.off += nf32
        if dt == BF16:
            ap = ap.bitcast(BF16)
        ap = ap[:, 0:n]
        if len(shape) == 2:
            ap = ap.rearrange("p (a b) -> p a b", a=shape[0])
        elif len(shape) == 3:
            ap = ap.rearrange("p (a b c) -> p a b c", a=shape[0], b=shape[1])
        elif len(shape) == 4:
            ap = ap.rearrange("p (a b c d) -> p a b c d", a=shape[0], b=shape[1], c=shape[2])
        return ap

    def buf(self, shape, dt):
        return Buf(self, self.alloc(shape, dt))

    def new_phase(self):
        self.P.barrier()
        self.off = 0
        self.sem_next = 0
        self.bank_free = [None] * 8

    def bank(self):
        i = self.bank_i
        self.bank_i = (i + 1) % 8
        return i

    def gemm(self, K, M, N, Lsrc, Rsrc, resident, epi, l_dt=BF16, r_dt=BF16, sblk=512, kp=128, deps=()):
        P = self.P
        KC = K // kp
        assert K % kp == 0

        def view(src):
            return src.rearrange("(c p) x -> p c x", p=kp)

        RESX = M if resident == 'L' else N
        STRX = N if resident == 'L' else M
        res = self.buf([KC, RESX], BF16)
        res_ap = res.ap if kp == 128 else res.ap[0:kp]
        rsrc = Lsrc if resident == 'L' else Rsrc
        ssrc = Rsrc if resident == 'L' else Lsrc
        nres_toks = []
        for x0 in range(0, RESX, 1024):
            x1 = min(RESX, x0 + 1024)
            nres_toks.append(res.load(view(rsrc(x0, x1)), deps=deps, dst=res_ap[:, :, x0:x1]))
            res.readers = []
        sblk = min(sblk, STRX)
        nblk = (STRX + sblk - 1) // sblk
        sb = [self.buf([KC, sblk], BF16) for _ in range(min(2, nblk))]

        def issue(s):
            b = sb[s % len(sb)]
            x0 = s * sblk
            x1 = min(STRX, x0 + sblk)
            dst = (b.ap if kp == 128 else b.ap[0:kp])[:, :, 0:x1 - x0]
            b.load(view(ssrc(x0, x1)), deps=deps, dst=dst)

        issue(0)
        for s in range(nblk):
            if s + 1 < nblk:
                issue(s + 1)
            b = sb[s % len(sb)]
            bap = b.ap if kp == 128 else b.ap[0:kp]
            x0 = s * sblk
            x1 = min(STRX, x0 + sblk)
            if resident == 'L':
                tiles = [(m0, min(M, m0 + 128), n0, min(x1, n0 + 512))
                         for m0 in range(0, M, 128) for n0 in range(x0, x1, 512)]
            else:
                tiles = [(m0, min(x1, m0 + 128), n0, min(N, n0 + 512))
                         for m0 in range(x0, x1, 128) for n0 in range(0, N, 512)]
            for (m0, m1, n0, n1) in tiles:
                bi = self.bank()
                ps = self.banks[bi][0:m1 - m0, 0:n1 - n0]
                tok = None
                for c in range(KC):
                    if resident == 'L':
                        lt = res_ap[:, c, m0:m1]
                        rt = bap[:, c, n0 - x0:n1 - x0]
                    else:
                        lt = bap[:, c, m0 - x0:m1 - x0]
                        rt = res_ap[:, c, n0:n1]
                    d = [self.bank_free[bi], b.ready] + nres_toks if c == 0 else []
                    last = (c == KC - 1)
                    tok = P.op(PE, (lambda e, ps=ps, lt=lt, rt=rt, c=c, last=last:
                                    e.matmul(ps, lt, rt, start=(c == 0), stop=last)),
                               deps=d, signal=last)
                b.read(tok)
                res.read(tok)
                self.bank_free[bi] = epi(m0, m1 - m0, n0, n1 - n0, ps, tok)

    def stagers(self, n, shape, dt):
        return [self.buf(shape, dt) for _ in range(n)]

    def epi_act(self, dst_fn, func=AF.Copy, dt=BF16, nst=4, alt=True, bias_fn=None):
        P = self.P
        st = self.stagers(nst, [512], dt)
        cnt = [0]

        def epi(m0, msz, n0, nsz, ps, tok):
            b = st[cnt[0] % nst]
            use_dve = alt and func == AF.Copy and bias_fn is None and (cnt[0] % 2 == 1)
            cnt[0] += 1
            o = b.ap[0:msz, 0:nsz]
            deps = [tok] + list(b.readers)
            if use_dve:
                t = P.op(DVE, lambda e: e.tensor_copy(out=o, in_=ps), deps=deps)
            elif bias_fn is not None:
                bcol = bias_fn(m0, msz)
                t = P.op(ACT, lambda e: e.activation(out=o, in_=ps, func=func, bias=bcol), deps=deps)
            else:
                t = P.op(ACT, lambda e: e.activation(out=o, in_=ps, func=func), deps=deps)
            b.wrote(t)
            b.store(dst_fn(m0, msz, n0, nsz), src=o)
            return t
        return epi

    def load_const(self, src, shape, dt, eng=POOL):
        b = self.buf(shape, dt)
        b.load(src, eng=eng)
        return b

    def silu_cols(self, cc, scT):
        P = self.P
        a = self.load_const(cc.rearrange("(c p) x -> p c x", p=128), [16, 2], F32)
        o = self.buf([16, 2], BF16)
        t = P.op(ACT, lambda e: e.activation(out=o.ap, in_=a.ap, func=AF.Silu), deps=[a.ready])
        o.wrote(t)
        o.store(scT.rearrange("(c p) x -> p c x", p=128))

    def mod(self, ada_w, ada_b48, scT, modD):
        bias = self.load_const(ada_b48, [48], F32)
        epi = self.epi_act(lambda m0, ms, n0, ns: modD[m0:m0 + ms, n0:n0 + ns], func=AF.Identity, dt=F32,
                           bias_fn=lambda m0, ms: bias.ap[0:ms, m0 // 128:m0 // 128 + 1])
        self.gemm(2048, 6144, 2, lambda a, b: ada_w[:, a:b], lambda a, b: scT[:, a:b], 'R', epi, deps=[bias.ready])

    def mod_cols(self, modD, g16, col):
        P = self.P
        m = self.load_const(modD.rearrange("(j p) x -> p j x", p=128), [48, 2], F32)
        g = self.load_const(g16, [16], F32)
        A = self.buf([16], F32)
        t = P.op(DVE, lambda e: e.tensor_scalar(A.ap, m.ap[:, 16:32, col], 1.0, 1.0, ALU.add, ALU.mult),
                 deps=[m.ready])
        t = P.op(DVE, lambda e: e.tensor_tensor(A.ap, A.ap, g.ap, ALU.mult), deps=[t, g.ready])
        A.wrote(t)
        sh = self.buf([16], F32)
        t2 = P.op(DVE, lambda e: e.tensor_copy(out=sh.ap, in_=m.ap[:, 0:16, col]), deps=[m.ready])
        sh.wrote(t2)
        gt = self.buf([16], F32)
        t3 = P.op(DVE, lambda e: e.tensor_copy(out=gt.ap, in_=m.ap[:, 32:48, col]), deps=[m.ready])
        gt.wrote(t3)
        return A, sh, gt

    def norm(self, xT, Tn, A, sh, hT=None, outF=None, tile=512):
        P = self.P
        ones = self.buf([128], BF16)
        t1 = P.op(DVE, lambda e: e.memset(ones.ap, 1.0))
        ones.wrote(t1)
        epsb = self.buf([1], F32)
        t1 = P.op(DVE, lambda e: e.memset(epsb.ap, EPS))
        epsb.wrote(t1)
        tile = min(tile, Tn)
        xb = [self.buf([16, tile], F32) for _ in range(2)]
        sq = self.buf([16, tile], BF16)
        rs = self.buf([tile], F32)
        tmp = [self.buf([tile], F32) for _ in range(2)]
        odt = F32 if outF is not None else BF16
        ob = [self.buf([16, tile], odt) for _ in range(1 if outF is not None else 2)]
        dst = outF if outF is not None else hT
        assert Tn % tile == 0
        nt = Tn // tile
        xv = xT.rearrange("(c p) t -> p c t", p=128)
        dv = dst.rearrange("(c p) t -> p c t", p=128)
        xb[0].load(xv[:, :, 0:tile])
        for i in range(nt):
            if i + 1 < nt:
                xb[(i + 1) % 2].load(xv[:, :, (i + 1) * tile:(i + 2) * tile])
            x = xb[i % 2]
            o = ob[i % len(ob)]
            t = P.op(ACT, lambda e, x=x: e.activation(out=sq.ap, in_=x.ap, func=AF.Square),
                     deps=[x.ready] + sq.readers)
            sq.wrote(t)
            bi = self.bank()
            ps = self.banks[bi][:, 0:tile]
            for c in range(16):
                tk = P.op(PE, lambda e, c=c, ps=ps: e.matmul(ps, ones.ap, sq.ap[:, c, :], start=(c == 0), stop=(c == 15)),
                          deps=[sq.ready, ones.ready, self.bank_free[bi]] if c == 0 else [], signal=(c == 15))
            sq.read(tk)
            t0_ = P.op(ACT, lambda e, ps=ps: e.activation(out=rs.ap, in_=ps, func=AF.Sqrt, scale=1.0 / 2048.0, bias=epsb.ap),
                       deps=[tk, epsb.ready] + rs.readers)
            t = P.op(DVE, lambda e: e.reciprocal(rs.ap, rs.ap), deps=[t0_])
            rs.wrote(t)
            self.bank_free[bi] = t
            last = []
            for c in range(16):
                tb = tmp[c % 2]
                t = P.op(DVE, lambda e, c=c, x=x, tb=tb: e.scalar_tensor_tensor(
                    out=tb.ap, in0=x.ap[:, c, :], scalar=A.ap[:, c:c + 1], in1=rs.ap, op0=ALU.mult, op1=ALU.mult),
                    deps=[rs.ready, A.ready, x.ready] + tb.readers)
                tb.wrote(t)
                if sh is not None:
                    t2 = P.op(ACT, lambda e, c=c, tb=tb, o=o: e.activation(
                        out=o.ap[:, c, :], in_=tb.ap, func=AF.Identity, bias=sh.ap[:, c:c + 1]),
                        deps=[t, sh.ready] + (o.readers if c == 0 else []))
                else:
                    t2 = P.op(ACT, lambda e, c=c, tb=tb, o=o: e.activation(out=o.ap[:, c, :], in_=tb.ap, func=AF.Copy),
                              deps=[t] + (o.readers if c == 0 else []))
                tb.read(t2)
                last.append(t2)
            x.read(last[-1])
            x.read(t)
            rs.read(t)
            o.wrote(last[-1])
            o.readers = []
            o.store(dv[:, :, i * tile:(i + 1) * tile])

    def ew_mul(self, a, b, out, rows, cols, dt_out=BF16):
        P = self.P
        R = rows // 128
        ct = min(cols, 512)
        rb = min(R, 8)
        av = a.rearrange("(c p) t -> p c t", p=128)
        bv = b.rearrange("(c p) t -> p c t", p=128)
        ov = out.rearrange("(c p) t -> p c t", p=128)
        ab = [self.buf([rb, ct], BF16) for _ in range(2)]
        bb = [self.buf([rb, ct], BF16) for _ in range(2)]
        ob = [self.buf([rb, ct], dt_out) for _ in range(2)]
        i = 0
        for r0 in range(0, R, rb):
            for c0 in range(0, cols, ct):
                A_, B_, O_ = ab[i % 2], bb[i % 2], ob[i % 2]
                A_.load(av[:, r0:r0 + rb, c0:c0 + ct])
                B_.load(bv[:, r0:r0 + rb, c0:c0 + ct], eng=SP)
                t = P.op(DVE if i % 2 == 0 else POOL, lambda e, A_=A_, B_=B_, O_=O_: e.tensor_tensor(O_.ap, A_.ap, B_.ap, ALU.mult),
                         deps=[A_.ready, B_.ready] + O_.readers)
                A_.read(t)
                B_.read(t)
                O_.wrote(t)
                O_.store(ov[:, r0:r0 + rb, c0:c0 + ct])
                i += 1

    def epi_resid(self, xT_in, xT_out, gate, nst=3):
        P = self.P
        xs = [self.buf([512], F32) for _ in range(nst)]
        os_ = [self.buf([512], F32) for _ in range(nst)]
        cnt = [0]

        def epi(m0, msz, n0, nsz, ps, tok):
            xb = xs[cnt[0] % nst]
            ob = os_[cnt[0] % nst]
            cnt[0] += 1
            xb.load(xT_in[m0:m0 + msz, n0:n0 + nsz], dst=xb.ap[0:msz, 0:nsz], eng=SP)
            j = m0 // 128
            t = P.op(DVE, lambda e: e.scalar_tensor_tensor(out=ob.ap[0:msz, 0:nsz], in0=ps, scalar=gate.ap[0:msz, j:j + 1],
                                                           in1=xb.ap[0:msz, 0:nsz], op0=ALU.mult, op1=ALU.add),
                     deps=[tok, xb.ready, gate.ready] + ob.readers)
            xb.read(t)
            ob.wrote(t)
            ob.store(xT_out[m0:m0 + msz, n0:n0 + nsz], src=ob.ap[0:msz, 0:nsz])
            return t
        return epi

    def outproj(self, w_out, yT, xT_in, xT_out, gate, Tn):
        step = min(Tn, 1024)
        for n0 in range(0, Tn, step):
            off0 = self.off
            epi = self.epi_resid(xT_in[:, n0:n0 + step], xT_out[:, n0:n0 + step], gate)
            self.gemm(4096, 2048, step, lambda a, b: w_out[:, a:b], lambda a, b, n0=n0: yT[:, n0 + a:n0 + b], 'R', epi)
            if n0 + step < Tn:
                self.P.barrier()
                self.off = off0
                self.bank_free = [None] * 8


def run_launch(build_fn, in_maps, out_names):
    nc = bass.Bass("TRN2", target_bir_lowering=False)
    with contextlib.ExitStack() as st:
        kb = KB(nc, st, ext_in=list(in_maps[0].keys()), ext_out=out_names)
        kb.in_shapes = {k: (v.shape, v.dtype) for k, v in in_maps[0].items()}
        build_fn(kb)
        toks = kb.P.all_tokens()
        kb.P.wait(SP, toks)
        kb.P.wait(POOL, toks)
        kb.P.emit()
    res = run_bass_kernel_spmd(nc, in_maps, core_ids=list(range(8)))
    return res.results


def IN(kb, name):
    shape, dt = kb.in_shapes[name]
    return kb.D(name, list(shape), BF16 if dt == NPBF else F32)


def col48(v):
    return np.ascontiguousarray(v.reshape(-1, 128).T)


def build_MOD(kb):
    cc = IN(kb, "cc")
    scT = kb.D("scT", [2048, 2], BF16)
    kb.silu_cols(cc, scT)
    kb.new_phase()
    modD = kb.D("modD", [6144, 2], F32)
    kb.mod(IN(kb, "ada_w"), IN(kb, "ada_b48"), scT, modD)


def front(kb, Tn, has_ctx, x_name="xT", h_name="hT"):
    modD = IN(kb, "modD")
    A, sh, gt = kb.mod_cols(modD, IN(kb, "g16"), 0)
    hT = kb.D(h_name, [2048, Tn], BF16)
    kb.norm(IN(kb, x_name), Tn, A, sh, hT=hT, tile=(384 if Tn == 2304 else 512))
    if has_ctx:
        kb.new_phase()
        A, sh, gt = kb.mod_cols(modD, IN(kb, "g16"), 1)
        hcT = kb.D("hcT", [2048, 256], BF16)
        kb.norm(IN(kb, "xcT"), 256, A, sh, hT=hcT, tile=256)
    kb.new_phase()


def build_A(has_ctx):
    def f(kb):
        front(kb, 2048, has_ctx)
        hT = kb.D("hT")
        w = IN(kb, "w_in")
        U = kb.D("U", [2048, 4096], BF16)
        SZT = kb.D("SZT", [4096, 2048], BF16)
        epi = kb.epi_act(lambda m0, ms, n0, ns: U[m0:m0 + ms, n0:n0 + ns])
        kb.gemm(2048, 2048, 4096, lambda a, b: hT[:, a:b], lambda a, b: w[:, a:b], 'L', epi)
        kb.new_phase()
        epi = kb.epi_act(lambda m0, ms, n0, ns: SZT[m0:m0 + ms, n0:n0 + ns], func=AF.Silu)
        kb.gemm(2048, 4096, 2048, lambda a, b: w[:, 4096 + a:4096 + b], lambda a, b: hT[:, a:b], 'R', epi)
        if has_ctx:
            kb.new_phase()
            hcT = kb.D("hcT")
            UC = kb.D("UC", [256, 4096], BF16)
            SZCT = kb.D("SZCT", [4096, 256], BF16)
            epi = kb.epi_act(lambda m0, ms, n0, ns: UC[m0:m0 + ms, n0:n0 + ns])
            kb.gemm(2048, 256, 4096, lambda a, b: hcT[:, a:b], lambda a, b: w[:, a:b], 'L', epi)
            kb.new_phase()
            epi = kb.epi_act(lambda m0, ms, n0, ns: SZCT[m0:m0 + ms, n0:n0 + ns], func=AF.Silu)
            kb.gemm(2048, 4096, 256, lambda a, b: w[:, 4096 + a:4096 + b], lambda a, b: hcT[:, a:b], 'R', epi)
    return f


def dft_consts():
    n2 = np.arange(128)
    ang = 2 * np.pi * np.outer(n2, n2) / 128.0
    sA = 1.0 / math.sqrt(128.0)
    CSa = np.concatenate([np.cos(ang), -np.sin(ang)], 1) * sA
    c = np.arange(256)
    angc = 2 * np.pi * np.outer(c, c) / 256.0
    Cc = np.cos(angc) / 16.0
    Sc = np.sin(angc) / 16.0
    CS256 = np.concatenate([np.cos(angc), -np.sin(angc)], 1) / 16.0
    n1 = np.arange(64)
    k1 = np.arange(64)
    TW = np.zeros((64, 256, 128), np.float32)
    for j in range(64):
        for e in range(2):
            k2 = 2 * j + e
            th = 2 * np.pi * (n1[:, None] * k2 / 8192.0 + np.outer(n1, k1) / 64.0)
            TW[j, e * 64:(e + 1) * 64, e * 64:(e + 1) * 64] = np.cos(th) / 8.0
            TW[j, 128 + e * 64:128 + (e + 1) * 64, e * 64:(e + 1) * 64] = np.sin(th) / 8.0
    bf = lambda a: np.ascontiguousarray(a.astype(np.float32)).astype(NPBF)
    return dict(CSa=bf(CSa), Cc=bf(Cc), Sc=bf(Sc), Scn=bf(-Sc), CS256=bf(CS256), TW=bf(TW))


def build_B(ngroups, has_ctx):
    def f(kb):
        UQ = IN(kb, "UQ")
        CSa = IN(kb, "CSa")
        A1 = kb.D("A1", [256, 65536], BF16)
        epi = kb.epi_act(lambda m0, ms, n0, ns: A1[m0:m0 + ms, n0:n0 + ns])
        kb.gemm(128, 256, 65536, lambda a, b: CSa[:, a:b], lambda a, b: UQ[:, a:b], 'L', epi)
        kb.new_phase()
        Wm = IN(kb, "Wm")
        MM = kb.D("MM", [3, ngroups * 256, 256], BF16)
        for g in range(ngroups):
            for t, nm in enumerate(("Cc", "Sc", "Scn")):
                Cm = IN(kb, nm)
                epi = kb.epi_act(lambda m0, ms, n0, ns, t=t, g=g: MM[t, g * 256 + m0:g * 256 + m0 + ms, n0:n0 + ns], nst=2)
                kb.gemm(256, 256, 256, lambda a, b: Cm[:, a:b], lambda a, b, g=g: Wm[g * 256:(g + 1) * 256, a:b], 'R', epi)
                kb.new_phase()
        if has_ctx:
            UC = IN(kb, "UC")
            CS256 = IN(kb, "CS256")
            AC = kb.D("AC", [4096, 512], BF16)
            epi = kb.epi_act(lambda m0, ms, n0, ns: AC[m0:m0 + ms, n0:n0 + ns])
            kb.gemm(256, 4096, 512, lambda a, b: UC[:, a:b], lambda a, b: CS256[:, a:b], 'R', epi)
    return f


def build_C(has_ctx):
    def f(kb):
        LB = IN(kb, "LB")
        RB = IN(kb, "RB")
        B1 = kb.D("B1", [4, 512, 8192], BF16)
        for g in range(4):
            epi = kb.epi_act(lambda m0, ms, n0, ns, g=g: B1[g, m0:m0 + ms, n0:n0 + ns])
            kb.gemm(512, 512, 8192, lambda a, b, g=g: RB[g, :, a:b], lambda a, b, g=g: LB[g, :, a:b], 'L', epi)
            kb.new_phase()
        if has_ctx:
            LCc = IN(kb, "LCc")
            RCc = IN(kb, "RCc")
            FCT = kb.D("FCT", [4096, 256], BF16)
            for g in range(16):
                epi = kb.epi_act(lambda m0, ms, n0, ns, g=g: FCT[g * 256 + m0:g * 256 + m0 + ms, n0:n0 + ns], nst=2)
                kb.gemm(512, 256, 256, lambda a, b, g=g: LCc[g, :, a:b], lambda a, b, g=g: RCc[g, :, a:b], 'R', epi)
                kb.new_phase()
    return f


def build_Dd(kb):
    LC = IN(kb, "LC")
    TW = IN(kb, "TW")
    FQ = kb.D("FQ", [64, 128, 1024], BF16)
    for j in range(64):
        epi = kb.epi_act(lambda m0, ms, n0, ns, j=j: FQ[j, m0:m0 + ms, n0:n0 + ns], nst=2)
        kb.gemm(256, 128, 1024, lambda a, b, j=j: TW[j, :, a:b], lambda a, b, j=j: LC[j, :, a:b], 'L', epi, sblk=1024)
        kb.new_phase()


def build_E(has_ctx, final):
    def f(kb):
        FT = IN(kb, "FT")
        SZT = IN(kb, "SZT")
        YT = kb.D("YT", [4096, 2048], BF16)
        kb.ew_mul(FT, SZT, YT, 4096, 2048)
        kb.new_phase()
        modD = IN(kb, "modD")
        A, sh, gt = kb.mod_cols(modD, IN(kb, "g16"), 0)
        xo = kb.D("xoT", [2048, 2048], F32)
        kb.outproj(IN(kb, "w_out"), YT, IN(kb, "xT"), xo, gt, 2048)
        if has_ctx:
            kb.new_phase()
            YCT = kb.D("YCT", [4096, 256], BF16)
            kb.ew_mul(IN(kb, "FCT"), IN(kb, "SZCT"), YCT, 4096, 256)
            kb.new_phase()
            A, sh, gtc = kb.mod_cols(modD, IN(kb, "g16"), 1)
            xco = kb.D("xcoT", [2048, 256], F32)
            kb.outproj(IN(kb, "w_out"), YCT, IN(kb, "xcT"), xco, gtc, 256)
        if final:
            kb.new_phase()
            fg = kb.load_const(IN(kb, "fg16"), [16], F32)
            outT = kb.D("outT", [2048, 2048], F32)
            kb.norm(xo, 2048, fg, None, outF=outT)
    return f


def T(a):
    return np.ascontiguousarray(np.asarray(a).T)


_CONSTS = {}


def consts():
    if not _CONSTS:
        _CONSTS.update(dft_consts())
    return _CONSTS


def run_mods(c, c_ctx, ada_w, ada_b):
    in_maps = []
    for core in range(8):
        b, i = core // 4, core % 4
        in_maps.append({"cc": np.ascontiguousarray(np.stack([c[b], c_ctx], 1)).astype(np.float32),
                        "ada_w": np.ascontiguousarray(ada_w[i]), "ada_b48": col48(ada_b[i])})
    res = run_launch(build_MOD, in_maps, ["modD"])
    return [[np.asarray(res[b * 4 + i]["modD"]) for b in range(2)] for i in range(4)]


def fnet_layer(xT, xcT, mods_i, g16, w_in, w_mix, w_out_i, has_ctx, final_g16=None):
    C = consts()
    in_maps = []
    for core in range(8):
        b = core // 4
        m = {"modD": mods_i[b], "g16": g16, "xT": xT[core], "w_in": w_in}
        if has_ctx:
            m["xcT"] = xcT[b]
        in_maps.append(m)
    outs = ["U", "SZT"] + (["UC", "SZCT"] if has_ctx else [])
    rA = run_launch(build_A(has_ctx), in_maps, outs)
    in_maps = []
    for core in range(8):
        b, q = core // 4, core % 4
        Ufull = np.concatenate([np.asarray(rA[b * 4 + s]["U"]) for s in range(4)], 0)
        UQ = np.ascontiguousarray(Ufull[:, q * 1024:(q + 1) * 1024]).reshape(128, 65536)
        m = {"UQ": UQ, "CSa": C["CSa"], "Cc": C["Cc"], "Sc": C["Sc"], "Scn": C["Scn"]}
        if has_ctx:
            m["Wm"] = np.ascontiguousarray(w_mix.reshape(16 * 256, 256))
            m["UC"] = np.asarray(rA[core]["UC"])
            m["CS256"] = C["CS256"]
        else:
            m["Wm"] = np.ascontiguousarray(w_mix[q * 4:(q + 1) * 4].reshape(4 * 256, 256))
        in_maps.append(m)
    ng = 16 if has_ctx else 4
    rB = run_launch(build_B(ng, has_ctx), in_maps, ["A1", "MM"] + (["AC"] if has_ctx else []))
    in_maps = []
    for core in range(8):
        q = core % 4
        A1 = np.asarray(rB[core]["A1"]).reshape(2, 128, 64, 4, 256)
        LB = np.ascontiguousarray(A1.transpose(3, 0, 4, 1, 2)).reshape(4, 512, 8192)
        MM = np.asarray(rB[core]["MM"]).reshape(3, ng, 256, 256)
        g0 = q * 4 if has_ctx else 0
        RB = np.zeros((4, 512, 512), NPBF)
        for gl in range(4):
            M1, M2, M2n = MM[0, g0 + gl], MM[1, g0 + gl], MM[2, g0 + gl]
            RB[gl, 0:256, 0:256] = M1
            RB[gl, 0:256, 256:512] = M2n
            RB[gl, 256:512, 0:256] = M2
            RB[gl, 256:512, 256:512] = M1
        m = {"LB": LB, "RB": RB}
        if has_ctx:
            AC = np.asarray(rB[core]["AC"]).reshape(16, 256, 2, 256)
            m["RCc"] = np.ascontiguousarray(AC.transpose(0, 2, 1, 3)).reshape(16, 512, 256)
            m["LCc"] = np.ascontiguousarray(np.concatenate([MM[0], MM[1]], 1))
        in_maps.append(m)
    rC = run_launch(build_C(has_ctx), in_maps, ["B1"] + (["FCT"] if has_ctx else []))
    in_maps = []
    for core in range(8):
        B1 = np.asarray(rC[core]["B1"]).reshape(4, 2, 256, 64, 2, 64)
        LC = np.ascontiguousarray(B1.transpose(3, 1, 4, 5, 0, 2)).reshape(64, 256, 1024)
        in_maps.append({"LC": LC, "TW": C["TW"]})
    rD = run_launch(build_Dd, in_maps, ["FQ"])
    Fq = []
    for core in range(8):
        FQ = np.asarray(rD[core]["FQ"]).reshape(64, 2, 64, 1024)
        Fq.append(np.ascontiguousarray(FQ.transpose(2, 0, 1, 3)).reshape(8192, 1024))
    in_maps = []
    for core in range(8):
        b, q = core // 4, core % 4
        FT = np.ascontiguousarray(np.concatenate([Fq[b * 4 + s][q * 2048:(q + 1) * 2048] for s in range(4)], 1).T)
        m = {"FT": FT, "SZT": np.asarray(rA[core]["SZT"]), "modD": mods_i[b], "g16": g16, "w_out": w_out_i,
             "xT": xT[core]}
        if has_ctx:
            m.update({"FCT": np.asarray(rC[core]["FCT"]), "SZCT": np.asarray(rA[core]["SZCT"]), "xcT": xcT[b]})
        if final_g16 is not None:
            m["fg16"] = final_g16
        in_maps.append(m)
    outs = ["xoT"] + (["xcoT"] if has_ctx else []) + (["outT"] if final_g16 is not None else [])
    rE = run_launch(build_E(has_ctx, final_g16 is not None), in_maps, outs)
    if final_g16 is not None:
        return [np.asarray(rE[k]["outT"]) for k in range(8)]
    new_x = [np.asarray(rE[k]["xoT"]) for k in range(8)]
    new_xc = [np.asarray(rE[b * 4]["xcoT"]) for b in range(2)] if has_ctx else xcT
    return new_x, new_xc


def build_F1(kb):
    front(kb, 2304, True)
    hT = kb.D("hT")
    hcT = kb.D("hcT")
    w = IN(kb, "w_in")
    QT0 = kb.D("QT0", [4096, 2048], BF16)
    KT0 = kb.D("KT0", [512, 2304], BF16)
    V = kb.D("V", [2304, 512], BF16)
    SZT = kb.D("SZT", [4096, 2048], BF16)
    KCT = kb.D("KCT", [512, 256], BF16)
    VC = kb.D("VC", [256, 512], BF16)
    epi = kb.epi_act(lambda m0, ms, n0, ns: QT0[m0:m0 + ms, n0:n0 + ns])
    kb.gemm(2048, 4096, 2048, lambda a, b: w[:, a:b], lambda a, b: hT[:, 128 + a:128 + b], 'R', epi)
    kb.new_phase()
    epi = kb.epi_act(lambda m0, ms, n0, ns: SZT[m0:m0 + ms, n0:n0 + ns], func=AF.Silu)
    kb.gemm(2048, 4096, 2048, lambda a, b: w[:, 5120 + a:5120 + b], lambda a, b: hT[:, 128 + a:128 + b], 'R', epi)
    kb.new_phase()
    epi = kb.epi_act(lambda m0, ms, n0, ns: KT0[m0:m0 + ms, n0:n0 + ns])
    kb.gemm(2048, 512, 2304, lambda a, b: w[:, 4096 + a:4096 + b], lambda a, b: hT[:, a:b], 'R', epi)
    kb.new_phase()
    epi = kb.epi_act(lambda m0, ms, n0, ns: V[m0:m0 + ms, n0:n0 + ns])
    kb.gemm(2048, 2304, 512, lambda a, b: hT[:, a:b], lambda a, b: w[:, 4608 + a:4608 + b], 'L', epi)
    kb.new_phase()
    epi = kb.epi_act(lambda m0, ms, n0, ns: KCT[m0:m0 + ms, n0:n0 + ns])
    kb.gemm(2048, 512, 256, lambda a, b: w[:, 4096 + a:4096 + b], lambda a, b: hcT[:, a:b], 'R', epi)
    kb.new_phase()
    epi = kb.epi_act(lambda m0, ms, n0, ns: VC[m0:m0 + ms, n0:n0 + ns])
    kb.gemm(2048, 256, 512, lambda a, b: hcT[:, a:b], lambda a, b: w[:, 4608 + a:4608 + b], 'L', epi)


def rope_tables(pos):
    rows = (pos // 64).astype(np.float64)
    cols = (pos % 64).astype(np.float64)
    inv = 10000.0 ** (-np.arange(16) / 16.0)
    cosT = np.zeros((64, len(pos)), np.float32)
    ssin = np.zeros((64, len(pos)), np.float32)
    for d in range(64):
        half, wv = d // 32, d % 32
        f, part = wv % 16, wv // 16
        ang = (rows if half == 0 else cols) * inv[f]
        cosT[d] = np.cos(ang)
        ssin[d] = np.sin(ang) * (-1.0 if part == 0 else 1.0)
    return np.ascontiguousarray(np.tile(cosT, (2, 1))), np.ascontiguousarray(np.tile(ssin, (2, 1)))


def rope_perm_rows(nheads):
    idx = np.arange(nheads * 64).reshape(nheads, 64)
    d = np.arange(64)
    wv = d % 32
    partner = np.where(wv // 16 == 0, d + 16, d - 16)
    return idx[:, partner].reshape(-1)


def build_G1a(kb):
    P = kb.P
    for (nm, rows, Tn) in (("Q", 4096, 2048), ("K", 512, 2304)):
        a = IN(kb, nm + "T0")
        bp = IN(kb, nm + "T0p")
        cosD = IN(kb, "cos" + nm)
        sinD = IN(kb, "ssin" + nm)
        out = kb.D(nm + "Tr", [rows, Tn], BF16)
        cs = kb.load_const(cosD, [Tn], F32)
        sn = kb.load_const(sinD, [Tn], F32)
        ab = [kb.buf([Tn], BF16) for _ in range(2)]
        bb = [kb.buf([Tn], BF16) for _ in range(2)]
        t1 = [kb.buf([Tn], F32) for _ in range(2)]
        t2 = [kb.buf([Tn], F32) for _ in range(2)]
        ob = [kb.buf([Tn], BF16) for _ in range(2)]
        for ch in range(rows // 128):
            i = ch % 2
            A_, B_, T1, T2, O_ = ab[i], bb[i], t1[i], t2[i], ob[i]
            A_.load(a[ch * 128:(ch + 1) * 128, :])
            B_.load(bp[ch * 128:(ch + 1) * 128, :], eng=SP)
            ta = P.op(DVE, lambda e, A_=A_, T1=T1, cs=cs: e.tensor_tensor(T1.ap, A_.ap, cs.ap, ALU.mult),
                      deps=[A_.ready, cs.ready] + T1.readers)
            T1.wrote(ta)
            A_.read(ta)
            tb = P.op(POOL, lambda e, B_=B_, T2=T2, sn=sn: e.tensor_tensor(T2.ap, B_.ap, sn.ap, ALU.mult),
                      deps=[B_.ready, sn.ready] + T2.readers)
            T2.wrote(tb)
            B_.read(tb)
            tc = P.op(DVE, lambda e, T1=T1, T2=T2, O_=O_: e.tensor_tensor(O_.ap, T1.ap, T2.ap, ALU.add),
                      deps=[ta, tb] + O_.readers)
            T1.read(tc)
            T2.read(tc)
            O_.wrote(tc)
            O_.store(out[ch * 128:(ch + 1) * 128, :])
        kb.new_phase()


def attn_core(kb, QTr, KTr, V, KCT, VC, SZT, YT, maskD, sinkD):
    P = kb.P
    Qv = QTr.rearrange("(h g d) t -> h d g t", g=8, d=64)
    Sv = SZT.rearrange("(h g d) t -> h d g t", g=8, d=64)
    Yv = YT.rearrange("(h g d) t -> h d g t", g=8, d=64)
    masks = kb.load_const(maskD.rearrange("k p n -> p k n"), [4, 512], BF16)
    srow = kb.buf([8192], F32)
    srow.load(sinkD, dst=srow.ap[0:1])
    esrow = kb.buf([8192], BF16)
    t = P.op(ACT, lambda e: e.activation(out=esrow.ap[0:1], in_=srow.ap[0:1], func=AF.Exp), deps=[srow.ready])
    esrow.wrote(t)
    ones = kb.buf([64], BF16)
    t = P.op(DVE, lambda e: e.memset(ones.ap, 1.0))
    ones.wrote(t)
    Qh = [kb.buf([8, 2048], BF16) for _ in range(2)]
    Kh = [kb.buf([2560], BF16) for _ in range(2)]
    Vh = [kb.buf([20, 64], BF16) for _ in range(2)]
    Pt = [kb.buf([5, 512], BF16) for _ in range(3)]
    rD = [kb.buf([512], F32) for _ in range(2)]
    yt = [kb.buf([512], F32) for _ in range(2)]
    szb = [kb.buf([8, 128], BF16) for _ in range(3)]
    yb = [kb.buf([8, 128], BF16) for _ in range(3)]
    kv = {}

    def load_hk(hk):
        Q_, K_, V_ = Qh[hk % 2], Kh[hk % 2], Vh[hk % 2]
        Q_.load(Qv[hk], dst=Q_.ap[0:64])
        K_.load(KTr[hk * 64:(hk + 1) * 64, :], dst=K_.ap[0:64, 0:2304], eng=SP)
        tk2 = K_.ready
        K_.readers = []
        t2 = P.dma(SP, K_.sem, K_.ap[0:64, 2304:2560], KCT[hk * 64:(hk + 1) * 64, :])
        K_.ready = t2
        V_.load(V[:, hk * 64:(hk + 1) * 64].rearrange("(b p) d -> p b d", p=128), dst=V_.ap[:, 0:18, :], eng=SP)
        tv1 = V_.ready
        V_.readers = []
        tv2 = P.dma(SP, V_.sem, V_.ap[:, 18:20, :], VC[:, hk * 64:(hk + 1) * 64].rearrange("(b p) d -> p b d", p=128))
        V_.ready = tv2
        kv[hk] = ([tk2, t2], [tv1, tv2])

    steps = [(hk, b, half) for hk in range(8) for b in range(16) for half in range(2)]
    st = {}

    def emit_S(i):
        hk, b, half = steps[i]
        if b == 0 and half == 0:
            if hk == 0:
                load_hk(0)
            if hk + 1 < 8:
                load_hk(hk + 1)
        Q_, K_ = Qh[hk % 2], Kh[hk % 2]
        kdeps = kv[hk][0]
        P_ = Pt[i % 3]
        rhs = Q_.ap[0:64, 4 * half:4 * half + 4, b * 128:(b + 1) * 128]
        blks = [b, b + 1, b + 2, 18, 19]
        pt_toks = []
        tm = None
        for j, blk in enumerate(blks):
            bi = kb.bank()
            ps = kb.banks[bi][:, 0:512]
            kcols = K_.ap[0:64, blk * 128:(blk + 1) * 128]
            tm = P.op(PE, lambda e, ps=ps, kcols=kcols, rhs=rhs: e.matmul(ps, kcols, rhs, start=True, stop=True),
                      deps=[kb.bank_free[bi], Q_.ready] + kdeps)
            te = P.op(ACT, lambda e, ps=ps, P_=P_, j=j: e.activation(out=P_.ap[:, j, :], in_=ps, func=AF.Exp, scale=0.125),
                      deps=[tm] + (P_.readers if j == 0 else []))
            kb.bank_free[bi] = te
            if j in (0, 2):
                mi = (0 if j == 0 else 1)
                if j == 0 and b == 0:
                    mi = 2
                if j == 2 and b == 15:
                    mi = 3
                te = P.op(DVE, lambda e, P_=P_, j=j, mi=mi: e.tensor_tensor(P_.ap[:, j, :], P_.ap[:, j, :], masks.ap[:, mi, :], ALU.mult),
                          deps=[te, masks.ready])
            pt_toks.append(te)
        Q_.read(tm)
        K_.read(tm)
        P_.wrote(pt_toks[-1])
        st[i] = pt_toks

    def emit_PV(i):
        hk, b, half = steps[i]
        V_ = Vh[hk % 2]
        vdeps = kv[hk][1]
        P_ = Pt[i % 3]
        pt_toks = st.pop(i)
        blks = [b, b + 1, b + 2, 18, 19]
        it = (hk * 16 + b) % 3
        SZ_, Y_ = szb[it], yb[it]
        if half == 0:
            SZ_.load(Sv[hk][:, :, b * 128:(b + 1) * 128], dst=SZ_.ap[0:64], eng=SP)
        bo, bd = kb.bank(), kb.bank()
        pso = kb.banks[bo][0:64, 0:512]
        psd = kb.banks[bd][0:64, 0:512]
        to = None
        for j, blk in enumerate(blks):
            to = P.op(PE, lambda e, pso=pso, V_=V_, blk=blk, P_=P_, j=j: e.matmul(pso, V_.ap[:, blk, :], P_.ap[:, j, :], start=(j == 0), stop=(j == 4)),
                      deps=(pt_toks + vdeps + [kb.bank_free[bo]]) if j == 0 else [], signal=(j == 4))
        for j in range(5):
            P.op(PE, lambda e, psd=psd, P_=P_, j=j: e.matmul(psd, ones.ap, P_.ap[:, j, :], start=(j == 0), stop=False),
                 deps=[kb.bank_free[bd], ones.ready] if j == 0 else [], signal=False)
        h0 = hk * 8 + 4 * half
        td = P.op(PE, lambda e, psd=psd, h0=h0: e.matmul(psd, ones.ap[0:1, :], esrow.ap[0:1, h0 * 128:h0 * 128 + 512], start=False, stop=True),
                  deps=[esrow.ready])
        P_.read(td)
        V_.read(td)
        R_, T_ = rD[half], yt[half]
        tr = P.op(DVE, lambda e, psd=psd, R_=R_: e.reciprocal(R_.ap[0:64], psd), deps=[td] + R_.readers)
        R_.wrote(tr)
        kb.bank_free[bd] = tr
        ty = P.op(DVE, lambda e, pso=pso, R_=R_, T_=T_: e.tensor_tensor(T_.ap[0:64], pso, R_.ap[0:64], ALU.mult),
                  deps=[to, tr] + T_.readers)
        T_.wrote(ty)
        R_.read(ty)
        kb.bank_free[bo] = ty
        tz = P.op(POOL, lambda e, T_=T_, Y_=Y_, SZ_=SZ_, half=half: e.tensor_tensor(
            Y_.ap[0:64, 4 * half:4 * half + 4, :], T_.ap[0:64].rearrange("p (g q) -> p g q", g=4),
            SZ_.ap[0:64, 4 * half:4 * half + 4, :], ALU.mult),
            deps=[ty, SZ_.ready] + (Y_.readers if half == 0 else []))
        T_.read(tz)
        if half == 1:
            SZ_.read(tz)
            Y_.wrote(tz)
            Y_.store(Yv[hk][:, :, b * 128:(b + 1) * 128], src=Y_.ap[0:64])

    n = len(steps)
    emit_S(0)
    for i in range(n):
        if i + 1 < n:
            emit_S(i + 1)
        emit_PV(i)


def build_G1b(kb):
    build_G1a(kb)
    YT = kb.D("YT", [4096, 2048], BF16)
    attn_core(kb, kb.D("QTr"), kb.D("KTr"), IN(kb, "V"), IN(kb, "KCT"), IN(kb, "VC"), IN(kb, "SZT"), YT,
              IN(kb, "maskx"), IN(kb, "sinkrep"))
    kb.new_phase()
    A, sh, gt = kb.mod_cols(IN(kb, "modD"), IN(kb, "g16"), 0)
    xo = kb.D("xoT", [2048, 2048], F32)
    kb.outproj(IN(kb, "w_out"), YT, IN(kb, "xT"), xo, gt, 2048)


def attn_layer(xT, xcT, mods_i, g16, w_in, sink, w_out_i):
    in_maps = []
    for core in range(8):
        b, q = core // 4, core % 4
        left = xT[core - 1][:, -128:] if q > 0 else np.zeros((2048, 128), np.float32)
        right = xT[core + 1][:, :128] if q < 3 else np.zeros((2048, 128), np.float32)
        xh = np.ascontiguousarray(np.concatenate([left, xT[core], right], 1))
        in_maps.append({"modD": mods_i[b], "g16": g16, "xT": xh, "xcT": xcT[b], "w_in": w_in})
    rF = run_launch(build_F1, in_maps, ["QT0", "KT0", "V", "SZT", "KCT", "VC"])
    pq, pk = rope_perm_rows(64), rope_perm_rows(8)
    in_maps = []
    for core in range(8):
        q = core % 4
        cq, sq = rope_tables(q * 2048 + np.arange(2048))
        ck, sk = rope_tables(np.abs(q * 2048 - 128 + np.arange(2304)))
        QT0 = np.asarray(rF[core]["QT0"])
        KT0 = np.asarray(rF[core]["KT0"])
        in_maps.append({"QT0": QT0, "QT0p": np.ascontiguousarray(QT0[pq]), "KT0": KT0, "KT0p": np.ascontiguousarray(KT0[pk]),
                        "cosQ": cq, "ssinQ": sq, "cosK": ck, "ssinK": sk})
    rope_maps = in_maps
    kj = np.arange(128)[:, None]
    qi = np.arange(128)[None, :]
    mprev = np.tile((kj >= qi).astype(np.float32), (1, 4))
    mnext = np.tile((kj <= qi).astype(np.float32), (1, 4))
    zero = np.zeros_like(mprev)
    in_maps = []
    for core in range(8):
        b, q = core // 4, core % 4
        maskx = np.stack([mprev, mnext, zero if q == 0 else mprev, zero if q == 3 else mnext], 0).astype(NPBF)
        in_maps.append({**rope_maps[core], "V": np.asarray(rF[core]["V"]), "KCT": np.asarray(rF[core]["KCT"]), "VC": np.asarray(rF[core]["VC"]),
                        "SZT": np.asarray(rF[core]["SZT"]), "maskx": maskx,
                        "sinkrep": np.ascontiguousarray(np.repeat(sink.astype(np.float32), 128)[None, :]),
                        "modD": mods_i[b], "g16": g16, "w_out": w_out_i, "xT": xT[core]})
    rH = run_launch(build_G1b, in_maps, ["xoT"])
    return [np.asarray(rH[k]["xoT"]) for k in range(8)]


AXX = mybir.AxisListType.X


def build_H2(kb):
    P = kb.P
    front(kb, 2048, False)
    hT = kb.D("hT")
    w = IN(kb, "w_in")
    U = kb.D("U", [2048, 4096], BF16)
    Vf = kb.D("Vf", [2048, 4096], F32)
    SZ = kb.D("SZ", [2048, 4096], BF16)
    e_u = kb.epi_act(lambda m0, ms, n0, ns: U[m0:m0 + ms, n0:n0 + ns], func=AF.Gelu_apprx_tanh, nst=3)
    e_v = kb.epi_act(lambda m0, ms, n0, ns: Vf[m0:m0 + ms, n0 - 4096:n0 - 4096 + ns], func=AF.Gelu_apprx_tanh, dt=F32, nst=3)
    e_z = kb.epi_act(lambda m0, ms, n0, ns: SZ[m0:m0 + ms, n0 - 8192:n0 - 8192 + ns], func=AF.Silu, nst=3)

    def epi(m0, ms, n0, ns, ps, tok):
        return (e_u if n0 < 4096 else (e_v if n0 < 8192 else e_z))(m0, ms, n0, ns, ps, tok)
    kb.gemm(2048, 2048, 12288, lambda a, b: hT[:, a:b], lambda a, b: w[:, a:b], 'L', epi)
    kb.new_phase()
    VN = kb.D("VN", [2048, 4096], BF16)
    G = kb.load_const(IN(kb, "lnG"), [4096], F32)
    Bt = kb.load_const(IN(kb, "lnB"), [4096], F32, eng=SP)
    epsb = kb.buf([1], F32)
    t = P.op(DVE, lambda e: e.memset(epsb.ap, EPS))
    epsb.wrote(t)
    vb = [kb.buf([4096], F32) for _ in range(2)]
    sq = kb.buf([4096], F32)
    vh = kb.buf([4096], F32)
    ob = [kb.buf([4096], BF16) for _ in range(2)]
    st = [kb.buf([8], F32) for _ in range(2)]
    vb[0].load(Vf[0:128, :])
    for i in range(16):
        if i + 1 < 16:
            vb[(i + 1) % 2].load(Vf[(i + 1) * 128:(i + 2) * 128, :])
        v, o, s = vb[i % 2], ob[i % 2], st[i % 2]
        t_sq = P.op(ACT, lambda e, v=v: e.activation(out=sq.ap, in_=v.ap, func=AF.Square), deps=[v.ready] + sq.readers)
        sq.wrote(t_sq)
        t1 = P.op(DVE, lambda e, v=v, s=s: e.reduce_sum(out=s.ap[:, 0:1], in_=v.ap, axis=AXX), deps=[v.ready] + s.readers)
        t2 = P.op(DVE, lambda e, s=s: e.reduce_sum(out=s.ap[:, 1:2], in_=sq.ap, axis=AXX), deps=[t_sq, t1])
        sq.read(t2)
        t3 = P.op(DVE, lambda e, s=s: e.tensor_scalar(s.ap[:, 2:3], s.ap[:, 0:1], 1.0 / 4096.0, None, ALU.mult), deps=[t2])
        t4 = P.op(DVE, lambda e, s=s: e.tensor_tensor(s.ap[:, 3:4], s.ap[:, 2:3], s.ap[:, 2:3], ALU.mult), deps=[t3])
        t5 = P.op(DVE, lambda e, s=s: e.scalar_tensor_tensor(out=s.ap[:, 4:5], in0=s.ap[:, 1:2], scalar=1.0 / 4096.0,
                                                             in1=s.ap[:, 3:4], op0=ALU.mult, op1=ALU.subtract), deps=[t4])
        t6 = P.op(ACT, lambda e, s=s: e.activation(out=s.ap[:, 5:6], in_=s.ap[:, 4:5], func=AF.Sqrt, bias=epsb.ap), deps=[t5, epsb.ready])
        t7 = P.op(DVE, lambda e, s=s: e.reciprocal(s.ap[:, 5:6], s.ap[:, 5:6]), deps=[t6])
        t8 = P.op(DVE, lambda e, s=s: e.scalar_tensor_tensor(out=s.ap[:, 6:7], in0=s.ap[:, 2:3], scalar=-1.0,
                                                             in1=s.ap[:, 5:6], op0=ALU.mult, op1=ALU.mult), deps=[t7])
        t9 = P.op(ACT, lambda e, v=v, s=s: e.activation(out=vh.ap, in_=v.ap, func=AF.Identity, scale=s.ap[:, 5:6], bias=s.ap[:, 6:7]),
                  deps=[t8] + vh.readers)
        vh.wrote(t9)
        v.read(t9)
        v.read(t2)
        t10 = P.op(DVE, lambda e: e.tensor_tensor(vh.ap, vh.ap, G.ap, ALU.mult), deps=[t9, G.ready])
        t11 = P.op(DVE, lambda e, o=o: e.tensor_tensor(o.ap, vh.ap, Bt.ap, ALU.add), deps=[t10, Bt.ready] + o.readers)
        vh.read(t11)
        s.read(t11)
        o.wrote(t11)
        o.store(VN[i * 128:(i + 1) * 128, :])
    kb.new_phase()
    Y = kb.D("Y", [2048, 4096], BF16)
    WsT = IN(kb, "WsT")
    bs = kb.load_const(IN(kb, "bsT"), [16], F32)

    def view(D_, g, kk):
        return D_.rearrange("(k t) (g c) -> g t k c", t=128, c=256)[g][:, kk:kk + 2, :]
    wt = [kb.buf([128], BF16) for _ in range(2)]
    rb = [kb.buf([2, 256], BF16) for _ in range(2)]
    ub = [kb.buf([2, 256], BF16) for _ in range(2)]
    zb = [kb.buf([2, 256], BF16) for _ in range(2)]
    sb_ = [kb.buf([512], F32) for _ in range(2)]
    yb = [kb.buf([2, 256], BF16) for _ in range(2)]
    it = 0
    for g in range(16):
        W_ = wt[g % 2]
        W_.load(WsT[g])
        for kk in range(0, 16, 2):
            i = it % 2
            it += 1
            R_, U_, Z_, S_, Y_ = rb[i], ub[i], zb[i], sb_[i], yb[i]
            R_.load(view(VN, g, kk))
            U_.load(view(U, g, kk), eng=SP)
            Z_.load(view(SZ, g, kk), eng=SP)
            bi = kb.bank()
            ps = kb.banks[bi][:, 0:512]
            tm = P.op(PE, lambda e, ps=ps, W_=W_, R_=R_: e.matmul(ps, W_.ap, R_.ap.rearrange("p k c -> p (k c)"), start=True, stop=True),
                      deps=[W_.ready, R_.ready, kb.bank_free[bi]])
            W_.read(tm)
            R_.read(tm)
            ts = P.op(ACT, lambda e, ps=ps, S_=S_, g=g: e.activation(out=S_.ap, in_=ps, func=AF.Identity, bias=bs.ap[:, g:g + 1]),
                      deps=[tm, bs.ready] + S_.readers)
            kb.bank_free[bi] = ts
            S_.wrote(ts)
            ta = P.op(DVE, lambda e, S_=S_, U_=U_: e.tensor_tensor(S_.ap, S_.ap, U_.ap.rearrange("p k c -> p (k c)"), ALU.mult),
                      deps=[ts, U_.ready])
            U_.read(ta)
            tb = P.op(DVE, lambda e, S_=S_, Z_=Z_, Y_=Y_: e.tensor_tensor(Y_.ap.rearrange("p k c -> p (k c)"), S_.ap,
                                                                         Z_.ap.rearrange("p k c -> p (k c)"), ALU.mult),
                      deps=[ta, Z_.ready] + Y_.readers)
            Z_.read(tb)
            S_.read(tb)
            Y_.wrote(tb)
            Y_.store(view(Y, g, kk))


def build_I(final):
    def f(kb):
        A, sh, gt = kb.mod_cols(IN(kb, "modD"), IN(kb, "g16"), 0)
        xo = kb.D("xoT", [2048, 2048], F32)
        kb.outproj(IN(kb, "w_out"), IN(kb, "YT"), IN(kb, "xT"), xo, gt, 2048)
    return f


def gmlp_layer(xT, mods_i, g16, w_in, w_s, b_s, ln_g, ln_b, w_out_i):
    in_maps = []
    for core in range(8):
        b = core // 4
        in_maps.append({"modD": mods_i[b], "g16": g16, "xT": xT[core], "w_in": w_in,
                        "lnG": np.ascontiguousarray(np.broadcast_to(ln_g[None, :], (128, 4096))).astype(np.float32),
                        "lnB": np.ascontiguousarray(np.broadcast_to(ln_b[None, :], (128, 4096))).astype(np.float32),
                        "WsT": np.ascontiguousarray(w_s.transpose(0, 2, 1)), "bsT": T(b_s)})
    rH = run_launch(build_H2, in_maps, ["Y"])
    in_maps = []
    for core in range(8):
        b = core // 4
        in_maps.append({"YT": T(rH[core]["Y"]), "modD": mods_i[b], "g16": g16, "w_out": w_out_i, "xT": xT[core]})
    rI = run_launch(build_I(False), in_maps, ["xoT"])
    return [np.asarray(rI[k]["xoT"]) for k in range(8)]


def kernel(x, c, ctx, c_ctx, norm_g, ada_w, ada_b, w_out, fnet_w_in, fnet_w_mix, attn_w_in, attn_sink,
           gmlp_w_in, gmlp_w_s, gmlp_b_s, gmlp_ln_g, gmlp_ln_b, final_g):
    f32 = lambda a: np.asarray(a, np.float32)
    x, c, ctx, c_ctx, norm_g, ada_w, ada_b, w_out = map(f32, (x, c, ctx, c_ctx, norm_g, ada_w, ada_b, w_out))
    fnet_w_in, fnet_w_mix, attn_w_in, attn_sink = map(f32, (fnet_w_in, fnet_w_mix, attn_w_in, attn_sink))
    gmlp_w_in, gmlp_w_s, gmlp_b_s, gmlp_ln_g, gmlp_ln_b, final_g = map(
        f32, (gmlp_w_in, gmlp_w_s, gmlp_b_s, gmlp_ln_g, gmlp_ln_b, final_g))
    mods = run_mods(c, c_ctx, ada_w, ada_b)
    xT = [T(x[k // 4, (k % 4) * 2048:(k % 4 + 1) * 2048]) for k in range(8)]
    xcT = [T(ctx[b]) for b in range(2)]
    xT, xcT = fnet_layer(xT, xcT, mods[0], col48(norm_g[0]), fnet_w_in[0], fnet_w_mix[0], w_out[0], True)
    xT = attn_layer(xT, xcT, mods[1], col48(norm_g[1]), attn_w_in[0], attn_sink[0], w_out[1])
    xT = gmlp_layer(xT, mods[2], col48(norm_g[2]), gmlp_w_in[0], gmlp_w_s[0], gmlp_b_s[0], gmlp_ln_g[0], gmlp_ln_b[0], w_out[2])
    oT = fnet_layer(xT, None, mods[3], col48(norm_g[3]), fnet_w_in[1], fnet_w_mix[1], w_out[3], False,
                    final_g16=col48(final_g))
    out = np.empty((2, 8192, 2048), np.float32)
    for k in range(8):
        out[k // 4, (k % 4) * 2048:(k % 4 + 1) * 2048] = oT[k].T
    return out
```

```python
import contextlib
import math
import numpy as np
import ml_dtypes
import concourse.bass as bass
import concourse.mybir as mybir
from concourse.bass_utils import run_bass_kernel_spmd

F32 = mybir.dt.float32
BF16 = mybir.dt.bfloat16
AF = mybir.ActivationFunctionType
ALU = mybir.AluOpType
NPBF = ml_dtypes.bfloat16

PE, ACT, DVE, POOL, SP = "pe", "act", "dve", "pool", "sp"
ENGS = (PE, ACT, DVE, POOL, SP)

D_MODEL = 2048
D_BRANCH = 4096
EPS = 1e-6
ARENA_F32 = 46 * 1024


class Prog:
    def __init__(self, nc, stack):
        self.nc = nc
        self.stack = stack
        self.q = {e: [] for e in ENGS}
        self.nsem = 0
        self.esem = {e: self.sem("e_" + e) for e in (PE, ACT, DVE, POOL)}
        self.ecnt = {e: 0 for e in (PE, ACT, DVE, POOL)}
        self.dsems = []
        self.dcnt = {}

    def sem(self, name):
        self.nsem += 1
        return self.stack.enter_context(self.nc.semaphore(f"{name}_{self.nsem}"))

    def dsem(self, name="d"):
        s = self.sem(name)
        self.dsems.append(s)
        self.dcnt[id(s)] = 0
        return s

    def op(self, eng, fn, deps=(), signal=True):
        tok = None
        inc = None
        if signal:
            self.ecnt[eng] += 1
            tok = (self.esem[eng], self.ecnt[eng])
            inc = (self.esem[eng], 1)
        self.q[eng].append((tuple(d for d in deps if d is not None), fn, inc))
        return tok

    def dma(self, eng, sem, out, in_, deps=()):
        self.dcnt[id(sem)] += 16
        tok = (sem, self.dcnt[id(sem)])
        self.q[eng].append((tuple(d for d in deps if d is not None),
                            (lambda e, out=out, in_=in_: e.dma_start(out=out, in_=in_)), (sem, 16)))
        return tok

    def coll(self, sem, kind, src, dst, groups, deps=()):
        self.dcnt[id(sem)] += 1
        tok = (sem, self.dcnt[id(sem)])
        self.q[POOL].append((tuple(d for d in deps if d is not None),
                             (lambda e: e.collective_compute(kind, ALU.bypass, replica_groups=groups,
                                                             ins=[src.opt()], outs=[dst.opt()])), (sem, 1)))
        return tok

    def wait(self, eng, deps):
        self.q[eng].append((tuple(d for d in deps if d is not None), None, None))

    def all_tokens(self):
        toks = [(self.esem[e], self.ecnt[e]) for e in self.ecnt if self.ecnt[e] > 0]
        toks += [(s, self.dcnt[id(s)]) for s in self.dsems if self.dcnt[id(s)] > 0]
        return toks

    def barrier(self):
        toks = self.all_tokens()
        for e in ENGS:
            self.wait(e, toks)

    def emit(self):
        nc = self.nc
        with nc.Block() as block:
            def replay(name):
                def run(e):
                    seen = {}
                    for deps, fn, inc in self.q[name]:
                        for (s, v) in deps:
                            k = id(s)
                            if seen.get(k, 0) < v:
                                e.wait_ge(s, v)
                                seen[k] = v
                        if fn is None:
                            continue
                        ins = fn(e)
                        if inc is not None:
                            ins.then_inc(inc[0], inc[1])
                return run
            block.tensor(replay(PE))
            block.scalar(replay(ACT))
            block.vector(replay(DVE))
            block.gpsimd(replay(POOL))
            block.sync(replay(SP))


class Buf:
    def __init__(self, kb, ap):
        self.kb = kb
        self.ap = ap
        self.sem = kb.get_dsem()
        self.ready = None
        self.readers = []

    def load(self, src, eng=POOL, deps=(), dst=None):
        P = self.kb.P
        t = P.dma(eng, self.sem, self.ap if dst is None else dst, src, deps=list(self.readers) + list(deps))
        self.ready = t
        self.readers = []
        return t

    def wrote(self, tok):
        self.ready = tok
        self.readers = []

    def read(self, tok):
        if tok is not None:
            self.readers.append(tok)
            if len(self.readers) > 24:
                self.readers = self.readers[-24:]

    def store(self, dst, deps=(), src=None, eng=SP):
        P = self.kb.P
        t = P.dma(eng, self.sem, dst, self.ap if src is None else src, deps=[self.ready] + list(deps))
        self.readers.append(t)
        return t


class KB:
    def __init__(self, nc, stack, ext_in=(), ext_out=()):
        self.nc = nc
        self.P = Prog(nc, stack)
        self.arena = stack.enter_context(nc.sbuf_tensor("arena", [128, ARENA_F32], F32))
        self.banks = [stack.enter_context(nc.psum_tensor(f"bank{i}", [128, 512], F32)) for i in range(8)]
        self.bank_free = [None] * 8
        self.bank_i = 0
        self.off = 0
        self.dram = {}
        self.ext_in = set(ext_in)
        self.ext_out = set(ext_out)
        self.out_toks = []

    def get_dsem(self):
        if not hasattr(self, "sem_pool"):
            self.sem_pool = []
            self.sem_next = 0
        if self.sem_next >= len(self.sem_pool):
            self.sem_pool.append(self.P.dsem("b"))
        s = self.sem_pool[self.sem_next]
        self.sem_next += 1
        return s

    def D(self, name, shape=None, dt=None):
        if name in self.dram:
            return self.dram[name]
        kind = "ExternalInput" if name in self.ext_in else ("ExternalOutput" if name in self.ext_out else "Internal")
        t = self.nc.dram_tensor(name, list(shape), dt, kind=kind).ap()
        self.dram[name] = t
        return t

    def alloc(self, shape, dt):
        n = int(np.prod(shape))
        nf32 = (n * (2 if dt == BF16 else 4) + 3) // 4
        nf32 = (nf32 + 7) // 8 * 8
        assert self.off + nf32 <= ARENA_F32, (self.off, nf32, shape)
        ap = self.arena[:, self.off:self.off + nf32]
        self.off += nf32
        if dt == BF16:
            ap = ap.bitcast(BF16)
        ap = ap[:, 0:n]
        if len(shape) == 2:
            ap = ap.rearrange("p (a b) -> p a b", a=shape[0])
        elif len(shape) == 3:
            ap = ap.rearrange("p (a b c) -> p a b c", a=shape[0], b=shape[1])
        elif len(shape) == 4:
            ap = ap.rearrange("p (a b c d) -> p a b c d", a=shape[0], b=shape[1], c=shape[2])
        return ap

    def buf(self, shape, dt):
        return Buf(self, self.alloc(shape, dt))

    def new_phase(self):
        self.P.barrier()
        self.off = 0
        self.sem_next = 0
        self.bank_free = [None] * 8

    def bank(self):
        i = self.bank_i
        self.bank_i = (i + 1) % 8
        return i

    def gemm(self, K, M, N, Lsrc, Rsrc, resident, epi, l_dt=BF16, r_dt=BF16, sblk=512, kp=128, deps=()):
        P = self.P
        KC = K // kp
        assert K % kp == 0

        def view(src):
            return src.rearrange("(c p) x -> p c x", p=kp)

        RESX = M if resident == 'L' else N
        STRX = N if resident == 'L' else M
        res = self.buf([KC, RESX], BF16)
        res_ap = res.ap if kp == 128 else res.ap[0:kp]
        rsrc = Lsrc if resident == 'L' else Rsrc
        ssrc = Rsrc if resident == 'L' else Lsrc
        nres_toks = []
        for x0 in range(0, RESX, 1024):
            x1 = min(RESX, x0 + 1024)
            nres_toks.append(res.load(view(rsrc(x0, x1)), deps=deps, dst=res_ap[:, :, x0:x1]))
            res.readers = []
        sblk = min(sblk, STRX)
        nblk = (STRX + sblk - 1) // sblk
        sb = [self.buf([KC, sblk], BF16) for _ in range(min(2, nblk))]

        def issue(s):
            b = sb[s % len(sb)]
            x0 = s * sblk
            x1 = min(STRX, x0 + sblk)
            dst = (b.ap if kp == 128 else b.ap[0:kp])[:, :, 0:x1 - x0]
            b.load(view(ssrc(x0, x1)), deps=deps, dst=dst)

        issue(0)
        for s in range(nblk):
            if s + 1 < nblk:
                issue(s + 1)
            b = sb[s % len(sb)]
            bap = b.ap if kp == 128 else b.ap[0:kp]
            x0 = s * sblk
            x1 = min(STRX, x0 + sblk)
            if resident == 'L':
                tiles = [(m0, min(M, m0 + 128), n0, min(x1, n0 + 512))
                         for m0 in range(0, M, 128) for n0 in range(x0, x1, 512)]
            else:
                tiles = [(m0, min(x1, m0 + 128), n0, min(N, n0 + 512))
                         for m0 in range(x0, x1, 128) for n0 in range(0, N, 512)]
            for (m0, m1, n0, n1) in tiles:
                bi = self.bank()
                ps = self.banks[bi][0:m1 - m0, 0:n1 - n0]
                tok = None
                for c in range(KC):
                    if resident == 'L':
                        lt = res_ap[:, c, m0:m1]
                        rt = bap[:, c, n0 - x0:n1 - x0]
                    else:
                        lt = bap[:, c, m0 - x0:m1 - x0]
                        rt = res_ap[:, c, n0:n1]
                    d = [self.bank_free[bi], b.ready] + nres_toks if c == 0 else []
                    last = (c == KC - 1)
                    tok = P.op(PE, (lambda e, ps=ps, lt=lt, rt=rt, c=c, last=last:
                                    e.matmul(ps, lt, rt, start=(c == 0), stop=last)),
                               deps=d, signal=last)
                b.read(tok)
                res.read(tok)
                self.bank_free[bi] = epi(m0, m1 - m0, n0, n1 - n0, ps, tok)

    def stagers(self, n, shape, dt):
        return [self.buf(shape, dt) for _ in range(n)]

    def epi_act(self, dst_fn, func=AF.Copy, dt=BF16, nst=4, alt=True, bias_fn=None):
        P = self.P
        st = self.stagers(nst, [512], dt)
        cnt = [0]

        def epi(m0, msz, n0, nsz, ps, tok):
            b = st[cnt[0] % nst]
            use_dve = alt and func == AF.Copy and bias_fn is None and (cnt[0] % 2 == 1)
            cnt[0] += 1
            o = b.ap[0:msz, 0:nsz]
            deps = [tok] + list(b.readers)
            if use_dve:
                t = P.op(DVE, lambda e: e.tensor_copy(out=o, in_=ps), deps=deps)
            elif bias_fn is not None:
                bcol = bias_fn(m0, msz)
                t = P.op(ACT, lambda e: e.activation(out=o, in_=ps, func=func, bias=bcol), deps=deps)
            else:
                t = P.op(ACT, lambda e: e.activation(out=o, in_=ps, func=func), deps=deps)
            b.wrote(t)
            b.store(dst_fn(m0, msz, n0, nsz), src=o)
            return t
        return epi

    def load_const(self, src, shape, dt, eng=POOL):
        b = self.buf(shape, dt)
        b.load(src, eng=eng)
        return b

    def silu_cols(self, cc, scT):
        P = self.P
        a = self.load_const(cc.rearrange("(c p) x -> p c x", p=128), [16, 2], F32)
        o = self.buf([16, 2], BF16)
        t = P.op(ACT, lambda e: e.activation(out=o.ap, in_=a.ap, func=AF.Silu), deps=[a.ready])
        o.wrote(t)
        o.store(scT.rearrange("(c p) x -> p c x", p=128))

    def mod(self, ada_w, ada_b48, scT, modD):
        bias = self.load_const(ada_b48, [48], F32)
        epi = self.epi_act(lambda m0, ms, n0, ns: modD[m0:m0 + ms, n0:n0 + ns], func=AF.Identity, dt=F32,
                           bias_fn=lambda m0, ms: bias.ap[0:ms, m0 // 128:m0 // 128 + 1])
        self.gemm(2048, 6144, 2, lambda a, b: ada_w[:, a:b], lambda a, b: scT[:, a:b], 'R', epi, deps=[bias.ready])

    def mod_cols(self, modD, g16, col):
        P = self.P
        m = self.load_const(modD.rearrange("(j p) x -> p j x", p=128), [48, 2], F32)
        g = self.load_const(g16, [16], F32)
        A = self.buf([16], F32)
        t = P.op(DVE, lambda e: e.tensor_scalar(A.ap, m.ap[:, 16:32, col], 1.0, 1.0, ALU.add, ALU.mult),
                 deps=[m.ready])
        t = P.op(DVE, lambda e: e.tensor_tensor(A.ap, A.ap, g.ap, ALU.mult), deps=[t, g.ready])
        A.wrote(t)
        sh = self.buf([16], F32)
        t2 = P.op(DVE, lambda e: e.tensor_copy(out=sh.ap, in_=m.ap[:, 0:16, col]), deps=[m.ready])
        sh.wrote(t2)
        gt = self.buf([16], F32)
        t3 = P.op(DVE, lambda e: e.tensor_copy(out=gt.ap, in_=m.ap[:, 32:48, col]), deps=[m.ready])
        gt.wrote(t3)
        return A, sh, gt

    def norm(self, xT, Tn, A, sh, hT=None, outF=None, tile=512):
        P = self.P
        ones = self.buf([128], BF16)
        t1 = P.op(DVE, lambda e: e.memset(ones.ap, 1.0))
        ones.wrote(t1)
        epsb = self.buf([1], F32)
        t1 = P.op(DVE, lambda e: e.memset(epsb.ap, EPS))
        epsb.wrote(t1)
        tile = min(tile, Tn)
        xb = [self.buf([16, tile], F32) for _ in range(2)]
        sq = self.buf([16, tile], BF16)
        rs = self.buf([tile], F32)
        tmp = [self.buf([tile], F32) for _ in range(2)]
        odt = F32 if outF is not None else BF16
        ob = [self.buf([16, tile], odt) for _ in range(1 if outF is not None else 2)]
        dst = outF if outF is not None else hT
        assert Tn % tile == 0
        nt = Tn // tile
        xv = xT.rearrange("(c p) t -> p c t", p=128)
        dv = dst.rearrange("(c p) t -> p c t", p=128)
        xb[0].load(xv[:, :, 0:tile])
        for i in range(nt):
            if i + 1 < nt:
                xb[(i + 1) % 2].load(xv[:, :, (i + 1) * tile:(i + 2) * tile])
            x = xb[i % 2]
            o = ob[i % len(ob)]
            t = P.op(ACT, lambda e, x=x: e.activation(out=sq.ap, in_=x.ap, func=AF.Square),
                     deps=[x.ready] + sq.readers)
            sq.wrote(t)
            bi = self.bank()
            ps = self.banks[bi][:, 0:tile]
            for c in range(16):
                tk = P.op(PE, lambda e, c=c, ps=ps: e.matmul(ps, ones.ap, sq.ap[:, c, :], start=(c == 0), stop=(c == 15)),
                          deps=[sq.ready, ones.ready, self.bank_free[bi]] if c == 0 else [], signal=(c == 15))
            sq.read(tk)
            t0_ = P.op(ACT, lambda e, ps=ps: e.activation(out=rs.ap, in_=ps, func=AF.Sqrt, scale=1.0 / 2048.0, bias=epsb.ap),
                       deps=[tk, epsb.ready] + rs.readers)
            t = P.op(DVE, lambda e: e.reciprocal(rs.ap, rs.ap), deps=[t0_])
            rs.wrote(t)
            self.bank_free[bi] = t
            last = []
            for c in range(16):
                tb = tmp[c % 2]
                t = P.op(DVE, lambda e, c=c, x=x, tb=tb: e.scalar_tensor_tensor(
                    out=tb.ap, in0=x.ap[:, c, :], scalar=A.ap[:, c:c + 1], in1=rs.ap, op0=ALU.mult, op1=ALU.mult),
                    deps=[rs.ready, A.ready, x.ready] + tb.readers)
                tb.wrote(t)
                if sh is not None:
                    t2 = P.op(ACT, lambda e, c=c, tb=tb, o=o: e.activation(
                        out=o.ap[:, c, :], in_=tb.ap, func=AF.Identity, bias=sh.ap[:, c:c + 1]),
                        deps=[t, sh.ready] + (o.readers if c == 0 else []))
                else:
                    t2 = P.op(ACT, lambda e, c=c, tb=tb, o=o: e.activation(out=o.ap[:, c, :], in_=tb.ap, func=AF.Copy),
                              deps=[t] + (o.readers if c == 0 else []))
                tb.read(t2)
                last.append(t2)
            x.read(last[-1])
            x.read(t)
            rs.read(t)
            o.wrote(last[-1])
            o.readers = []
            o.store(dv[:, :, i * tile:(i + 1) * tile])

    def ew_mul(self, a, b, out, rows, cols, dt_out=BF16):
        P = self.P
        R = rows // 128
        ct = min(cols, 512)
        rb = min(R, 8)
        av = a.rearrange("(c p) t -> p c t", p=128)
        bv = b.rearrange("(c p) t -> p c t", p=128)
        ov = out.rearrange("(c p) t -> p c t", p=128)
        ab = [self.buf([rb, ct], BF16) for _ in range(2)]
        bb = [self.buf([rb, ct], BF16) for _ in range(2)]
        ob = [self.buf([rb, ct], dt_out) for _ in range(2)]
        i = 0
        for r0 in range(0, R, rb):
            for c0 in range(0, cols, ct):
                A_, B_, O_ = ab[i % 2], bb[i % 2], ob[i % 2]
                A_.load(av[:, r0:r0 + rb, c0:c0 + ct])
                B_.load(bv[:, r0:r0 + rb, c0:c0 + ct], eng=SP)
                t = P.op(DVE if i % 2 == 0 else POOL, lambda e, A_=A_, B_=B_, O_=O_: e.tensor_tensor(O_.ap, A_.ap, B_.ap, ALU.mult),
                         deps=[A_.ready, B_.ready] + O_.readers)
                A_.read(t)
                B_.read(t)
                O_.wrote(t)
                O_.store(ov[:, r0:r0 + rb, c0:c0 + ct])
                i += 1

    def epi_resid(self, xT_in, xT_out, gate, nst=3):
        P = self.P
        xs = [self.buf([512], F32) for _ in range(nst)]
        os_ = [self.buf([512], F32) for _ in range(nst)]
        cnt = [0]

        def epi(m0, msz, n0, nsz, ps, tok):
            xb = xs[cnt[0] % nst]
            ob = os_[cnt[0] % nst]
            cnt[0] += 1
            xb.load(xT_in[m0:m0 + msz, n0:n0 + nsz], dst=xb.ap[0:msz, 0:nsz], eng=SP)
            j = m0 // 128
            t = P.op(DVE, lambda e: e.scalar_tensor_tensor(out=ob.ap[0:msz, 0:nsz], in0=ps, scalar=gate.ap[0:msz, j:j + 1],
                                                           in1=xb.ap[0:msz, 0:nsz], op0=ALU.mult, op1=ALU.add),
                     deps=[tok, xb.ready, gate.ready] + ob.readers)
            xb.read(t)
            ob.wrote(t)
            ob.store(xT_out[m0:m0 + msz, n0:n0 + nsz], src=ob.ap[0:msz, 0:nsz])
            return t
        return epi

    def outproj(self, w_out, yT, xT_in, xT_out, gate, Tn):
        step = min(Tn, 1024)
        for n0 in range(0, Tn, step):
            off0 = self.off
            epi = self.epi_resid(xT_in[:, n0:n0 + step], xT_out[:, n0:n0 + step], gate)
            self.gemm(4096, 2048, step, lambda a, b: w_out[:, a:b], lambda a, b, n0=n0: yT[:, n0 + a:n0 + b], 'R', epi)
            if n0 + step < Tn:
                self.P.barrier()
                self.off = off0
                self.bank_free = [None] * 8


def run_launch(build_fn, in_maps, out_names):
    nc = bass.Bass("TRN2", target_bir_lowering=False)
    with contextlib.ExitStack() as st:
        kb = KB(nc, st, ext_in=list(in_maps[0].keys()), ext_out=out_names)
        kb.in_shapes = {k: (v.shape, v.dtype) for k, v in in_maps[0].items()}
        build_fn(kb)
        toks = kb.P.all_tokens()
        kb.P.wait(SP, toks)
        kb.P.wait(POOL, toks)
        kb.P.emit()
    res = run_bass_kernel_spmd(nc, in_maps, core_ids=list(range(8)))
    return res.results


def IN(kb, name):
    shape, dt = kb.in_shapes[name]
    return kb.D(name, list(shape), BF16 if dt == NPBF else F32)


def col48(v):
    return np.ascontiguousarray(v.reshape(-1, 128).T)


def build_MOD(kb):
    cc = IN(kb, "cc")
    scT = kb.D("scT", [2048, 2], BF16)
    kb.silu_cols(cc, scT)
    kb.new_phase()
    modD = kb.D("modD", [6144, 2], F32)
    kb.mod(IN(kb, "ada_w"), IN(kb, "ada_b48"), scT, modD)


def front(kb, Tn, has_ctx, x_name="xT", h_name="hT"):
    modD = IN(kb, "modD")
    A, sh, gt = kb.mod_cols(modD, IN(kb, "g16"), 0)
    hT = kb.D(h_name, [2048, Tn], BF16)
    kb.norm(IN(kb, x_name), Tn, A, sh, hT=hT, tile=(384 if Tn == 2304 else 512))
    if has_ctx:
        kb.new_phase()
        A, sh, gt = kb.mod_cols(modD, IN(kb, "g16"), 1)
        hcT = kb.D("hcT", [2048, 256], BF16)
        kb.norm(IN(kb, "xcT"), 256, A, sh, hT=hcT, tile=256)
    kb.new_phase()


def build_A(has_ctx):
    def f(kb):
        front(kb, 2048, has_ctx)
        hT = kb.D("hT")
        w = IN(kb, "w_in")
        U = kb.D("U", [2048, 4096], BF16)
        SZT = kb.D("SZT", [4096, 2048], BF16)
        epi = kb.epi_act(lambda m0, ms, n0, ns: U[m0:m0 + ms, n0:n0 + ns])
        kb.gemm(2048, 2048, 4096, lambda a, b: hT[:, a:b], lambda a, b: w[:, a:b], 'L', epi)
        kb.new_phase()
        epi = kb.epi_act(lambda m0, ms, n0, ns: SZT[m0:m0 + ms, n0:n0 + ns], func=AF.Silu)
        kb.gemm(2048, 4096, 2048, lambda a, b: w[:, 4096 + a:4096 + b], lambda a, b: hT[:, a:b], 'R', epi)
        if has_ctx:
            kb.new_phase()
            hcT = kb.D("hcT")
            UC = kb.D("UC", [256, 4096], BF16)
            SZCT = kb.D("SZCT", [4096, 256], BF16)
            epi = kb.epi_act(lambda m0, ms, n0, ns: UC[m0:m0 + ms, n0:n0 + ns])
            kb.gemm(2048, 256, 4096, lambda a, b: hcT[:, a:b], lambda a, b: w[:, a:b], 'L', epi)
            kb.new_phase()
            epi = kb.epi_act(lambda m0, ms, n0, ns: SZCT[m0:m0 + ms, n0:n0 + ns], func=AF.Silu)
            kb.gemm(2048, 4096, 256, lambda a, b: w[:, 4096 + a:4096 + b], lambda a, b: hcT[:, a:b], 'R', epi)
    return f


def dft_consts():
    n2 = np.arange(128)
    ang = 2 * np.pi * np.outer(n2, n2) / 128.0
    sA = 1.0 / math.sqrt(128.0)
    CSa = np.concatenate([np.cos(ang), -np.sin(ang)], 1) * sA
    c = np.arange(256)
    angc = 2 * np.pi * np.outer(c, c) / 256.0
    Cc = np.cos(angc) / 16.0
    Sc = np.sin(angc) / 16.0
    CS256 = np.concatenate([np.cos(angc), -np.sin(angc)], 1) / 16.0
    n1 = np.arange(64)
    k1 = np.arange(64)
    TW = np.zeros((64, 256, 128), np.float32)
    for j in range(64):
        for e in range(2):
            k2 = 2 * j + e
            th = 2 * np.pi * (n1[:, None] * k2 / 8192.0 + np.outer(n1, k1) / 64.0)
            TW[j, e * 64:(e + 1) * 64, e * 64:(e + 1) * 64] = np.cos(th) / 8.0
            TW[j, 128 + e * 64:128 + (e + 1) * 64, e * 64:(e + 1) * 64] = np.sin(th) / 8.0
    bf = lambda a: np.ascontiguousarray(a.astype(np.float32)).astype(NPBF)
    return dict(CSa=bf(CSa), Cc=bf(Cc), Sc=bf(Sc), Scn=bf(-Sc), CS256=bf(CS256), TW=bf(TW))


def build_B(ngroups, has_ctx):
    def f(kb):
        UQ = IN(kb, "UQ")
        CSa = IN(kb, "CSa")
        A1 = kb.D("A1", [256, 65536], BF16)
        epi = kb.epi_act(lambda m0, ms, n0, ns: A1[m0:m0 + ms, n0:n0 + ns])
        kb.gemm(128, 256, 65536, lambda a, b: CSa[:, a:b], lambda a, b: UQ[:, a:b], 'L', epi)
        kb.new_phase()
        WmT = IN(kb, "WmT")
        MM = kb.D("MM", [3, 256, ngroups * 256], BF16)
        for t, nm in enumerate(("Cc", "Sc", "Scn")):
            Cm = IN(kb, nm)
            epi = kb.epi_act(lambda m0, ms, n0, ns, t=t: MM[t, m0:m0 + ms, n0:n0 + ns], nst=3)
            kb.gemm(256, 256, ngroups * 256, lambda a, b, Cm=Cm: Cm[:, a:b], lambda a, b: WmT[:, a:b], 'L', epi)
            kb.new_phase()
        if has_ctx:
            UC = IN(kb, "UC")
            CS256 = IN(kb, "CS256")
            AC = kb.D("AC", [4096, 512], BF16)
            epi = kb.epi_act(lambda m0, ms, n0, ns: AC[m0:m0 + ms, n0:n0 + ns])
            kb.gemm(256, 4096, 512, lambda a, b: UC[:, a:b], lambda a, b: CS256[:, a:b], 'R', epi)
    return f


def build_C(has_ctx):
    def f(kb):
        LB = IN(kb, "LB")
        RB = IN(kb, "RB")
        B1 = kb.D("B1", [4, 512, 8192], BF16)
        for g in range(4):
            epi = kb.epi_act(lambda m0, ms, n0, ns, g=g: B1[g, m0:m0 + ms, n0:n0 + ns])
            kb.gemm(512, 512, 8192, lambda a, b, g=g: RB[g, :, a:b], lambda a, b, g=g: LB[g, :, a:b], 'L', epi)
            kb.new_phase()
        if has_ctx:
            LCc = IN(kb, "LCc")
            RCc = IN(kb, "RCc")
            FCT = kb.D("FCT", [4096, 256], BF16)
            for g in range(16):
                epi = kb.epi_act(lambda m0, ms, n0, ns, g=g: FCT[g * 256 + m0:g * 256 + m0 + ms, n0:n0 + ns], nst=2)
                kb.gemm(512, 256, 256, lambda a, b, g=g: LCc[g, :, a:b], lambda a, b, g=g: RCc[g, :, a:b], 'R', epi)
                kb.new_phase()
    return f


def build_Dd(kb):
    P = kb.P
    LC = IN(kb, "LC")
    TW = IN(kb, "TW")
    FQ = kb.D("FQ", [64, 128, 1024], BF16)
    tw = kb.buf([64, 2, 128], BF16)
    twv = TW.rearrange("j (c p) n -> p j c n", p=128)
    tw_toks = []
    for j0 in range(0, 64, 16):
        tw_toks.append(P.dma(POOL, tw.sem, tw.ap[:, j0:j0 + 16], twv[:, j0:j0 + 16]))
    lc = [kb.buf([2, 1024], BF16) for _ in range(4)]
    ob = [kb.buf([1024], BF16) for _ in range(4)]
    n = 0
    for j in range(64):
        b = lc[j % 4]
        o = ob[j % 4]
        b.load(LC[j].rearrange("(c p) n -> p c n", p=128), eng=(SP if j % 2 else POOL))
        etoks = []
        tok = None
        for nt in range(2):
            bi = kb.bank()
            ps = kb.banks[bi][:, 0:512]
            for c in range(2):
                tok = P.op(PE, lambda e, ps=ps, j=j, c=c, b=b, nt=nt: e.matmul(ps, tw.ap[:, j, c, :], b.ap[:, c, nt * 512:(nt + 1) * 512],
                                                                               start=(c == 0), stop=(c == 1)),
                           deps=([kb.bank_free[bi], b.ready] + tw_toks) if c == 0 else [], signal=(c == 1))
            dst = o.ap[:, nt * 512:(nt + 1) * 512]
            deps = [tok] + list(o.readers)
            if n % 2 == 0:
                t = P.op(ACT, lambda e, dst=dst, ps=ps: e.activation(out=dst, in_=ps, func=AF.Copy), deps=deps)
            else:
                t = P.op(DVE, lambda e, dst=dst, ps=ps: e.tensor_copy(out=dst, in_=ps), deps=deps)
            n += 1
            etoks.append(t)
            kb.bank_free[bi] = t
        b.read(tok)
        o.readers = []
        t_st = P.dma(SP, o.sem, FQ[j], o.ap, deps=etoks)
        o.readers.append(t_st)


def build_E(has_ctx, final):
    def f(kb):
        FT = IN(kb, "FT")
        SZT = IN(kb, "SZT")
        YT = kb.D("YT", [4096, 2048], BF16)
        kb.ew_mul(FT, SZT, YT, 4096, 2048)
        kb.new_phase()
        modD = IN(kb, "modD")
        A, sh, gt = kb.mod_cols(modD, IN(kb, "g16"), 0)
        xo = kb.D("xoT", [2048, 2048], F32)
        kb.outproj(IN(kb, "w_out"), YT, IN(kb, "xT"), xo, gt, 2048)
        if has_ctx:
            kb.new_phase()
            YCT = kb.D("YCT", [4096, 256], BF16)
            kb.ew_mul(IN(kb, "FCT"), IN(kb, "SZCT"), YCT, 4096, 256)
            kb.new_phase()
            A, sh, gtc = kb.mod_cols(modD, IN(kb, "g16"), 1)
            xco = kb.D("xcoT", [2048, 256], F32)
            kb.outproj(IN(kb, "w_out"), YCT, IN(kb, "xcT"), xco, gtc, 256)
        if final:
            kb.new_phase()
            fg = kb.load_const(IN(kb, "fg16"), [16], F32)
            outT = kb.D("outT", [2048, 2048], F32)
            kb.norm(xo, 2048, fg, None, outF=outT)
    return f


def T(a):
    return np.ascontiguousarray(np.asarray(a).T)


_CONSTS = {}


def consts():
    if not _CONSTS:
        _CONSTS.update(dft_consts())
    return _CONSTS


def run_mods(c, c_ctx, ada_w, ada_b):
    in_maps = []
    for core in range(8):
        b, i = core // 4, core % 4
        in_maps.append({"cc": np.ascontiguousarray(np.stack([c[b], c_ctx], 1)).astype(np.float32),
                        "ada_w": np.ascontiguousarray(ada_w[i]), "ada_b48": col48(ada_b[i])})
    res = run_launch(build_MOD, in_maps, ["modD"])
    return [[np.asarray(res[b * 4 + i]["modD"]) for b in range(2)] for i in range(4)]


def fnet_layer(xT, xcT, mods_i, g16, w_in, w_mix, w_out_i, has_ctx, final_g16=None):
    C = consts()
    in_maps = []
    for core in range(8):
        b = core // 4
        m = {"modD": mods_i[b], "g16": g16, "xT": xT[core], "w_in": w_in}
        if has_ctx:
            m["xcT"] = xcT[b]
        in_maps.append(m)
    outs = ["U", "SZT"] + (["UC", "SZCT"] if has_ctx else [])
    rA = run_launch(build_A(has_ctx), in_maps, outs)
    in_maps = []
    for core in range(8):
        b, q = core // 4, core % 4
        Ufull = np.concatenate([np.asarray(rA[b * 4 + s]["U"]) for s in range(4)], 0)
        UQ = np.ascontiguousarray(Ufull[:, q * 1024:(q + 1) * 1024]).reshape(128, 65536)
        m = {"UQ": UQ, "CSa": C["CSa"], "Cc": C["Cc"], "Sc": C["Sc"], "Scn": C["Scn"]}
        if has_ctx:
            m["WmT"] = np.ascontiguousarray(w_mix.transpose(1, 0, 2).reshape(256, 16 * 256))
            m["UC"] = np.asarray(rA[core]["UC"])
            m["CS256"] = C["CS256"]
        else:
            m["WmT"] = np.ascontiguousarray(w_mix[q * 4:(q + 1) * 4].transpose(1, 0, 2).reshape(256, 4 * 256))
        in_maps.append(m)
    ng = 16 if has_ctx else 4
    rB = run_launch(build_B(ng, has_ctx), in_maps, ["A1", "MM"] + (["AC"] if has_ctx else []))
    in_maps = []
    for core in range(8):
        q = core % 4
        A1 = np.asarray(rB[core]["A1"]).reshape(2, 128, 64, 4, 256)
        LB = np.ascontiguousarray(A1.transpose(3, 0, 4, 1, 2)).reshape(4, 512, 8192)
        MM = np.ascontiguousarray(np.asarray(rB[core]["MM"]).reshape(3, 256, ng, 256).transpose(0, 2, 1, 3))
        g0 = q * 4 if has_ctx else 0
        RB = np.zeros((4, 512, 512), NPBF)
        for gl in range(4):
            M1, M2, M2n = MM[0, g0 + gl], MM[1, g0 + gl], MM[2, g0 + gl]
            RB[gl, 0:256, 0:256] = M1
            RB[gl, 0:256, 256:512] = M2n
            RB[gl, 256:512, 0:256] = M2
            RB[gl, 256:512, 256:512] = M1
        m = {"LB": LB, "RB": RB}
        if has_ctx:
            AC = np.asarray(rB[core]["AC"]).reshape(16, 256, 2, 256)
            m["RCc"] = np.ascontiguousarray(AC.transpose(0, 2, 1, 3)).reshape(16, 512, 256)
            m["LCc"] = np.ascontiguousarray(np.concatenate([MM[0], MM[1]], 1))
        in_maps.append(m)
    rC = run_launch(build_C(has_ctx), in_maps, ["B1"] + (["FCT"] if has_ctx else []))
    in_maps = []
    for core in range(8):
        B1 = np.asarray(rC[core]["B1"]).reshape(4, 2, 256, 64, 2, 64)
        LC = np.ascontiguousarray(B1.transpose(3, 1, 4, 5, 0, 2)).reshape(64, 256, 1024)
        in_maps.append({"LC": LC, "TW": C["TW"]})
    rD = run_launch(build_Dd, in_maps, ["FQ"])
    Fq = []
    for core in range(8):
        FQ = np.asarray(rD[core]["FQ"]).reshape(64, 2, 64, 1024)
        Fq.append(np.ascontiguousarray(FQ.transpose(2, 0, 1, 3)).reshape(8192, 1024))
    in_maps = []
    for core in range(8):
        b, q = core // 4, core % 4
        FT = np.ascontiguousarray(np.concatenate([Fq[b * 4 + s][q * 2048:(q + 1) * 2048] for s in range(4)], 1).T)
        m = {"FT": FT, "SZT": np.asarray(rA[core]["SZT"]), "modD": mods_i[b], "g16": g16, "w_out": w_out_i,
             "xT": xT[core]}
        if has_ctx:
            m.update({"FCT": np.asarray(rC[core]["FCT"]), "SZCT": np.asarray(rA[core]["SZCT"]), "xcT": xcT[b]})
        if final_g16 is not None:
            m["fg16"] = final_g16
        in_maps.append(m)
    outs = ["xoT"] + (["xcoT"] if has_ctx else []) + (["outT"] if final_g16 is not None else [])
    rE = run_launch(build_E(has_ctx, final_g16 is not None), in_maps, outs)
    if final_g16 is not None:
        return [np.asarray(rE[k]["outT"]) for k in range(8)]
    new_x = [np.asarray(rE[k]["xoT"]) for k in range(8)]
    new_xc = [np.asarray(rE[b * 4]["xcoT"]) for b in range(2)] if has_ctx else xcT
    return new_x, new_xc


def build_F1(kb):
    front(kb, 2304, True)
    hT = kb.D("hT")
    hcT = kb.D("hcT")
    w = IN(kb, "w_in")
    QT0 = kb.D("QT0", [4096, 2048], BF16)
    KT0 = kb.D("KT0", [512, 2304], BF16)
    V = kb.D("V", [2304, 512], BF16)
    SZT = kb.D("SZT", [4096, 2048], BF16)
    KCT = kb.D("KCT", [512, 256], BF16)
    VC = kb.D("VC", [256, 512], BF16)
    epi = kb.epi_act(lambda m0, ms, n0, ns: QT0[m0:m0 + ms, n0:n0 + ns])
    kb.gemm(2048, 4096, 2048, lambda a, b: w[:, a:b], lambda a, b: hT[:, 128 + a:128 + b], 'R', epi)
    kb.new_phase()
    epi = kb.epi_act(lambda m0, ms, n0, ns: SZT[m0:m0 + ms, n0:n0 + ns], func=AF.Silu)
    kb.gemm(2048, 4096, 2048, lambda a, b: w[:, 5120 + a:5120 + b], lambda a, b: hT[:, 128 + a:128 + b], 'R', epi)
    kb.new_phase()
    epi = kb.epi_act(lambda m0, ms, n0, ns: KT0[m0:m0 + ms, n0:n0 + ns])
    kb.gemm(2048, 512, 2304, lambda a, b: w[:, 4096 + a:4096 + b], lambda a, b: hT[:, a:b], 'R', epi)
    kb.new_phase()
    epi = kb.epi_act(lambda m0, ms, n0, ns: V[m0:m0 + ms, n0:n0 + ns])
    kb.gemm(2048, 2304, 512, lambda a, b: hT[:, a:b], lambda a, b: w[:, 4608 + a:4608 + b], 'L', epi)
    kb.new_phase()
    epi = kb.epi_act(lambda m0, ms, n0, ns: KCT[m0:m0 + ms, n0:n0 + ns])
    kb.gemm(2048, 512, 256, lambda a, b: w[:, 4096 + a:4096 + b], lambda a, b: hcT[:, a:b], 'R', epi)
    kb.new_phase()
    epi = kb.epi_act(lambda m0, ms, n0, ns: VC[m0:m0 + ms, n0:n0 + ns])
    kb.gemm(2048, 256, 512, lambda a, b: hcT[:, a:b], lambda a, b: w[:, 4608 + a:4608 + b], 'L', epi)


def rope_tables(pos):
    rows = (pos // 64).astype(np.float64)
    cols = (pos % 64).astype(np.float64)
    inv = 10000.0 ** (-np.arange(16) / 16.0)
    cosT = np.zeros((64, len(pos)), np.float32)
    ssin = np.zeros((64, len(pos)), np.float32)
    for d in range(64):
        half, wv = d // 32, d % 32
        f, part = wv % 16, wv // 16
        ang = (rows if half == 0 else cols) * inv[f]
        cosT[d] = np.cos(ang)
        ssin[d] = np.sin(ang) * (-1.0 if part == 0 else 1.0)
    return np.ascontiguousarray(np.tile(cosT, (2, 1))), np.ascontiguousarray(np.tile(ssin, (2, 1)))


def rope_perm_rows(nheads):
    idx = np.arange(nheads * 64).reshape(nheads, 64)
    d = np.arange(64)
    wv = d % 32
    partner = np.where(wv // 16 == 0, d + 16, d - 16)
    return idx[:, partner].reshape(-1)


def build_G1a(kb):
    P = kb.P
    for (nm, rows, Tn) in (("Q", 4096, 2048), ("K", 512, 2304)):
        a = IN(kb, nm + "T0")
        bp = IN(kb, nm + "T0p")
        cosD = IN(kb, "cos" + nm)
        sinD = IN(kb, "ssin" + nm)
        out = kb.D(nm + "Tr", [rows, Tn], BF16)
        cs = kb.load_const(cosD, [Tn], F32)
        sn = kb.load_const(sinD, [Tn], F32)
        ab = [kb.buf([Tn], BF16) for _ in range(2)]
        bb = [kb.buf([Tn], BF16) for _ in range(2)]
        t1 = [kb.buf([Tn], F32) for _ in range(2)]
        t2 = [kb.buf([Tn], F32) for _ in range(2)]
        ob = [kb.buf([Tn], BF16) for _ in range(2)]
        for ch in range(rows // 128):
            i = ch % 2
            A_, B_, T1, T2, O_ = ab[i], bb[i], t1[i], t2[i], ob[i]
            A_.load(a[ch * 128:(ch + 1) * 128, :])
            B_.load(bp[ch * 128:(ch + 1) * 128, :], eng=SP)
            ta = P.op(DVE, lambda e, A_=A_, T1=T1, cs=cs: e.tensor_tensor(T1.ap, A_.ap, cs.ap, ALU.mult),
                      deps=[A_.ready, cs.ready] + T1.readers)
            T1.wrote(ta)
            A_.read(ta)
            tb = P.op(POOL, lambda e, B_=B_, T2=T2, sn=sn: e.tensor_tensor(T2.ap, B_.ap, sn.ap, ALU.mult),
                      deps=[B_.ready, sn.ready] + T2.readers)
            T2.wrote(tb)
            B_.read(tb)
            tc = P.op(DVE, lambda e, T1=T1, T2=T2, O_=O_: e.tensor_tensor(O_.ap, T1.ap, T2.ap, ALU.add),
                      deps=[ta, tb] + O_.readers)
            T1.read(tc)
            T2.read(tc)
            O_.wrote(tc)
            O_.store(out[ch * 128:(ch + 1) * 128, :])
        kb.new_phase()


def attn_core(kb, QTr, KTr, V, KCT, VC, SZT, YT, maskD, sinkD):
    P = kb.P
    Qv = QTr.rearrange("(h g d) t -> h d g t", g=8, d=64)
    Sv = SZT.rearrange("(h g d) t -> h d g t", g=8, d=64)
    Yv = YT.rearrange("(h g d) t -> h d g t", g=8, d=64)
    masks = kb.load_const(maskD.rearrange("k p n -> p k n"), [4, 512], BF16)
    srow = kb.buf([8192], F32)
    srow.load(sinkD, dst=srow.ap[0:1])
    esrow = kb.buf([8192], BF16)
    t = P.op(ACT, lambda e: e.activation(out=esrow.ap[0:1], in_=srow.ap[0:1], func=AF.Exp), deps=[srow.ready])
    esrow.wrote(t)
    ones = kb.buf([64], BF16)
    t = P.op(DVE, lambda e: e.memset(ones.ap, 1.0))
    ones.wrote(t)
    Qh = [kb.buf([8, 2048], BF16) for _ in range(2)]
    Kh = [kb.buf([2560], BF16) for _ in range(2)]
    Vh = [kb.buf([20, 64], BF16) for _ in range(2)]
    Pt = [kb.buf([5, 512], BF16) for _ in range(3)]
    rD = [kb.buf([512], F32) for _ in range(2)]
    yt = [kb.buf([512], F32) for _ in range(2)]
    szb = [kb.buf([8, 128], BF16) for _ in range(3)]
    yb = [kb.buf([8, 128], BF16) for _ in range(3)]
    kv = {}

    def load_hk(hk):
        Q_, K_, V_ = Qh[hk % 2], Kh[hk % 2], Vh[hk % 2]
        Q_.load(Qv[hk], dst=Q_.ap[0:64])
        K_.load(KTr[hk * 64:(hk + 1) * 64, :], dst=K_.ap[0:64, 0:2304], eng=SP)
        tk2 = K_.ready
        K_.readers = []
        t2 = P.dma(SP, K_.sem, K_.ap[0:64, 2304:2560], KCT[hk * 64:(hk + 1) * 64, :])
        K_.ready = t2
        V_.load(V[:, hk * 64:(hk + 1) * 64].rearrange("(b p) d -> p b d", p=128), dst=V_.ap[:, 0:18, :], eng=SP)
        tv1 = V_.ready
        V_.readers = []
        tv2 = P.dma(SP, V_.sem, V_.ap[:, 18:20, :], VC[:, hk * 64:(hk + 1) * 64].rearrange("(b p) d -> p b d", p=128))
        V_.ready = tv2
        kv[hk] = ([tk2, t2], [tv1, tv2])

    steps = [(hk, b, half) for hk in range(8) for b in range(16) for half in range(2)]
    st = {}

    def emit_S(i):
        hk, b, half = steps[i]
        if b == 0 and half == 0:
            if hk == 0:
                load_hk(0)
            if hk + 1 < 8:
                load_hk(hk + 1)
        Q_, K_ = Qh[hk % 2], Kh[hk % 2]
        kdeps = kv[hk][0]
        P_ = Pt[i % 3]
        rhs = Q_.ap[0:64, 4 * half:4 * half + 4, b * 128:(b + 1) * 128]
        blks = [b, b + 1, b + 2, 18, 19]
        pt_toks = []
        tm = None
        for j, blk in enumerate(blks):
            bi = kb.bank()
            ps = kb.banks[bi][:, 0:512]
            kcols = K_.ap[0:64, blk * 128:(blk + 1) * 128]
            tm = P.op(PE, lambda e, ps=ps, kcols=kcols, rhs=rhs: e.matmul(ps, kcols, rhs, start=True, stop=True),
                      deps=[kb.bank_free[bi], Q_.ready] + kdeps)
            te = P.op(ACT, lambda e, ps=ps, P_=P_, j=j: e.activation(out=P_.ap[:, j, :], in_=ps, func=AF.Exp, scale=0.125),
                      deps=[tm] + (P_.readers if j == 0 else []))
            kb.bank_free[bi] = te
            if j in (0, 2):
                mi = (0 if j == 0 else 1)
                if j == 0 and b == 0:
                    mi = 2
                if j == 2 and b == 15:
                    mi = 3
                te = P.op(DVE, lambda e, P_=P_, j=j, mi=mi: e.tensor_tensor(P_.ap[:, j, :], P_.ap[:, j, :], masks.ap[:, mi, :], ALU.mult),
                          deps=[te, masks.ready])
            pt_toks.append(te)
        Q_.read(tm)
        K_.read(tm)
        P_.wrote(pt_toks[-1])
        st[i] = pt_toks

    def emit_PV(i):
        hk, b, half = steps[i]
        V_ = Vh[hk % 2]
        vdeps = kv[hk][1]
        P_ = Pt[i % 3]
        pt_toks = st.pop(i)
        blks = [b, b + 1, b + 2, 18, 19]
        it = (hk * 16 + b) % 3
        SZ_, Y_ = szb[it], yb[it]
        if half == 0:
            SZ_.load(Sv[hk][:, :, b * 128:(b + 1) * 128], dst=SZ_.ap[0:64], eng=SP)
        bo, bd = kb.bank(), kb.bank()
        pso = kb.banks[bo][0:64, 0:512]
        psd = kb.banks[bd][0:64, 0:512]
        to = None
        for j, blk in enumerate(blks):
            to = P.op(PE, lambda e, pso=pso, V_=V_, blk=blk, P_=P_, j=j: e.matmul(pso, V_.ap[:, blk, :], P_.ap[:, j, :], start=(j == 0), stop=(j == 4)),
                      deps=(pt_toks + vdeps + [kb.bank_free[bo]]) if j == 0 else [], signal=(j == 4))
        for j in range(5):
            P.op(PE, lambda e, psd=psd, P_=P_, j=j: e.matmul(psd, ones.ap, P_.ap[:, j, :], start=(j == 0), stop=False),
                 deps=[kb.bank_free[bd], ones.ready] if j == 0 else [], signal=False)
        h0 = hk * 8 + 4 * half
        td = P.op(PE, lambda e, psd=psd, h0=h0: e.matmul(psd, ones.ap[0:1, :], esrow.ap[0:1, h0 * 128:h0 * 128 + 512], start=False, stop=True),
                  deps=[esrow.ready])
        P_.read(td)
        V_.read(td)
        R_, T_ = rD[half], yt[half]
        tr = P.op(DVE, lambda e, psd=psd, R_=R_: e.reciprocal(R_.ap[0:64], psd), deps=[td] + R_.readers)
        R_.wrote(tr)
        kb.bank_free[bd] = tr
        ty = P.op(DVE, lambda e, pso=pso, R_=R_, T_=T_: e.tensor_tensor(T_.ap[0:64], pso, R_.ap[0:64], ALU.mult),
                  deps=[to, tr] + T_.readers)
        T_.wrote(ty)
        R_.read(ty)
        kb.bank_free[bo] = ty
        tz = P.op(POOL, lambda e, T_=T_, Y_=Y_, SZ_=SZ_, half=half: e.tensor_tensor(
            Y_.ap[0:64, 4 * half:4 * half + 4, :], T_.ap[0:64].rearrange("p (g q) -> p g q", g=4),
            SZ_.ap[0:64, 4 * half:4 * half + 4, :], ALU.mult),
            deps=[ty, SZ_.ready] + (Y_.readers if half == 0 else []))
        T_.read(tz)
        if half == 1:
            SZ_.read(tz)
            Y_.wrote(tz)
            Y_.store(Yv[hk][:, :, b * 128:(b + 1) * 128], src=Y_.ap[0:64])

    n = len(steps)
    emit_S(0)
    for i in range(n):
        if i + 1 < n:
            emit_S(i + 1)
        emit_PV(i)


def build_G1b(kb):
    build_G1a(kb)
    YT = kb.D("YT", [4096, 2048], BF16)
    attn_core(kb, kb.D("QTr"), kb.D("KTr"), IN(kb, "V"), IN(kb, "KCT"), IN(kb, "VC"), IN(kb, "SZT"), YT,
              IN(kb, "maskx"), IN(kb, "sinkrep"))
    kb.new_phase()
    A, sh, gt = kb.mod_cols(IN(kb, "modD"), IN(kb, "g16"), 0)
    xo = kb.D("xoT", [2048, 2048], F32)
    kb.outproj(IN(kb, "w_out"), YT, IN(kb, "xT"), xo, gt, 2048)


def attn_layer(xT, xcT, mods_i, g16, w_in, sink, w_out_i):
    in_maps = []
    for core in range(8):
        b, q = core // 4, core % 4
        left = xT[core - 1][:, -128:] if q > 0 else np.zeros((2048, 128), np.float32)
        right = xT[core + 1][:, :128] if q < 3 else np.zeros((2048, 128), np.float32)
        xh = np.ascontiguousarray(np.concatenate([left, xT[core], right], 1))
        in_maps.append({"modD": mods_i[b], "g16": g16, "xT": xh, "xcT": xcT[b], "w_in": w_in})
    rF = run_launch(build_F1, in_maps, ["QT0", "KT0", "V", "SZT", "KCT", "VC"])
    pq, pk = rope_perm_rows(64), rope_perm_rows(8)
    in_maps = []
    for core in range(8):
        q = core % 4
        cq, sq = rope_tables(q * 2048 + np.arange(2048))
        ck, sk = rope_tables(np.abs(q * 2048 - 128 + np.arange(2304)))
        QT0 = np.asarray(rF[core]["QT0"])
        KT0 = np.asarray(rF[core]["KT0"])
        in_maps.append({"QT0": QT0, "QT0p": np.ascontiguousarray(QT0[pq]), "KT0": KT0, "KT0p": np.ascontiguousarray(KT0[pk]),
                        "cosQ": cq, "ssinQ": sq, "cosK": ck, "ssinK": sk})
    rope_maps = in_maps
    kj = np.arange(128)[:, None]
    qi = np.arange(128)[None, :]
    mprev = np.tile((kj >= qi).astype(np.float32), (1, 4))
    mnext = np.tile((kj <= qi).astype(np.float32), (1, 4))
    zero = np.zeros_like(mprev)
    in_maps = []
    for core in range(8):
        b, q = core // 4, core % 4
        maskx = np.stack([mprev, mnext, zero if q == 0 else mprev, zero if q == 3 else mnext], 0).astype(NPBF)
        in_maps.append({**rope_maps[core], "V": np.asarray(rF[core]["V"]), "KCT": np.asarray(rF[core]["KCT"]), "VC": np.asarray(rF[core]["VC"]),
                        "SZT": np.asarray(rF[core]["SZT"]), "maskx": maskx,
                        "sinkrep": np.ascontiguousarray(np.repeat(sink.astype(np.float32), 128)[None, :]),
                        "modD": mods_i[b], "g16": g16, "w_out": w_out_i, "xT": xT[core]})
    rH = run_launch(build_G1b, in_maps, ["xoT"])
    return [np.asarray(rH[k]["xoT"]) for k in range(8)]


AXX = mybir.AxisListType.X


def build_H2(kb):
    P = kb.P
    front(kb, 2048, False)
    hT = kb.D("hT")
    w = IN(kb, "w_in")
    U = kb.D("U", [2048, 4096], BF16)
    Vf = kb.D("Vf", [2048, 4096], F32)
    SZ = kb.D("SZ", [2048, 4096], BF16)
    e_u = kb.epi_act(lambda m0, ms, n0, ns: U[m0:m0 + ms, n0:n0 + ns], func=AF.Gelu_apprx_tanh, nst=3)
    e_v = kb.epi_act(lambda m0, ms, n0, ns: Vf[m0:m0 + ms, n0 - 4096:n0 - 4096 + ns], func=AF.Gelu_apprx_tanh, dt=F32, nst=3)
    e_z = kb.epi_act(lambda m0, ms, n0, ns: SZ[m0:m0 + ms, n0 - 8192:n0 - 8192 + ns], func=AF.Silu, nst=3)

    def epi(m0, ms, n0, ns, ps, tok):
        return (e_u if n0 < 4096 else (e_v if n0 < 8192 else e_z))(m0, ms, n0, ns, ps, tok)
    kb.gemm(2048, 2048, 12288, lambda a, b: hT[:, a:b], lambda a, b: w[:, a:b], 'L', epi)
    kb.new_phase()
    VN = kb.D("VN", [2048, 4096], BF16)
    G = kb.load_const(IN(kb, "lnG"), [4096], F32)
    Bt = kb.load_const(IN(kb, "lnB"), [4096], F32, eng=SP)
    epsb = kb.buf([1], F32)
    t = P.op(DVE, lambda e: e.memset(epsb.ap, EPS))
    epsb.wrote(t)
    vb = [kb.buf([4096], F32) for _ in range(2)]
    sq = kb.buf([4096], F32)
    vh = kb.buf([4096], F32)
    ob = [kb.buf([4096], BF16) for _ in range(2)]
    st = [kb.buf([8], F32) for _ in range(2)]
    vb[0].load(Vf[0:128, :])
    for i in range(16):
        if i + 1 < 16:
            vb[(i + 1) % 2].load(Vf[(i + 1) * 128:(i + 2) * 128, :])
        v, o, s = vb[i % 2], ob[i % 2], st[i % 2]
        t_sq = P.op(ACT, lambda e, v=v: e.activation(out=sq.ap, in_=v.ap, func=AF.Square), deps=[v.ready] + sq.readers)
        sq.wrote(t_sq)
        t1 = P.op(DVE, lambda e, v=v, s=s: e.reduce_sum(out=s.ap[:, 0:1], in_=v.ap, axis=AXX), deps=[v.ready] + s.readers)
        t2 = P.op(DVE, lambda e, s=s: e.reduce_sum(out=s.ap[:, 1:2], in_=sq.ap, axis=AXX), deps=[t_sq, t1])
        sq.read(t2)
        t3 = P.op(DVE, lambda e, s=s: e.tensor_scalar(s.ap[:, 2:3], s.ap[:, 0:1], 1.0 / 4096.0, None, ALU.mult), deps=[t2])
        t4 = P.op(DVE, lambda e, s=s: e.tensor_tensor(s.ap[:, 3:4], s.ap[:, 2:3], s.ap[:, 2:3], ALU.mult), deps=[t3])
        t5 = P.op(DVE, lambda e, s=s: e.scalar_tensor_tensor(out=s.ap[:, 4:5], in0=s.ap[:, 1:2], scalar=1.0 / 4096.0,
                                                             in1=s.ap[:, 3:4], op0=ALU.mult, op1=ALU.subtract), deps=[t4])
        t6 = P.op(ACT, lambda e, s=s: e.activation(out=s.ap[:, 5:6], in_=s.ap[:, 4:5], func=AF.Sqrt, bias=epsb.ap), deps=[t5, epsb.ready])
        t7 = P.op(DVE, lambda e, s=s: e.reciprocal(s.ap[:, 5:6], s.ap[:, 5:6]), deps=[t6])
        t8 = P.op(DVE, lambda e, s=s: e.scalar_tensor_tensor(out=s.ap[:, 6:7], in0=s.ap[:, 2:3], scalar=-1.0,
                                                             in1=s.ap[:, 5:6], op0=ALU.mult, op1=ALU.mult), deps=[t7])
        t9 = P.op(ACT, lambda e, v=v, s=s: e.activation(out=vh.ap, in_=v.ap, func=AF.Identity, scale=s.ap[:, 5:6], bias=s.ap[:, 6:7]),
                  deps=[t8] + vh.readers)
        vh.wrote(t9)
        v.read(t9)
        v.read(t2)
        t10 = P.op(DVE, lambda e: e.tensor_tensor(vh.ap, vh.ap, G.ap, ALU.mult), deps=[t9, G.ready])
        t11 = P.op(DVE, lambda e, o=o: e.tensor_tensor(o.ap, vh.ap, Bt.ap, ALU.add), deps=[t10, Bt.ready] + o.readers)
        vh.read(t11)
        s.read(t11)
        o.wrote(t11)
        o.store(VN[i * 128:(i + 1) * 128, :])
    kb.new_phase()
    Y = kb.D("Y", [2048, 4096], BF16)
    WsT = IN(kb, "WsT")
    bs = kb.load_const(IN(kb, "bsT"), [16], F32)

    def view(D_, g, kk):
        return D_.rearrange("(k t) (g c) -> g t k c", t=128, c=256)[g][:, kk:kk + 2, :]
    wt = [kb.buf([128], BF16) for _ in range(2)]
    rb = [kb.buf([2, 256], BF16) for _ in range(2)]
    ub = [kb.buf([2, 256], BF16) for _ in range(2)]
    zb = [kb.buf([2, 256], BF16) for _ in range(2)]
    sb_ = [kb.buf([512], F32) for _ in range(2)]
    yb = [kb.buf([2, 256], BF16) for _ in range(2)]
    it = 0
    for g in range(16):
        W_ = wt[g % 2]
        W_.load(WsT[g])
        for kk in range(0, 16, 2):
            i = it % 2
            it += 1
            R_, U_, Z_, S_, Y_ = rb[i], ub[i], zb[i], sb_[i], yb[i]
            R_.load(view(VN, g, kk))
            U_.load(view(U, g, kk), eng=SP)
            Z_.load(view(SZ, g, kk), eng=SP)
            bi = kb.bank()
            ps = kb.banks[bi][:, 0:512]
            tm = P.op(PE, lambda e, ps=ps, W_=W_, R_=R_: e.matmul(ps, W_.ap, R_.ap.rearrange("p k c -> p (k c)"), start=True, stop=True),
                      deps=[W_.ready, R_.ready, kb.bank_free[bi]])
            W_.read(tm)
            R_.read(tm)
            ts = P.op(ACT, lambda e, ps=ps, S_=S_, g=g: e.activation(out=S_.ap, in_=ps, func=AF.Identity, bias=bs.ap[:, g:g + 1]),
                      deps=[tm, bs.ready] + S_.readers)
            kb.bank_free[bi] = ts
            S_.wrote(ts)
            ta = P.op(DVE, lambda e, S_=S_, U_=U_: e.tensor_tensor(S_.ap, S_.ap, U_.ap.rearrange("p k c -> p (k c)"), ALU.mult),
                      deps=[ts, U_.ready])
            U_.read(ta)
            tb = P.op(DVE, lambda e, S_=S_, Z_=Z_, Y_=Y_: e.tensor_tensor(Y_.ap.rearrange("p k c -> p (k c)"), S_.ap,
                                                                         Z_.ap.rearrange("p k c -> p (k c)"), ALU.mult),
                      deps=[ta, Z_.ready] + Y_.readers)
            Z_.read(tb)
            S_.read(tb)
            Y_.wrote(tb)
            Y_.store(view(Y, g, kk))


def build_I(final):
    def f(kb):
        A, sh, gt = kb.mod_cols(IN(kb, "modD"), IN(kb, "g16"), 0)
        xo = kb.D("xoT", [2048, 2048], F32)
        kb.outproj(IN(kb, "w_out"), IN(kb, "YT"), IN(kb, "xT"), xo, gt, 2048)
    return f


def gmlp_layer(xT, mods_i, g16, w_in, w_s, b_s, ln_g, ln_b, w_out_i):
    in_maps = []
    for core in range(8):
        b = core // 4
        in_maps.append({"modD": mods_i[b], "g16": g16, "xT": xT[core], "w_in": w_in,
                        "lnG": np.ascontiguousarray(np.broadcast_to(ln_g[None, :], (128, 4096))).astype(np.float32),
                        "lnB": np.ascontiguousarray(np.broadcast_to(ln_b[None, :], (128, 4096))).astype(np.float32),
                        "WsT": np.ascontiguousarray(w_s.transpose(0, 2, 1)), "bsT": T(b_s)})
    rH = run_launch(build_H2, in_maps, ["Y"])
    in_maps = []
    for core in range(8):
        b = core // 4
        in_maps.append({"YT": T(rH[core]["Y"]), "modD": mods_i[b], "g16": g16, "w_out": w_out_i, "xT": xT[core]})
    rI = run_launch(build_I(False), in_maps, ["xoT"])
    return [np.asarray(rI[k]["xoT"]) for k in range(8)]


def kernel(x, c, ctx, c_ctx, norm_g, ada_w, ada_b, w_out, fnet_w_in, fnet_w_mix, attn_w_in, attn_sink,
           gmlp_w_in, gmlp_w_s, gmlp_b_s, gmlp_ln_g, gmlp_ln_b, final_g):
    f32 = lambda a: np.asarray(a, np.float32)
    x, c, ctx, c_ctx, norm_g, ada_w, ada_b, w_out = map(f32, (x, c, ctx, c_ctx, norm_g, ada_w, ada_b, w_out))
    fnet_w_in, fnet_w_mix, attn_w_in, attn_sink = map(f32, (fnet_w_in, fnet_w_mix, attn_w_in, attn_sink))
    gmlp_w_in, gmlp_w_s, gmlp_b_s, gmlp_ln_g, gmlp_ln_b, final_g = map(
        f32, (gmlp_w_in, gmlp_w_s, gmlp_b_s, gmlp_ln_g, gmlp_ln_b, final_g))
    mods = run_mods(c, c_ctx, ada_w, ada_b)
    xT = [T(x[k // 4, (k % 4) * 2048:(k % 4 + 1) * 2048]) for k in range(8)]
    xcT = [T(ctx[b]) for b in range(2)]
    xT, xcT = fnet_layer(xT, xcT, mods[0], col48(norm_g[0]), fnet_w_in[0], fnet_w_mix[0], w_out[0], True)
    xT = attn_layer(xT, xcT, mods[1], col48(norm_g[1]), attn_w_in[0], attn_sink[0], w_out[1])
    xT = gmlp_layer(xT, mods[2], col48(norm_g[2]), gmlp_w_in[0], gmlp_w_s[0], gmlp_b_s[0], gmlp_ln_g[0], gmlp_ln_b[0], w_out[2])
    oT = fnet_layer(xT, None, mods[3], col48(norm_g[3]), fnet_w_in[1], fnet_w_mix[1], w_out[3], False,
                    final_g16=col48(final_g))
    out = np.empty((2, 8192, 2048), np.float32)
    for k in range(8):
        out[k // 4, (k % 4) * 2048:(k % 4 + 1) * 2048] = oT[k].T
    return out
```

```python
import contextlib
import math
import numpy as np
import ml_dtypes
import concourse.bass as bass
import concourse.mybir as mybir
from concourse.bass_utils import run_bass_kernel_spmd

F32 = mybir.dt.float32
BF16 = mybir.dt.bfloat16
AF = mybir.ActivationFunctionType
ALU = mybir.AluOpType
NPBF = ml_dtypes.bfloat16

PE, ACT, DVE, POOL, SP = "pe", "act", "dve", "pool", "sp"
ENGS = (PE, ACT, DVE, POOL, SP)

D_MODEL = 2048
D_BRANCH = 4096
EPS = 1e-6
ARENA_F32 = 46 * 1024


class Prog:
    def __init__(self, nc, stack):
        self.nc = nc
        self.stack = stack
        self.q = {e: [] for e in ENGS}
        self.nsem = 0
        self.esem = {e: self.sem("e_" + e) for e in (PE, ACT, DVE, POOL)}
        self.ecnt = {e: 0 for e in (PE, ACT, DVE, POOL)}
        self.dsems = []
        self.dcnt = {}

    def sem(self, name):
        self.nsem += 1
        return self.stack.enter_context(self.nc.semaphore(f"{name}_{self.nsem}"))

    def dsem(self, name="d"):
        s = self.sem(name)
        self.dsems.append(s)
        self.dcnt[id(s)] = 0
        return s

    def op(self, eng, fn, deps=(), signal=True):
        tok = None
        inc = None
        if signal:
            self.ecnt[eng] += 1
            tok = (self.esem[eng], self.ecnt[eng])
            inc = (self.esem[eng], 1)
        self.q[eng].append((tuple(d for d in deps if d is not None), fn, inc))
        return tok

    def dma(self, eng, sem, out, in_, deps=()):
        self.dcnt[id(sem)] += 16
        tok = (sem, self.dcnt[id(sem)])
        self.q[eng].append((tuple(d for d in deps if d is not None),
                            (lambda e, out=out, in_=in_: e.dma_start(out=out, in_=in_)), (sem, 16)))
        return tok

    def coll(self, sem, kind, src, dst, groups, deps=()):
        self.dcnt[id(sem)] += 1
        tok = (sem, self.dcnt[id(sem)])
        self.q[POOL].append((tuple(d for d in deps if d is not None),
                             (lambda e: e.collective_compute(kind, ALU.bypass, replica_groups=groups,
                                                             ins=[src.opt()], outs=[dst.opt()])), (sem, 1)))
        return tok

    def wait(self, eng, deps):
        self.q[eng].append((tuple(d for d in deps if d is not None), None, None))

    def all_tokens(self):
        toks = [(self.esem[e], self.ecnt[e]) for e in self.ecnt if self.ecnt[e] > 0]
        toks += [(s, self.dcnt[id(s)]) for s in self.dsems if self.dcnt[id(s)] > 0]
        return toks

    def barrier(self):
        toks = self.all_tokens()
        for e in ENGS:
            self.wait(e, toks)

    def emit(self):
        nc = self.nc
        with nc.Block() as block:
            def replay(name):
                def run(e):
                    seen = {}
                    for deps, fn, inc in self.q[name]:
                        for (s, v) in deps:
                            k = id(s)
                            if seen.get(k, 0) < v:
                                e.wait_ge(s, v)
                                seen[k] = v
                        if fn is None:
                            continue
                        ins = fn(e)
                        if inc is not None:
                            ins.then_inc(inc[0], inc[1])
                return run
            block.tensor(replay(PE))
            block.scalar(replay(ACT))
            block.vector(replay(DVE))
            block.gpsimd(replay(POOL))
            block.sync(replay(SP))


class Buf:
    def __init__(self, kb, ap):
        self.kb = kb
        self.ap = ap
        self.sem = kb.get_dsem()
        self.ready = None
        self.readers = []

    def load(self, src, eng=POOL, deps=(), dst=None):
        P = self.kb.P
        t = P.dma(eng, self.sem, self.ap if dst is None else dst, src, deps=list(self.readers) + list(deps))
        self.ready = t
        self.readers = []
        return t

    def wrote(self, tok):
        self.ready = tok
        self.readers = []

    def read(self, tok):
        if tok is not None:
            self.readers.append(tok)
            if len(self.readers) > 24:
                self.readers = self.readers[-24:]

    def store(self, dst, deps=(), src=None, eng=SP):
        P = self.kb.P
        t = P.dma(eng, self.sem, dst, self.ap if src is None else src, deps=[self.ready] + list(deps))
        self.readers.append(t)
        return t


class KB:
    def __init__(self, nc, stack, ext_in=(), ext_out=()):
        self.nc = nc
        self.P = Prog(nc, stack)
        self.arena = stack.enter_context(nc.sbuf_tensor("arena", [128, ARENA_F32], F32))
        self.banks = [stack.enter_context(nc.psum_tensor(f"bank{i}", [128, 512], F32)) for i in range(8)]
        self.bank_free = [None] * 8
        self.bank_i = 0
        self.off = 0
        self.dram = {}
        self.ext_in = set(ext_in)
        self.ext_out = set(ext_out)
        self.out_toks = []

    def get_dsem(self):
        if not hasattr(self, "sem_pool"):
            self.sem_pool = []
            self.sem_next = 0
        if self.sem_next >= len(self.sem_pool):
            self.sem_pool.append(self.P.dsem("b"))
        s = self.sem_pool[self.sem_next]
        self.sem_next += 1
        return s

    def D(self, name, shape=None, dt=None):
        if name in self.dram:
            return self.dram[name]
        kind = "ExternalInput" if name in self.ext_in else ("ExternalOutput" if name in self.ext_out else "Internal")
        t = self.nc.dram_tensor(name, list(shape), dt, kind=kind).ap()
        self.dram[name] = t
        return t

    def alloc(self, shape, dt):
        n = int(np.prod(shape))
        nf32 = (n * (2 if dt == BF16 else 4) + 3) // 4
        nf32 = (nf32 + 7) // 8 * 8
        assert self.off + nf32 <= ARENA_F32, (self.off, nf32, shape)
        ap = self.arena[:, self.off:self.off + nf32]
        self.off += nf32
        if dt == BF16:
            ap = ap.bitcast(BF16)
        ap = ap[:, 0:n]
        if len(shape) == 2:
            ap = ap.rearrange("p (a b) -> p a b", a=shape[0])
        elif len(shape) == 3:
            ap = ap.rearrange("p (a b c) -> p a b c", a=shape[0], b=shape[1])
        elif len(shape) == 4:
            ap = ap.rearrange("p (a b c d) -> p a b c d", a=shape[0], b=shape[1], c=shape[2])
        return ap

    def buf(self, shape, dt):
        return Buf(self, self.alloc(shape, dt))

    def new_phase(self):
        self.P.barrier()
        self.off = 0
        self.sem_next = 0
        self.bank_free = [None] * 8

    def bank(self):
        i = self.bank_i
        self.bank_i = (i + 1) % 8
        return i

    def gemm(self, K, M, N, Lsrc, Rsrc, resident, epi, l_dt=BF16, r_dt=BF16, sblk=512, kp=128, deps=()):
        P = self.P
        KC = K // kp
        assert K % kp == 0

        def view(src):
            return src.rearrange("(c p) x -> p c x", p=kp)

        RESX = M if resident == 'L' else N
        STRX = N if resident == 'L' else M
        res = self.buf([KC, RESX], BF16)
        res_ap = res.ap if kp == 128 else res.ap[0:kp]
        rsrc = Lsrc if resident == 'L' else Rsrc
        ssrc = Rsrc if resident == 'L' else Lsrc
        nres_toks = []
        for x0 in range(0, RESX, 1024):
            x1 = min(RESX, x0 + 1024)
            nres_toks.append(res.load(view(rsrc(x0, x1)), deps=deps, dst=res_ap[:, :, x0:x1]))
            res.readers = []
        sblk = min(sblk, STRX)
        nblk = (STRX + sblk - 1) // sblk
        sb = [self.buf([KC, sblk], BF16) for _ in range(min(2, nblk))]

        def issue(s):
            b = sb[s % len(sb)]
            x0 = s * sblk
            x1 = min(STRX, x0 + sblk)
            dst = (b.ap if kp == 128 else b.ap[0:kp])[:, :, 0:x1 - x0]
            b.load(view(ssrc(x0, x1)), deps=deps, dst=dst)

        issue(0)
        for s in range(nblk):
            if s + 1 < nblk:
                issue(s + 1)
            b = sb[s % len(sb)]
            bap = b.ap if kp == 128 else b.ap[0:kp]
            x0 = s * sblk
            x1 = min(STRX, x0 + sblk)
            if resident == 'L':
                tiles = [(m0, min(M, m0 + 128), n0, min(x1, n0 + 512))
                         for m0 in range(0, M, 128) for n0 in range(x0, x1, 512)]
            else:
                tiles = [(m0, min(x1, m0 + 128), n0, min(N, n0 + 512))
                         for m0 in range(x0, x1, 128) for n0 in range(0, N, 512)]
            for (m0, m1, n0, n1) in tiles:
                bi = self.bank()
                ps = self.banks[bi][0:m1 - m0, 0:n1 - n0]
                tok = None
                for c in range(KC):
                    if resident == 'L':
                        lt = res_ap[:, c, m0:m1]
                        rt = bap[:, c, n0 - x0:n1 - x0]
                    else:
                        lt = bap[:, c, m0 - x0:m1 - x0]
                        rt = res_ap[:, c, n0:n1]
                    d = [self.bank_free[bi], b.ready] + nres_toks if c == 0 else []
                    last = (c == KC - 1)
                    tok = P.op(PE, (lambda e, ps=ps, lt=lt, rt=rt, c=c, last=last:
                                    e.matmul(ps, lt, rt, start=(c == 0), stop=last)),
                               deps=d, signal=last)
                b.read(tok)
                res.read(tok)
                self.bank_free[bi] = epi(m0, m1 - m0, n0, n1 - n0, ps, tok)

    def stagers(self, n, shape, dt):
        return [self.buf(shape, dt) for _ in range(n)]

    def epi_act(self, dst_fn, func=AF.Copy, dt=BF16, nst=4, alt=True, bias_fn=None):
        P = self.P
        st = self.stagers(nst, [512], dt)
        cnt = [0]

        def epi(m0, msz, n0, nsz, ps, tok):
            b = st[cnt[0] % nst]
            use_dve = alt and func == AF.Copy and bias_fn is None and (cnt[0] % 2 == 1)
            cnt[0] += 1
            o = b.ap[0:msz, 0:nsz]
            deps = [tok] + list(b.readers)
            if use_dve:
                t = P.op(DVE, lambda e: e.tensor_copy(out=o, in_=ps), deps=deps)
            elif bias_fn is not None:
                bcol = bias_fn(m0, msz)
                t = P.op(ACT, lambda e: e.activation(out=o, in_=ps, func=func, bias=bcol), deps=deps)
            else:
                t = P.op(ACT, lambda e: e.activation(out=o, in_=ps, func=func), deps=deps)
            b.wrote(t)
            b.store(dst_fn(m0, msz, n0, nsz), src=o)
            return t
        return epi

    def load_const(self, src, shape, dt, eng=POOL):
        b = self.buf(shape, dt)
        b.load(src, eng=eng)
        return b

    def silu_cols(self, cc, scT):
        P = self.P
        a = self.load_const(cc.rearrange("(c p) x -> p c x", p=128), [16, 2], F32)
        o = self.buf([16, 2], BF16)
        t = P.op(ACT, lambda e: e.activation(out=o.ap, in_=a.ap, func=AF.Silu), deps=[a.ready])
        o.wrote(t)
        o.store(scT.rearrange("(c p) x -> p c x", p=128))

    def mod(self, ada_w, ada_b48, scT, modD):
        bias = self.load_const(ada_b48, [48], F32)
        epi = self.epi_act(lambda m0, ms, n0, ns: modD[m0:m0 + ms, n0:n0 + ns], func=AF.Identity, dt=F32,
                           bias_fn=lambda m0, ms: bias.ap[0:ms, m0 // 128:m0 // 128 + 1])
        self.gemm(2048, 6144, 2, lambda a, b: ada_w[:, a:b], lambda a, b: scT[:, a:b], 'R', epi, deps=[bias.ready])

    def mod_cols(self, modD, g16, col):
        P = self.P
        m = self.load_const(modD.rearrange("(j p) x -> p j x", p=128), [48, 2], F32)
        g = self.load_const(g16, [16], F32)
        A = self.buf([16], F32)
        t = P.op(DVE, lambda e: e.tensor_scalar(A.ap, m.ap[:, 16:32, col], 1.0, 1.0, ALU.add, ALU.mult),
                 deps=[m.ready])
        t = P.op(DVE, lambda e: e.tensor_tensor(A.ap, A.ap, g.ap, ALU.mult), deps=[t, g.ready])
        A.wrote(t)
        sh = self.buf([16], F32)
        t2 = P.op(DVE, lambda e: e.tensor_copy(out=sh.ap, in_=m.ap[:, 0:16, col]), deps=[m.ready])
        sh.wrote(t2)
        gt = self.buf([16], F32)
        t3 = P.op(DVE, lambda e: e.tensor_copy(out=gt.ap, in_=m.ap[:, 32:48, col]), deps=[m.ready])
        gt.wrote(t3)
        return A, sh, gt

    def norm(self, xT, Tn, A, sh, hT=None, outF=None, tile=512):
        P = self.P
        ones = self.buf([128], BF16)
        t1 = P.op(DVE, lambda e: e.memset(ones.ap, 1.0))
        ones.wrote(t1)
        epsb = self.buf([1], F32)
        t1 = P.op(DVE, lambda e: e.memset(epsb.ap, EPS))
        epsb.wrote(t1)
        tile = min(tile, Tn)
        xb = [self.buf([16, tile], F32) for _ in range(2)]
        sq = self.buf([16, tile], BF16)
        rs = self.buf([tile], F32)
        tmp = [self.buf([tile], F32) for _ in range(2)]
        odt = F32 if outF is not None else BF16
        ob = [self.buf([16, tile], odt) for _ in range(1 if outF is not None else 2)]
        dst = outF if outF is not None else hT
        assert Tn % tile == 0
        nt = Tn // tile
        xv = xT.rearrange("(c p) t -> p c t", p=128)
        dv = dst.rearrange("(c p) t -> p c t", p=128)
        xb[0].load(xv[:, :, 0:tile])
        for i in range(nt):
            if i + 1 < nt:
                xb[(i + 1) % 2].load(xv[:, :, (i + 1) * tile:(i + 2) * tile])
            x = xb[i % 2]
            o = ob[i % len(ob)]
            t = P.op(ACT, lambda e, x=x: e.activation(out=sq.ap, in_=x.ap, func=AF.Square),
                     deps=[x.ready] + sq.readers)
            sq.wrote(t)
            bi = self.bank()
            ps = self.banks[bi][:, 0:tile]
            for c in range(16):
                tk = P.op(PE, lambda e, c=c, ps=ps: e.matmul(ps, ones.ap, sq.ap[:, c, :], start=(c == 0), stop=(c == 15)),
                          deps=[sq.ready, ones.ready, self.bank_free[bi]] if c == 0 else [], signal=(c == 15))
            sq.read(tk)
            t0_ = P.op(ACT, lambda e, ps=ps: e.activation(out=rs.ap, in_=ps, func=AF.Sqrt, scale=1.0 / 2048.0, bias=epsb.ap),
                       deps=[tk, epsb.ready] + rs.readers)
            t = P.op(DVE, lambda e: e.reciprocal(rs.ap, rs.ap), deps=[t0_])
            rs.wrote(t)
            self.bank_free[bi] = t
            last = []
            for c in range(16):
                tb = tmp[c % 2]
                t = P.op(DVE, lambda e, c=c, x=x, tb=tb: e.scalar_tensor_tensor(
                    out=tb.ap, in0=x.ap[:, c, :], scalar=A.ap[:, c:c + 1], in1=rs.ap, op0=ALU.mult, op1=ALU.mult),
                    deps=[rs.ready, A.ready, x.ready] + tb.readers)
                tb.wrote(t)
                if sh is not None:
                    t2 = P.op(ACT, lambda e, c=c, tb=tb, o=o: e.activation(
                        out=o.ap[:, c, :], in_=tb.ap, func=AF.Identity, bias=sh.ap[:, c:c + 1]),
                        deps=[t, sh.ready] + (o.readers if c == 0 else []))
                else:
                    t2 = P.op(ACT, lambda e, c=c, tb=tb, o=o: e.activation(out=o.ap[:, c, :], in_=tb.ap, func=AF.Copy),
                              deps=[t] + (o.readers if c == 0 else []))
                tb.read(t2)
                last.append(t2)
            x.read(last[-1])
            x.read(t)
            rs.read(t)
            o.wrote(last[-1])
            o.readers = []
            o.store(dv[:, :, i * tile:(i + 1) * tile])

    def ew_mul(self, a, b, out, rows, cols, dt_out=BF16):
        P = self.P
        R = rows // 128
        ct = min(cols, 512)
        rb = min(R, 8)
        av = a.rearrange("(c p) t -> p c t", p=128)
        bv = b.rearrange("(c p) t -> p c t", p=128)
        ov = out.rearrange("(c p) t -> p c t", p=128)
        ab = [self.buf([rb, ct], BF16) for _ in range(2)]
        bb = [self.buf([rb, ct], BF16) for _ in range(2)]
        ob = [self.buf([rb, ct], dt_out) for _ in range(2)]
        i = 0
        for r0 in range(0, R, rb):
            for c0 in range(0, cols, ct):
                A_, B_, O_ = ab[i % 2], bb[i % 2], ob[i % 2]
                A_.load(av[:, r0:r0 + rb, c0:c0 + ct])
                B_.load(bv[:, r0:r0 + rb, c0:c0 + ct], eng=SP)
                t = P.op(DVE if i % 2 == 0 else POOL, lambda e, A_=A_, B_=B_, O_=O_: e.tensor_tensor(O_.ap, A_.ap, B_.ap, ALU.mult),
                         deps=[A_.ready, B_.ready] + O_.readers)
                A_.read(t)
                B_.read(t)
                O_.wrote(t)
                O_.store(ov[:, r0:r0 + rb, c0:c0 + ct])
                i += 1

    def epi_resid(self, xT_in, xT_out, gate, nst=3):
        P = self.P
        xs = [self.buf([512], F32) for _ in range(nst)]
        os_ = [self.buf([512], F32) for _ in range(nst)]
        cnt = [0]

        def epi(m0, msz, n0, nsz, ps, tok):
            xb = xs[cnt[0] % nst]
            ob = os_[cnt[0] % nst]
            cnt[0] += 1
            xb.load(xT_in[m0:m0 + msz, n0:n0 + nsz], dst=xb.ap[0:msz, 0:nsz], eng=SP)
            j = m0 // 128
            t = P.op(DVE, lambda e: e.scalar_tensor_tensor(out=ob.ap[0:msz, 0:nsz], in0=ps, scalar=gate.ap[0:msz, j:j + 1],
                                                           in1=xb.ap[0:msz, 0:nsz], op0=ALU.mult, op1=ALU.add),
                     deps=[tok, xb.ready, gate.ready] + ob.readers)
            xb.read(t)
            ob.wrote(t)
            ob.store(xT_out[m0:m0 + msz, n0:n0 + nsz], src=ob.ap[0:msz, 0:nsz])
            return t
        return epi

    def outproj(self, w_out, yT, xT_in, xT_out, gate, Tn):
        step = min(Tn, 2048)
        for n0 in range(0, Tn, step):
            off0 = self.off
            epi = self.epi_resid(xT_in[:, n0:n0 + step], xT_out[:, n0:n0 + step], gate)
            self.gemm(4096, 2048, step, lambda a, b: w_out[:, a:b], lambda a, b, n0=n0: yT[:, n0 + a:n0 + b], 'R', epi,
                      sblk=(256 if step > 1024 else 512))
            if n0 + step < Tn:
                self.P.barrier()
                self.off = off0
                self.bank_free = [None] * 8


def run_launch(build_fn, in_maps, out_names):
    nc = bass.Bass("TRN2", target_bir_lowering=False)
    with contextlib.ExitStack() as st:
        kb = KB(nc, st, ext_in=list(in_maps[0].keys()), ext_out=out_names)
        kb.in_shapes = {k: (v.shape, v.dtype) for k, v in in_maps[0].items()}
        build_fn(kb)
        toks = kb.P.all_tokens()
        kb.P.wait(SP, toks)
        kb.P.wait(POOL, toks)
        kb.P.emit()
    res = run_bass_kernel_spmd(nc, in_maps, core_ids=list(range(8)))
    return res.results


def IN(kb, name):
    shape, dt = kb.in_shapes[name]
    return kb.D(name, list(shape), BF16 if dt == NPBF else F32)


def col48(v):
    return np.ascontiguousarray(v.reshape(-1, 128).T)


def build_MOD(kb):
    cc = IN(kb, "cc")
    scT = kb.D("scT", [2048, 2], BF16)
    kb.silu_cols(cc, scT)
    kb.new_phase()
    modD = kb.D("modD", [6144, 2], F32)
    kb.mod(IN(kb, "ada_w"), IN(kb, "ada_b48"), scT, modD)


def front(kb, Tn, has_ctx, x_name="xT", h_name="hT"):
    modD = IN(kb, "modD")
    A, sh, gt = kb.mod_cols(modD, IN(kb, "g16"), 0)
    hT = kb.D(h_name, [2048, Tn], BF16)
    kb.norm(IN(kb, x_name), Tn, A, sh, hT=hT, tile=(384 if Tn == 2304 else 512))
    if has_ctx:
        kb.new_phase()
        A, sh, gt = kb.mod_cols(modD, IN(kb, "g16"), 1)
        hcT = kb.D("hcT", [2048, 256], BF16)
        kb.norm(IN(kb, "xcT"), 256, A, sh, hT=hcT, tile=256)
    kb.new_phase()


def build_A(has_ctx):
    def f(kb):
        front(kb, 2048, has_ctx)
        hT = kb.D("hT")
        w = IN(kb, "w_in")
        U = kb.D("U", [2048, 4096], BF16)
        SZT = kb.D("SZT", [4096, 2048], BF16)
        epi = kb.epi_act(lambda m0, ms, n0, ns: U[m0:m0 + ms, n0:n0 + ns])
        kb.gemm(2048, 2048, 4096, lambda a, b: hT[:, a:b], lambda a, b: w[:, a:b], 'L', epi)
        kb.new_phase()
        epi = kb.epi_act(lambda m0, ms, n0, ns: SZT[m0:m0 + ms, n0:n0 + ns], func=AF.Silu)
        kb.gemm(2048, 4096, 2048, lambda a, b: w[:, 4096 + a:4096 + b], lambda a, b: hT[:, a:b], 'R', epi)
        if has_ctx:
            kb.new_phase()
            hcT = kb.D("hcT")
            UC = kb.D("UC", [256, 4096], BF16)
            SZCT = kb.D("SZCT", [4096, 256], BF16)
            epi = kb.epi_act(lambda m0, ms, n0, ns: UC[m0:m0 + ms, n0:n0 + ns])
            kb.gemm(2048, 256, 4096, lambda a, b: hcT[:, a:b], lambda a, b: w[:, a:b], 'L', epi)
            kb.new_phase()
            epi = kb.epi_act(lambda m0, ms, n0, ns: SZCT[m0:m0 + ms, n0:n0 + ns], func=AF.Silu)
            kb.gemm(2048, 4096, 256, lambda a, b: w[:, 4096 + a:4096 + b], lambda a, b: hcT[:, a:b], 'R', epi)
    return f


def dft_consts():
    n2 = np.arange(128)
    ang = 2 * np.pi * np.outer(n2, n2) / 128.0
    sA = 1.0 / math.sqrt(128.0)
    CSa = np.concatenate([np.cos(ang), -np.sin(ang)], 1) * sA
    c = np.arange(256)
    angc = 2 * np.pi * np.outer(c, c) / 256.0
    Cc = np.cos(angc) / 16.0
    Sc = np.sin(angc) / 16.0
    CS256 = np.concatenate([np.cos(angc), -np.sin(angc)], 1) / 16.0
    n1 = np.arange(64)
    k1 = np.arange(64)
    TW = np.zeros((64, 256, 128), np.float32)
    for j in range(64):
        for e in range(2):
            k2 = 2 * j + e
            th = 2 * np.pi * (n1[:, None] * k2 / 8192.0 + np.outer(n1, k1) / 64.0)
            TW[j, e * 64:(e + 1) * 64, e * 64:(e + 1) * 64] = np.cos(th) / 8.0
            TW[j, 128 + e * 64:128 + (e + 1) * 64, e * 64:(e + 1) * 64] = np.sin(th) / 8.0
    bf = lambda a: np.ascontiguousarray(a.astype(np.float32)).astype(NPBF)
    return dict(CSa=bf(CSa), Cc=bf(Cc), Sc=bf(Sc), Scn=bf(-Sc), CS256=bf(CS256), TW=bf(TW))


def build_B(ngroups, has_ctx):
    def f(kb):
        UQ = IN(kb, "UQ")
        CSa = IN(kb, "CSa")
        A1 = kb.D("A1", [256, 65536], BF16)
        epi = kb.epi_act(lambda m0, ms, n0, ns: A1[m0:m0 + ms, n0:n0 + ns])
        kb.gemm(128, 256, 65536, lambda a, b: CSa[:, a:b], lambda a, b: UQ[:, a:b], 'L', epi)
        kb.new_phase()
        WmT = IN(kb, "WmT")
        MM = kb.D("MM", [3, 256, ngroups * 256], BF16)
        for t, nm in enumerate(("Cc", "Sc", "Scn")):
            Cm = IN(kb, nm)
            epi = kb.epi_act(lambda m0, ms, n0, ns, t=t: MM[t, m0:m0 + ms, n0:n0 + ns], nst=3)
            kb.gemm(256, 256, ngroups * 256, lambda a, b, Cm=Cm: Cm[:, a:b], lambda a, b: WmT[:, a:b], 'L', epi)
            kb.new_phase()
        if has_ctx:
            UC = IN(kb, "UC")
            CS256 = IN(kb, "CS256")
            AC = kb.D("AC", [4096, 512], BF16)
            epi = kb.epi_act(lambda m0, ms, n0, ns: AC[m0:m0 + ms, n0:n0 + ns])
            kb.gemm(256, 4096, 512, lambda a, b: UC[:, a:b], lambda a, b: CS256[:, a:b], 'R', epi)
    return f


def build_C(has_ctx):
    def f(kb):
        LB = IN(kb, "LB")
        RB = IN(kb, "RB")
        B1 = kb.D("B1", [4, 512, 8192], BF16)
        for g in range(4):
            epi = kb.epi_act(lambda m0, ms, n0, ns, g=g: B1[g, m0:m0 + ms, n0:n0 + ns])
            kb.gemm(512, 512, 8192, lambda a, b, g=g: RB[g, :, a:b], lambda a, b, g=g: LB[g, :, a:b], 'L', epi)
            kb.new_phase()
        if has_ctx:
            LCc = IN(kb, "LCc")
            RCc = IN(kb, "RCc")
            FCT = kb.D("FCT", [4096, 256], BF16)
            for g in range(16):
                epi = kb.epi_act(lambda m0, ms, n0, ns, g=g: FCT[g * 256 + m0:g * 256 + m0 + ms, n0:n0 + ns], nst=2)
                kb.gemm(512, 256, 256, lambda a, b, g=g: LCc[g, :, a:b], lambda a, b, g=g: RCc[g, :, a:b], 'R', epi)
                kb.new_phase()
    return f


def build_Dd(kb):
    P = kb.P
    LC = IN(kb, "LC")
    TW = IN(kb, "TW")
    FQ = kb.D("FQ", [64, 128, 1024], BF16)
    tw = kb.buf([64, 2, 128], BF16)
    twv = TW.rearrange("j (c p) n -> p j c n", p=128)
    tw_toks = []
    for j0 in range(0, 64, 16):
        tw_toks.append(P.dma(POOL, tw.sem, tw.ap[:, j0:j0 + 16], twv[:, j0:j0 + 16]))
    lc = [kb.buf([2, 1024], BF16) for _ in range(4)]
    ob = [kb.buf([1024], BF16) for _ in range(4)]
    n = 0
    for j in range(64):
        b = lc[j % 4]
        o = ob[j % 4]
        b.load(LC[j].rearrange("(c p) n -> p c n", p=128), eng=(SP if j % 2 else POOL))
        etoks = []
        tok = None
        for nt in range(2):
            bi = kb.bank()
            ps = kb.banks[bi][:, 0:512]
            for c in range(2):
                tok = P.op(PE, lambda e, ps=ps, j=j, c=c, b=b, nt=nt: e.matmul(ps, tw.ap[:, j, c, :], b.ap[:, c, nt * 512:(nt + 1) * 512],
                                                                               start=(c == 0), stop=(c == 1)),
                           deps=([kb.bank_free[bi], b.ready] + tw_toks) if c == 0 else [], signal=(c == 1))
            dst = o.ap[:, nt * 512:(nt + 1) * 512]
            deps = [tok] + list(o.readers)
            if n % 2 == 0:
                t = P.op(ACT, lambda e, dst=dst, ps=ps: e.activation(out=dst, in_=ps, func=AF.Copy), deps=deps)
            else:
                t = P.op(DVE, lambda e, dst=dst, ps=ps: e.tensor_copy(out=dst, in_=ps), deps=deps)
            n += 1
            etoks.append(t)
            kb.bank_free[bi] = t
        b.read(tok)
        o.readers = []
        t_st = P.dma(SP, o.sem, FQ[j], o.ap, deps=etoks)
        o.readers.append(t_st)


def build_E(has_ctx, final):
    def f(kb):
        FT = IN(kb, "FT")
        SZT = IN(kb, "SZT")
        YT = kb.D("YT", [4096, 2048], BF16)
        kb.ew_mul(FT, SZT, YT, 4096, 2048)
        kb.new_phase()
        modD = IN(kb, "modD")
        A, sh, gt = kb.mod_cols(modD, IN(kb, "g16"), 0)
        xo = kb.D("xoT", [2048, 2048], F32)
        kb.outproj(IN(kb, "w_out"), YT, IN(kb, "xT"), xo, gt, 2048)
        if has_ctx:
            kb.new_phase()
            YCT = kb.D("YCT", [4096, 256], BF16)
            kb.ew_mul(IN(kb, "FCT"), IN(kb, "SZCT"), YCT, 4096, 256)
            kb.new_phase()
            A, sh, gtc = kb.mod_cols(modD, IN(kb, "g16"), 1)
            xco = kb.D("xcoT", [2048, 256], F32)
            kb.outproj(IN(kb, "w_out"), YCT, IN(kb, "xcT"), xco, gtc, 256)
        if final:
            kb.new_phase()
            fg = kb.load_const(IN(kb, "fg16"), [16], F32)
            outT = kb.D("outT", [2048, 2048], F32)
            kb.norm(xo, 2048, fg, None, outF=outT)
    return f


def T(a):
    return np.ascontiguousarray(np.asarray(a).T)


_CONSTS = {}


def consts():
    if not _CONSTS:
        _CONSTS.update(dft_consts())
    return _CONSTS


def run_mods(c, c_ctx, ada_w, ada_b):
    in_maps = []
    for core in range(8):
        b, i = core // 4, core % 4
        in_maps.append({"cc": np.ascontiguousarray(np.stack([c[b], c_ctx], 1)).astype(np.float32),
                        "ada_w": np.ascontiguousarray(ada_w[i]), "ada_b48": col48(ada_b[i])})
    res = run_launch(build_MOD, in_maps, ["modD"])
    return [[np.asarray(res[b * 4 + i]["modD"]) for b in range(2)] for i in range(4)]


def fnet_layer(xT, xcT, mods_i, g16, w_in, w_mix, w_out_i, has_ctx, final_g16=None):
    C = consts()
    in_maps = []
    for core in range(8):
        b = core // 4
        m = {"modD": mods_i[b], "g16": g16, "xT": xT[core], "w_in": w_in}
        if has_ctx:
            m["xcT"] = xcT[b]
        in_maps.append(m)
    outs = ["U", "SZT"] + (["UC", "SZCT"] if has_ctx else [])
    rA = run_launch(build_A(has_ctx), in_maps, outs)
    in_maps = []
    for core in range(8):
        b, q = core // 4, core % 4
        Ufull = np.concatenate([np.asarray(rA[b * 4 + s]["U"]) for s in range(4)], 0)
        UQ = np.ascontiguousarray(Ufull[:, q * 1024:(q + 1) * 1024]).reshape(128, 65536)
        m = {"UQ": UQ, "CSa": C["CSa"], "Cc": C["Cc"], "Sc": C["Sc"], "Scn": C["Scn"]}
        if has_ctx:
            m["WmT"] = np.ascontiguousarray(w_mix.transpose(1, 0, 2).reshape(256, 16 * 256))
            m["UC"] = np.asarray(rA[core]["UC"])
            m["CS256"] = C["CS256"]
        else:
            m["WmT"] = np.ascontiguousarray(w_mix[q * 4:(q + 1) * 4].transpose(1, 0, 2).reshape(256, 4 * 256))
        in_maps.append(m)
    ng = 16 if has_ctx else 4
    rB = run_launch(build_B(ng, has_ctx), in_maps, ["A1", "MM"] + (["AC"] if has_ctx else []))
    in_maps = []
    for core in range(8):
        q = core % 4
        A1 = np.asarray(rB[core]["A1"]).reshape(2, 128, 64, 4, 256)
        LB = np.ascontiguousarray(A1.transpose(3, 0, 4, 1, 2)).reshape(4, 512, 8192)
        MM = np.ascontiguousarray(np.asarray(rB[core]["MM"]).reshape(3, 256, ng, 256).transpose(0, 2, 1, 3))
        g0 = q * 4 if has_ctx else 0
        RB = np.zeros((4, 512, 512), NPBF)
        for gl in range(4):
            M1, M2, M2n = MM[0, g0 + gl], MM[1, g0 + gl], MM[2, g0 + gl]
            RB[gl, 0:256, 0:256] = M1
            RB[gl, 0:256, 256:512] = M2n
            RB[gl, 256:512, 0:256] = M2
            RB[gl, 256:512, 256:512] = M1
        m = {"LB": LB, "RB": RB}
        if has_ctx:
            AC = np.asarray(rB[core]["AC"]).reshape(16, 256, 2, 256)
            m["RCc"] = np.ascontiguousarray(AC.transpose(0, 2, 1, 3)).reshape(16, 512, 256)
            m["LCc"] = np.ascontiguousarray(np.concatenate([MM[0], MM[1]], 1))
        in_maps.append(m)
    rC = run_launch(build_C(has_ctx), in_maps, ["B1"] + (["FCT"] if has_ctx else []))
    in_maps = []
    for core in range(8):
        B1 = np.asarray(rC[core]["B1"]).reshape(4, 2, 256, 64, 2, 64)
        LC = np.ascontiguousarray(B1.transpose(3, 1, 4, 5, 0, 2)).reshape(64, 256, 1024)
        in_maps.append({"LC": LC, "TW": C["TW"]})
    rD = run_launch(build_Dd, in_maps, ["FQ"])
    Fq = []
    for core in range(8):
        FQ = np.asarray(rD[core]["FQ"]).reshape(64, 2, 64, 1024)
        Fq.append(np.ascontiguousarray(FQ.transpose(2, 0, 1, 3)).reshape(8192, 1024))
    in_maps = []
    for core in range(8):
        b, q = core // 4, core % 4
        FT = np.ascontiguousarray(np.concatenate([Fq[b * 4 + s][q * 2048:(q + 1) * 2048] for s in range(4)], 1).T)
        m = {"FT": FT, "SZT": np.asarray(rA[core]["SZT"]), "modD": mods_i[b], "g16": g16, "w_out": w_out_i,
             "xT": xT[core]}
        if has_ctx:
            m.update({"FCT": np.asarray(rC[core]["FCT"]), "SZCT": np.asarray(rA[core]["SZCT"]), "xcT": xcT[b]})
        if final_g16 is not None:
            m["fg16"] = final_g16
        in_maps.append(m)
    outs = ["xoT"] + (["xcoT"] if has_ctx else []) + (["outT"] if final_g16 is not None else [])
    rE = run_launch(build_E(has_ctx, final_g16 is not None), in_maps, outs)
    if final_g16 is not None:
        return [np.asarray(rE[k]["outT"]) for k in range(8)]
    new_x = [np.asarray(rE[k]["xoT"]) for k in range(8)]
    new_xc = [np.asarray(rE[b * 4]["xcoT"]) for b in range(2)] if has_ctx else xcT
    return new_x, new_xc


def build_F1(kb):
    front(kb, 2304, True)
    hT = kb.D("hT")
    hcT = kb.D("hcT")
    w = IN(kb, "w_in")
    QT0 = kb.D("QT0", [4096, 2048], BF16)
    KT0 = kb.D("KT0", [512, 2304], BF16)
    V = kb.D("V", [2304, 512], BF16)
    SZT = kb.D("SZT", [4096, 2048], BF16)
    KCT = kb.D("KCT", [512, 256], BF16)
    VC = kb.D("VC", [256, 512], BF16)
    epi = kb.epi_act(lambda m0, ms, n0, ns: QT0[m0:m0 + ms, n0:n0 + ns])
    kb.gemm(2048, 4096, 2048, lambda a, b: w[:, a:b], lambda a, b: hT[:, 128 + a:128 + b], 'R', epi)
    kb.new_phase()
    epi = kb.epi_act(lambda m0, ms, n0, ns: SZT[m0:m0 + ms, n0:n0 + ns], func=AF.Silu)
    kb.gemm(2048, 4096, 2048, lambda a, b: w[:, 5120 + a:5120 + b], lambda a, b: hT[:, 128 + a:128 + b], 'R', epi)
    kb.new_phase()
    epi = kb.epi_act(lambda m0, ms, n0, ns: KT0[m0:m0 + ms, n0:n0 + ns])
    kb.gemm(2048, 512, 2304, lambda a, b: w[:, 4096 + a:4096 + b], lambda a, b: hT[:, a:b], 'R', epi)
    kb.new_phase()
    epi = kb.epi_act(lambda m0, ms, n0, ns: V[m0:m0 + ms, n0:n0 + ns])
    kb.gemm(2048, 2304, 512, lambda a, b: hT[:, a:b], lambda a, b: w[:, 4608 + a:4608 + b], 'L', epi)
    kb.new_phase()
    epi = kb.epi_act(lambda m0, ms, n0, ns: KCT[m0:m0 + ms, n0:n0 + ns])
    kb.gemm(2048, 512, 256, lambda a, b: w[:, 4096 + a:4096 + b], lambda a, b: hcT[:, a:b], 'R', epi)
    kb.new_phase()
    epi = kb.epi_act(lambda m0, ms, n0, ns: VC[m0:m0 + ms, n0:n0 + ns])
    kb.gemm(2048, 256, 512, lambda a, b: hcT[:, a:b], lambda a, b: w[:, 4608 + a:4608 + b], 'L', epi)


def rope_tables(pos):
    rows = (pos // 64).astype(np.float64)
    cols = (pos % 64).astype(np.float64)
    inv = 10000.0 ** (-np.arange(16) / 16.0)
    cosT = np.zeros((64, len(pos)), np.float32)
    ssin = np.zeros((64, len(pos)), np.float32)
    for d in range(64):
        half, wv = d // 32, d % 32
        f, part = wv % 16, wv // 16
        ang = (rows if half == 0 else cols) * inv[f]
        cosT[d] = np.cos(ang)
        ssin[d] = np.sin(ang) * (-1.0 if part == 0 else 1.0)
    return np.ascontiguousarray(np.tile(cosT, (2, 1))), np.ascontiguousarray(np.tile(ssin, (2, 1)))


def rope_perm_rows(nheads):
    idx = np.arange(nheads * 64).reshape(nheads, 64)
    d = np.arange(64)
    wv = d % 32
    partner = np.where(wv // 16 == 0, d + 16, d - 16)
    return idx[:, partner].reshape(-1)


def build_G1a(kb):
    P = kb.P
    for (nm, rows, Tn) in (("Q", 4096, 2048), ("K", 512, 2304)):
        a = IN(kb, nm + "T0")
        bp = IN(kb, nm + "T0p")
        cosD = IN(kb, "cos" + nm)
        sinD = IN(kb, "ssin" + nm)
        out = kb.D(nm + "Tr", [rows, Tn], BF16)
        cs = kb.load_const(cosD, [Tn], F32)
        sn = kb.load_const(sinD, [Tn], F32)
        ab = [kb.buf([Tn], BF16) for _ in range(2)]
        bb = [kb.buf([Tn], BF16) for _ in range(2)]
        t1 = [kb.buf([Tn], F32) for _ in range(2)]
        t2 = [kb.buf([Tn], F32) for _ in range(2)]
        ob = [kb.buf([Tn], BF16) for _ in range(2)]
        for ch in range(rows // 128):
            i = ch % 2
            A_, B_, T1, T2, O_ = ab[i], bb[i], t1[i], t2[i], ob[i]
            A_.load(a[ch * 128:(ch + 1) * 128, :])
            B_.load(bp[ch * 128:(ch + 1) * 128, :], eng=SP)
            ta = P.op(DVE, lambda e, A_=A_, T1=T1, cs=cs: e.tensor_tensor(T1.ap, A_.ap, cs.ap, ALU.mult),
                      deps=[A_.ready, cs.ready] + T1.readers)
            T1.wrote(ta)
            A_.read(ta)
            tb = P.op(POOL, lambda e, B_=B_, T2=T2, sn=sn: e.tensor_tensor(T2.ap, B_.ap, sn.ap, ALU.mult),
                      deps=[B_.ready, sn.ready] + T2.readers)
            T2.wrote(tb)
            B_.read(tb)
            tc = P.op(DVE, lambda e, T1=T1, T2=T2, O_=O_: e.tensor_tensor(O_.ap, T1.ap, T2.ap, ALU.add),
                      deps=[ta, tb] + O_.readers)
            T1.read(tc)
            T2.read(tc)
            O_.wrote(tc)
            O_.store(out[ch * 128:(ch + 1) * 128, :])
        kb.new_phase()


def attn_core(kb, QTr, KTr, V, KCT, VC, SZT, YT, maskD, sinkD):
    P = kb.P
    Qv = QTr.rearrange("(h g d) t -> h d g t", g=8, d=64)
    Sv = SZT.rearrange("(h g d) t -> h d g t", g=8, d=64)
    Yv = YT.rearrange("(h g d) t -> h d g t", g=8, d=64)
    masks = kb.load_const(maskD.rearrange("k p n -> p k n"), [4, 512], BF16)
    srow = kb.buf([8192], F32)
    srow.load(sinkD, dst=srow.ap[0:1])
    esrow = kb.buf([8192], BF16)
    t = P.op(ACT, lambda e: e.activation(out=esrow.ap[0:1], in_=srow.ap[0:1], func=AF.Exp), deps=[srow.ready])
    esrow.wrote(t)
    ones = kb.buf([64], BF16)
    t = P.op(DVE, lambda e: e.memset(ones.ap, 1.0))
    ones.wrote(t)
    Qh = [kb.buf([8, 2048], BF16) for _ in range(2)]
    Kh = [kb.buf([2560], BF16) for _ in range(2)]
    Vh = [kb.buf([20, 64], BF16) for _ in range(2)]
    Pt = [kb.buf([5, 512], BF16) for _ in range(3)]
    rD = [kb.buf([512], F32) for _ in range(2)]
    yt = [kb.buf([512], F32) for _ in range(2)]
    szb = [kb.buf([8, 128], BF16) for _ in range(3)]
    yb = [kb.buf([8, 128], BF16) for _ in range(3)]
    kv = {}

    def load_hk(hk):
        Q_, K_, V_ = Qh[hk % 2], Kh[hk % 2], Vh[hk % 2]
        Q_.load(Qv[hk], dst=Q_.ap[0:64])
        K_.load(KTr[hk * 64:(hk + 1) * 64, :], dst=K_.ap[0:64, 0:2304], eng=SP)
        tk2 = K_.ready
        K_.readers = []
        t2 = P.dma(SP, K_.sem, K_.ap[0:64, 2304:2560], KCT[hk * 64:(hk + 1) * 64, :])
        K_.ready = t2
        V_.load(V[:, hk * 64:(hk + 1) * 64].rearrange("(b p) d -> p b d", p=128), dst=V_.ap[:, 0:18, :], eng=SP)
        tv1 = V_.ready
        V_.readers = []
        tv2 = P.dma(SP, V_.sem, V_.ap[:, 18:20, :], VC[:, hk * 64:(hk + 1) * 64].rearrange("(b p) d -> p b d", p=128))
        V_.ready = tv2
        kv[hk] = ([tk2, t2], [tv1, tv2])

    steps = [(hk, b, half) for hk in range(8) for b in range(16) for half in range(2)]
    st = {}

    def emit_S(i):
        hk, b, half = steps[i]
        if b == 0 and half == 0:
            if hk == 0:
                load_hk(0)
            if hk + 1 < 8:
                load_hk(hk + 1)
        Q_, K_ = Qh[hk % 2], Kh[hk % 2]
        kdeps = kv[hk][0]
        P_ = Pt[i % 3]
        rhs = Q_.ap[0:64, 4 * half:4 * half + 4, b * 128:(b + 1) * 128]
        blks = [b, b + 1, b + 2, 18, 19]
        pt_toks = []
        tm = None
        for j, blk in enumerate(blks):
            bi = kb.bank()
            ps = kb.banks[bi][:, 0:512]
            kcols = K_.ap[0:64, blk * 128:(blk + 1) * 128]
            tm = P.op(PE, lambda e, ps=ps, kcols=kcols, rhs=rhs: e.matmul(ps, kcols, rhs, start=True, stop=True),
                      deps=[kb.bank_free[bi], Q_.ready] + kdeps)
            te = P.op(ACT, lambda e, ps=ps, P_=P_, j=j: e.activation(out=P_.ap[:, j, :], in_=ps, func=AF.Exp, scale=0.125),
                      deps=[tm] + (P_.readers if j == 0 else []))
            kb.bank_free[bi] = te
            if j in (0, 2):
                mi = (0 if j == 0 else 1)
                if j == 0 and b == 0:
                    mi = 2
                if j == 2 and b == 15:
                    mi = 3
                te = P.op(DVE, lambda e, P_=P_, j=j, mi=mi: e.tensor_tensor(P_.ap[:, j, :], P_.ap[:, j, :], masks.ap[:, mi, :], ALU.mult),
                          deps=[te, masks.ready])
            pt_toks.append(te)
        Q_.read(tm)
        K_.read(tm)
        P_.wrote(pt_toks[-1])
        st[i] = pt_toks

    def emit_PV(i):
        hk, b, half = steps[i]
        V_ = Vh[hk % 2]
        vdeps = kv[hk][1]
        P_ = Pt[i % 3]
        pt_toks = st.pop(i)
        blks = [b, b + 1, b + 2, 18, 19]
        it = (hk * 16 + b) % 3
        SZ_, Y_ = szb[it], yb[it]
        if half == 0:
            SZ_.load(Sv[hk][:, :, b * 128:(b + 1) * 128], dst=SZ_.ap[0:64], eng=SP)
        bo, bd = kb.bank(), kb.bank()
        pso = kb.banks[bo][0:64, 0:512]
        psd = kb.banks[bd][0:64, 0:512]
        to = None
        for j, blk in enumerate(blks):
            to = P.op(PE, lambda e, pso=pso, V_=V_, blk=blk, P_=P_, j=j: e.matmul(pso, V_.ap[:, blk, :], P_.ap[:, j, :], start=(j == 0), stop=(j == 4)),
                      deps=(pt_toks + vdeps + [kb.bank_free[bo]]) if j == 0 else [], signal=(j == 4))
        for j in range(5):
            P.op(PE, lambda e, psd=psd, P_=P_, j=j: e.matmul(psd, ones.ap, P_.ap[:, j, :], start=(j == 0), stop=False),
                 deps=[kb.bank_free[bd], ones.ready] if j == 0 else [], signal=False)
        h0 = hk * 8 + 4 * half
        td = P.op(PE, lambda e, psd=psd, h0=h0: e.matmul(psd, ones.ap[0:1, :], esrow.ap[0:1, h0 * 128:h0 * 128 + 512], start=False, stop=True),
                  deps=[esrow.ready])
        P_.read(td)
        V_.read(td)
        R_, T_ = rD[half], yt[half]
        tr = P.op(DVE, lambda e, psd=psd, R_=R_: e.reciprocal(R_.ap[0:64], psd), deps=[td] + R_.readers)
        R_.wrote(tr)
        kb.bank_free[bd] = tr
        ty = P.op(DVE, lambda e, pso=pso, R_=R_, T_=T_: e.tensor_tensor(T_.ap[0:64], pso, R_.ap[0:64], ALU.mult),
                  deps=[to, tr] + T_.readers)
        T_.wrote(ty)
        R_.read(ty)
        kb.bank_free[bo] = ty
        tz = P.op(POOL, lambda e, T_=T_, Y_=Y_, SZ_=SZ_, half=half: e.tensor_tensor(
            Y_.ap[0:64, 4 * half:4 * half + 4, :], T_.ap[0:64].rearrange("p (g q) -> p g q", g=4),
            SZ_.ap[0:64, 4 * half:4 * half + 4, :], ALU.mult),
            deps=[ty, SZ_.ready] + (Y_.readers if half == 0 else []))
        T_.read(tz)
        if half == 1:
            SZ_.read(tz)
            Y_.wrote(tz)
            Y_.store(Yv[hk][:, :, b * 128:(b + 1) * 128], src=Y_.ap[0:64])

    n = len(steps)
    emit_S(0)
    for i in range(n):
        if i + 1 < n:
            emit_S(i + 1)
        emit_PV(i)


def build_G1b(kb):
    build_G1a(kb)
    YT = kb.D("YT", [4096, 2048], BF16)
    attn_core(kb, kb.D("QTr"), kb.D("KTr"), IN(kb, "V"), IN(kb, "KCT"), IN(kb, "VC"), IN(kb, "SZT"), YT,
              IN(kb, "maskx"), IN(kb, "sinkrep"))
    kb.new_phase()
    A, sh, gt = kb.mod_cols(IN(kb, "modD"), IN(kb, "g16"), 0)
    xo = kb.D("xoT", [2048, 2048], F32)
    kb.outproj(IN(kb, "w_out"), YT, IN(kb, "xT"), xo, gt, 2048)


def attn_layer(xT, xcT, mods_i, g16, w_in, sink, w_out_i):
    in_maps = []
    for core in range(8):
        b, q = core // 4, core % 4
        left = xT[core - 1][:, -128:] if q > 0 else np.zeros((2048, 128), np.float32)
        right = xT[core + 1][:, :128] if q < 3 else np.zeros((2048, 128), np.float32)
        xh = np.ascontiguousarray(np.concatenate([left, xT[core], right], 1))
        in_maps.append({"modD": mods_i[b], "g16": g16, "xT": xh, "xcT": xcT[b], "w_in": w_in})
    rF = run_launch(build_F1, in_maps, ["QT0", "KT0", "V", "SZT", "KCT", "VC"])
    pq, pk = rope_perm_rows(64), rope_perm_rows(8)
    in_maps = []
    for core in range(8):
        q = core % 4
        cq, sq = rope_tables(q * 2048 + np.arange(2048))
        ck, sk = rope_tables(np.abs(q * 2048 - 128 + np.arange(2304)))
        QT0 = np.asarray(rF[core]["QT0"])
        KT0 = np.asarray(rF[core]["KT0"])
        in_maps.append({"QT0": QT0, "QT0p": np.ascontiguousarray(QT0[pq]), "KT0": KT0, "KT0p": np.ascontiguousarray(KT0[pk]),
                        "cosQ": cq, "ssinQ": sq, "cosK": ck, "ssinK": sk})
    rope_maps = in_maps
    kj = np.arange(128)[:, None]
    qi = np.arange(128)[None, :]
    mprev = np.tile((kj >= qi).astype(np.float32), (1, 4))
    mnext = np.tile((kj <= qi).astype(np.float32), (1, 4))
    zero = np.zeros_like(mprev)
    in_maps = []
    for core in range(8):
        b, q = core // 4, core % 4
        maskx = np.stack([mprev, mnext, zero if q == 0 else mprev, zero if q == 3 else mnext], 0).astype(NPBF)
        in_maps.append({**rope_maps[core], "V": np.asarray(rF[core]["V"]), "KCT": np.asarray(rF[core]["KCT"]), "VC": np.asarray(rF[core]["VC"]),
                        "SZT": np.asarray(rF[core]["SZT"]), "maskx": maskx,
                        "sinkrep": np.ascontiguousarray(np.repeat(sink.astype(np.float32), 128)[None, :]),
                        "modD": mods_i[b], "g16": g16, "w_out": w_out_i, "xT": xT[core]})
    rH = run_launch(build_G1b, in_maps, ["xoT"])
    return [np.asarray(rH[k]["xoT"]) for k in range(8)]


AXX = mybir.AxisListType.X


def build_H2(kb):
    P = kb.P
    front(kb, 2048, False)
    hT = kb.D("hT")
    w = IN(kb, "w_in")
    U = kb.D("U", [2048, 4096], BF16)
    Vf = kb.D("Vf", [2048, 4096], F32)
    SZ = kb.D("SZ", [2048, 4096], BF16)
    e_u = kb.epi_act(lambda m0, ms, n0, ns: U[m0:m0 + ms, n0:n0 + ns], func=AF.Gelu_apprx_tanh, nst=3)
    e_v = kb.epi_act(lambda m0, ms, n0, ns: Vf[m0:m0 + ms, n0 - 4096:n0 - 4096 + ns], func=AF.Gelu_apprx_tanh, dt=F32, nst=3)
    e_z = kb.epi_act(lambda m0, ms, n0, ns: SZ[m0:m0 + ms, n0 - 8192:n0 - 8192 + ns], func=AF.Silu, nst=3)

    def epi(m0, ms, n0, ns, ps, tok):
        return (e_u if n0 < 4096 else (e_v if n0 < 8192 else e_z))(m0, ms, n0, ns, ps, tok)
    kb.gemm(2048, 2048, 12288, lambda a, b: hT[:, a:b], lambda a, b: w[:, a:b], 'L', epi)
    kb.new_phase()
    VN = kb.D("VN", [2048, 4096], BF16)
    G = kb.load_const(IN(kb, "lnG"), [4096], F32)
    Bt = kb.load_const(IN(kb, "lnB"), [4096], F32, eng=SP)
    epsb = kb.buf([1], F32)
    t = P.op(DVE, lambda e: e.memset(epsb.ap, EPS))
    epsb.wrote(t)
    vb = [kb.buf([4096], F32) for _ in range(2)]
    sq = kb.buf([4096], F32)
    vh = kb.buf([4096], F32)
    ob = [kb.buf([4096], BF16) for _ in range(2)]
    st = [kb.buf([8], F32) for _ in range(2)]
    vb[0].load(Vf[0:128, :])
    for i in range(16):
        if i + 1 < 16:
            vb[(i + 1) % 2].load(Vf[(i + 1) * 128:(i + 2) * 128, :])
        v, o, s = vb[i % 2], ob[i % 2], st[i % 2]
        t_sq = P.op(ACT, lambda e, v=v: e.activation(out=sq.ap, in_=v.ap, func=AF.Square), deps=[v.ready] + sq.readers)
        sq.wrote(t_sq)
        t1 = P.op(DVE, lambda e, v=v, s=s: e.reduce_sum(out=s.ap[:, 0:1], in_=v.ap, axis=AXX), deps=[v.ready] + s.readers)
        t2 = P.op(DVE, lambda e, s=s: e.reduce_sum(out=s.ap[:, 1:2], in_=sq.ap, axis=AXX), deps=[t_sq, t1])
        sq.read(t2)
        t3 = P.op(DVE, lambda e, s=s: e.tensor_scalar(s.ap[:, 2:3], s.ap[:, 0:1], 1.0 / 4096.0, None, ALU.mult), deps=[t2])
        t4 = P.op(DVE, lambda e, s=s: e.tensor_tensor(s.ap[:, 3:4], s.ap[:, 2:3], s.ap[:, 2:3], ALU.mult), deps=[t3])
        t5 = P.op(DVE, lambda e, s=s: e.scalar_tensor_tensor(out=s.ap[:, 4:5], in0=s.ap[:, 1:2], scalar=1.0 / 4096.0,
                                                             in1=s.ap[:, 3:4], op0=ALU.mult, op1=ALU.subtract), deps=[t4])
        t6 = P.op(ACT, lambda e, s=s: e.activation(out=s.ap[:, 5:6], in_=s.ap[:, 4:5], func=AF.Sqrt, bias=epsb.ap), deps=[t5, epsb.ready])
        t7 = P.op(DVE, lambda e, s=s: e.reciprocal(s.ap[:, 5:6], s.ap[:, 5:6]), deps=[t6])
        t8 = P.op(DVE, lambda e, s=s: e.scalar_tensor_tensor(out=s.ap[:, 6:7], in0=s.ap[:, 2:3], scalar=-1.0,
                                                             in1=s.ap[:, 5:6], op0=ALU.mult, op1=ALU.mult), deps=[t7])
        t9 = P.op(ACT, lambda e, v=v, s=s: e.activation(out=vh.ap, in_=v.ap, func=AF.Identity, scale=s.ap[:, 5:6], bias=s.ap[:, 6:7]),
                  deps=[t8] + vh.readers)
        vh.wrote(t9)
        v.read(t9)
        v.read(t2)
        t10 = P.op(DVE, lambda e: e.tensor_tensor(vh.ap, vh.ap, G.ap, ALU.mult), deps=[t9, G.ready])
        t11 = P.op(DVE, lambda e, o=o: e.tensor_tensor(o.ap, vh.ap, Bt.ap, ALU.add), deps=[t10, Bt.ready] + o.readers)
        vh.read(t11)
        s.read(t11)
        o.wrote(t11)
        o.store(VN[i * 128:(i + 1) * 128, :])
    kb.new_phase()
    Y = kb.D("Y", [2048, 4096], BF16)
    WsT = IN(kb, "WsT")
    bs = kb.load_const(IN(kb, "bsT"), [16], F32)

    def view(D_, g, kk):
        return D_.rearrange("(k t) (g c) -> g t k c", t=128, c=256)[g][:, kk:kk + 2, :]
    wt = [kb.buf([128], BF16) for _ in range(2)]
    rb = [kb.buf([2, 256], BF16) for _ in range(2)]
    ub = [kb.buf([2, 256], BF16) for _ in range(2)]
    zb = [kb.buf([2, 256], BF16) for _ in range(2)]
    sb_ = [kb.buf([512], F32) for _ in range(2)]
    yb = [kb.buf([2, 256], BF16) for _ in range(2)]
    it = 0
    for g in range(16):
        W_ = wt[g % 2]
        W_.load(WsT[g])
        for kk in range(0, 16, 2):
            i = it % 2
            it += 1
            R_, U_, Z_, S_, Y_ = rb[i], ub[i], zb[i], sb_[i], yb[i]
            R_.load(view(VN, g, kk))
            U_.load(view(U, g, kk), eng=SP)
            Z_.load(view(SZ, g, kk), eng=SP)
            bi = kb.bank()
            ps = kb.banks[bi][:, 0:512]
            tm = P.op(PE, lambda e, ps=ps, W_=W_, R_=R_: e.matmul(ps, W_.ap, R_.ap.rearrange("p k c -> p (k c)"), start=True, stop=True),
                      deps=[W_.ready, R_.ready, kb.bank_free[bi]])
            W_.read(tm)
            R_.read(tm)
            ts = P.op(ACT, lambda e, ps=ps, S_=S_, g=g: e.activation(out=S_.ap, in_=ps, func=AF.Identity, bias=bs.ap[:, g:g + 1]),
                      deps=[tm, bs.ready] + S_.readers)
            kb.bank_free[bi] = ts
            S_.wrote(ts)
            ta = P.op(DVE, lambda e, S_=S_, U_=U_: e.tensor_tensor(S_.ap, S_.ap, U_.ap.rearrange("p k c -> p (k c)"), ALU.mult),
                      deps=[ts, U_.ready])
            U_.read(ta)
            tb = P.op(DVE, lambda e, S_=S_, Z_=Z_, Y_=Y_: e.tensor_tensor(Y_.ap.rearrange("p k c -> p (k c)"), S_.ap,
                                                                         Z_.ap.rearrange("p k c -> p (k c)"), ALU.mult),
                      deps=[ta, Z_.ready] + Y_.readers)
            Z_.read(tb)
            S_.read(tb)
            Y_.wrote(tb)
            Y_.store(view(Y, g, kk))


def build_I(final):
    def f(kb):
        A, sh, gt = kb.mod_cols(IN(kb, "modD"), IN(kb, "g16"), 0)
        xo = kb.D("xoT", [2048, 2048], F32)
        kb.outproj(IN(kb, "w_out"), IN(kb, "YT"), IN(kb, "xT"), xo, gt, 2048)
    return f


def gmlp_layer(xT, mods_i, g16, w_in, w_s, b_s, ln_g, ln_b, w_out_i):
    in_maps = []
    for core in range(8):
        b = core // 4
        in_maps.append({"modD": mods_i[b], "g16": g16, "xT": xT[core], "w_in": w_in,
                        "lnG": np.ascontiguousarray(np.broadcast_to(ln_g[None, :], (128, 4096))).astype(np.float32),
                        "lnB": np.ascontiguousarray(np.broadcast_to(ln_b[None, :], (128, 4096))).astype(np.float32),
                        "WsT": np.ascontiguousarray(w_s.transpose(0, 2, 1)), "bsT": T(b_s)})
    rH = run_launch(build_H2, in_maps, ["Y"])
    in_maps = []
    for core in range(8):
        b = core // 4
        in_maps.append({"YT": T(rH[core]["Y"]), "modD": mods_i[b], "g16": g16, "w_out": w_out_i, "xT": xT[core]})
    rI = run_launch(build_I(False), in_maps, ["xoT"])
    return [np.asarray(rI[k]["xoT"]) for k in range(8)]


def kernel(x, c, ctx, c_ctx, norm_g, ada_w, ada_b, w_out, fnet_w_in, fnet_w_mix, attn_w_in, attn_sink,
           gmlp_w_in, gmlp_w_s, gmlp_b_s, gmlp_ln_g, gmlp_ln_b, final_g):
    f32 = lambda a: np.asarray(a, np.float32)
    x, c, ctx, c_ctx, norm_g, ada_w, ada_b, w_out = map(f32, (x, c, ctx, c_ctx, norm_g, ada_w, ada_b, w_out))
    fnet_w_in, fnet_w_mix, attn_w_in, attn_sink = map(f32, (fnet_w_in, fnet_w_mix, attn_w_in, attn_sink))
    gmlp_w_in, gmlp_w_s, gmlp_b_s, gmlp_ln_g, gmlp_ln_b, final_g = map(
        f32, (gmlp_w_in, gmlp_w_s, gmlp_b_s, gmlp_ln_g, gmlp_ln_b, final_g))
    mods = run_mods(c, c_ctx, ada_w, ada_b)
    xT = [T(x[k // 4, (k % 4) * 2048:(k % 4 + 1) * 2048]) for k in range(8)]
    xcT = [T(ctx[b]) for b in range(2)]
    xT, xcT = fnet_layer(xT, xcT, mods[0], col48(norm_g[0]), fnet_w_in[0], fnet_w_mix[0], w_out[0], True)
    xT = attn_layer(xT, xcT, mods[1], col48(norm_g[1]), attn_w_in[0], attn_sink[0], w_out[1])
    xT = gmlp_layer(xT, mods[2], col48(norm_g[2]), gmlp_w_in[0], gmlp_w_s[0], gmlp_b_s[0], gmlp_ln_g[0], gmlp_ln_b[0], w_out[2])
    oT = fnet_layer(xT, None, mods[3], col48(norm_g[3]), fnet_w_in[1], fnet_w_mix[1], w_out[3], False,
                    final_g16=col48(final_g))
    out = np.empty((2, 8192, 2048), np.float32)
    for k in range(8):
        out[k // 4, (k % 4) * 2048:(k % 4 + 1) * 2048] = oT[k].T
    return out
```

```python
import contextlib
import math
import numpy as np
import ml_dtypes
import concourse.bass as bass
import concourse.mybir as mybir
from concourse.bass_utils import run_bass_kernel_spmd

F32 = mybir.dt.float32
BF16 = mybir.dt.bfloat16
AF = mybir.ActivationFunctionType
ALU = mybir.AluOpType
NPBF = ml_dtypes.bfloat16

PE, ACT, DVE, POOL, SP = "pe", "act", "dve", "pool", "sp"
ENGS = (PE, ACT, DVE, POOL, SP)

D_MODEL = 2048
D_BRANCH = 4096
EPS = 1e-6
ARENA_F32 = 46 * 1024


class Prog:
    def __init__(self, nc, stack):
        self.nc = nc
        self.stack = stack
        self.q = {e: [] for e in ENGS}
        self.nsem = 0
        self.esem = {e: self.sem("e_" + e) for e in (PE, ACT, DVE, POOL)}
        self.ecnt = {e: 0 for e in (PE, ACT, DVE, POOL)}
        self.dsems = []
        self.dcnt = {}

    def sem(self, name):
        self.nsem += 1
        return self.stack.enter_context(self.nc.semaphore(f"{name}_{self.nsem}"))

    def dsem(self, name="d"):
        s = self.sem(name)
        self.dsems.append(s)
        self.dcnt[id(s)] = 0
        return s

    def op(self, eng, fn, deps=(), signal=True):
        tok = None
        inc = None
        if signal:
            self.ecnt[eng] += 1
            tok = (self.esem[eng], self.ecnt[eng])
            inc = (self.esem[eng], 1)
        self.q[eng].append((tuple(d for d in deps if d is not None), fn, inc))
        return tok

    def dma(self, eng, sem, out, in_, deps=()):
        self.dcnt[id(sem)] += 16
        tok = (sem, self.dcnt[id(sem)])
        self.q[eng].append((tuple(d for d in deps if d is not None),
                            (lambda e, out=out, in_=in_: e.dma_start(out=out, in_=in_)), (sem, 16)))
        return tok

    def coll(self, sem, kind, src, dst, groups, deps=()):
        self.dcnt[id(sem)] += 1
        tok = (sem, self.dcnt[id(sem)])
        self.q[POOL].append((tuple(d for d in deps if d is not None),
                             (lambda e: e.collective_compute(kind, ALU.bypass, replica_groups=groups,
                                                             ins=[src.opt()], outs=[dst.opt()])), (sem, 1)))
        return tok

    def wait(self, eng, deps):
        self.q[eng].append((tuple(d for d in deps if d is not None), None, None))

    def all_tokens(self):
        toks = [(self.esem[e], self.ecnt[e]) for e in self.ecnt if self.ecnt[e] > 0]
        toks += [(s, self.dcnt[id(s)]) for s in self.dsems if self.dcnt[id(s)] > 0]
        return toks

    def barrier(self):
        toks = self.all_tokens()
        for e in ENGS:
            self.wait(e, toks)

    def emit(self):
        nc = self.nc
        with nc.Block() as block:
            def replay(name):
                def run(e):
                    seen = {}
                    for deps, fn, inc in self.q[name]:
                        need = {}
                        for (s, v) in deps:
                            k = id(s)
                            if k not in need or need[k][1] < v:
                                need[k] = (s, v)
                        for k, (s, v) in need.items():
                            if seen.get(k, 0) < v:
                                e.wait_ge(s, v)
                                seen[k] = v
                        if fn is None:
                            continue
                        ins = fn(e)
                        if inc is not None:
                            ins.then_inc(inc[0], inc[1])
                return run
            block.tensor(replay(PE))
            block.scalar(replay(ACT))
            block.vector(replay(DVE))
            block.gpsimd(replay(POOL))
            block.sync(replay(SP))


class Buf:
    def __init__(self, kb, ap):
        self.kb = kb
        self.ap = ap
        self.sem = kb.get_dsem()
        self.ready = None
        self.readers = []

    def load(self, src, eng=POOL, deps=(), dst=None):
        P = self.kb.P
        t = P.dma(eng, self.sem, self.ap if dst is None else dst, src, deps=list(self.readers) + list(deps))
        self.ready = t
        self.readers = []
        return t

    def wrote(self, tok):
        self.ready = tok
        self.readers = []

    def read(self, tok):
        if tok is not None:
            self.readers.append(tok)
            if len(self.readers) > 24:
                self.readers = self.readers[-24:]

    def store(self, dst, deps=(), src=None, eng=SP):
        P = self.kb.P
        t = P.dma(eng, self.sem, dst, self.ap if src is None else src, deps=[self.ready] + list(deps))
        self.readers.append(t)
        return t


class KB:
    def __init__(self, nc, stack, ext_in=(), ext_out=()):
        self.nc = nc
        self.P = Prog(nc, stack)
        self.arena = stack.enter_context(nc.sbuf_tensor("arena", [128, ARENA_F32], F32))
        self.banks = [stack.enter_context(nc.psum_tensor(f"bank{i}", [128, 512], F32)) for i in range(8)]
        self.bank_free = [None] * 8
        self.bank_i = 0
        self.off = 0
        self.dram = {}
        self.ext_in = set(ext_in)
        self.ext_out = set(ext_out)
        self.out_toks = []

    def get_dsem(self):
        if not hasattr(self, "sem_pool"):
            self.sem_pool = []
            self.sem_next = 0
        if self.sem_next >= len(self.sem_pool):
            self.sem_pool.append(self.P.dsem("b"))
        s = self.sem_pool[self.sem_next]
        self.sem_next += 1
        return s

    def D(self, name, shape=None, dt=None):
        if name in self.dram:
            return self.dram[name]
        kind = "ExternalInput" if name in self.ext_in else ("ExternalOutput" if name in self.ext_out else "Internal")
        t = self.nc.dram_tensor(name, list(shape), dt, kind=kind).ap()
        self.dram[name] = t
        return t

    def alloc(self, shape, dt):
        n = int(np.prod(shape))
        nf32 = (n * (2 if dt == BF16 else 4) + 3) // 4
        nf32 = (nf32 + 7) // 8 * 8
        assert self.off + nf32 <= ARENA_F32, (self.off, nf32, shape)
        ap = self.arena[:, self.off:self.off + nf32]
        self.off += nf32
        if dt == BF16:
            ap = ap.bitcast(BF16)
        ap = ap[:, 0:n]
        if len(shape) == 2:
            ap = ap.rearrange("p (a b) -> p a b", a=shape[0])
        elif len(shape) == 3:
            ap = ap.rearrange("p (a b c) -> p a b c", a=shape[0], b=shape[1])
        elif len(shape) == 4:
            ap = ap.rearrange("p (a b c d) -> p a b c d", a=shape[0], b=shape[1], c=shape[2])
        return ap

    def buf(self, shape, dt):
        return Buf(self, self.alloc(shape, dt))

    def new_phase(self):
        self.P.barrier()
        self.off = 0
        self.sem_next = 0
        self.bank_free = [None] * 8

    def bank(self):
        i = self.bank_i
        self.bank_i = (i + 1) % 8
        return i

    def gemm(self, K, M, N, Lsrc, Rsrc, resident, epi, l_dt=BF16, r_dt=BF16, sblk=512, kp=128, deps=()):
        P = self.P
        KC = K // kp
        assert K % kp == 0

        def view(src):
            return src.rearrange("(c p) x -> p c x", p=kp)

        RESX = M if resident == 'L' else N
        STRX = N if resident == 'L' else M
        res = self.buf([KC, RESX], BF16)
        res_ap = res.ap if kp == 128 else res.ap[0:kp]
        rsrc = Lsrc if resident == 'L' else Rsrc
        ssrc = Rsrc if resident == 'L' else Lsrc
        nres_toks = []
        for x0 in range(0, RESX, 1024):
            x1 = min(RESX, x0 + 1024)
            nres_toks.append(res.load(view(rsrc(x0, x1)), deps=deps, dst=res_ap[:, :, x0:x1]))
            res.readers = []
        sblk = min(sblk, STRX)
        nblk = (STRX + sblk - 1) // sblk
        sb = [self.buf([KC, sblk], BF16) for _ in range(min(2, nblk))]

        def issue(s):
            b = sb[s % len(sb)]
            x0 = s * sblk
            x1 = min(STRX, x0 + sblk)
            dst = (b.ap if kp == 128 else b.ap[0:kp])[:, :, 0:x1 - x0]
            b.load(view(ssrc(x0, x1)), deps=deps, dst=dst)

        issue(0)
        for s in range(nblk):
            if s + 1 < nblk:
                issue(s + 1)
            b = sb[s % len(sb)]
            bap = b.ap if kp == 128 else b.ap[0:kp]
            x0 = s * sblk
            x1 = min(STRX, x0 + sblk)
            if resident == 'L':
                tiles = [(m0, min(M, m0 + 128), n0, min(x1, n0 + 512))
                         for m0 in range(0, M, 128) for n0 in range(x0, x1, 512)]
            else:
                tiles = [(m0, min(x1, m0 + 128), n0, min(N, n0 + 512))
                         for m0 in range(x0, x1, 128) for n0 in range(0, N, 512)]
            for (m0, m1, n0, n1) in tiles:
                bi = self.bank()
                ps = self.banks[bi][0:m1 - m0, 0:n1 - n0]
                tok = None
                for c in range(KC):
                    if resident == 'L':
                        lt = res_ap[:, c, m0:m1]
                        rt = bap[:, c, n0 - x0:n1 - x0]
                    else:
                        lt = bap[:, c, m0 - x0:m1 - x0]
                        rt = res_ap[:, c, n0:n1]
                    d = [self.bank_free[bi], b.ready] + nres_toks if c == 0 else []
                    last = (c == KC - 1)
                    tok = P.op(PE, (lambda e, ps=ps, lt=lt, rt=rt, c=c, last=last:
                                    e.matmul(ps, lt, rt, start=(c == 0), stop=last)),
                               deps=d, signal=last)
                b.read(tok)
                res.read(tok)
                self.bank_free[bi] = epi(m0, m1 - m0, n0, n1 - n0, ps, tok)

    def stagers(self, n, shape, dt):
        return [self.buf(shape, dt) for _ in range(n)]

    def epi_act(self, dst_fn, func=AF.Copy, dt=BF16, nst=4, alt=True, bias_fn=None):
        P = self.P
        st = self.stagers(nst, [512], dt)
        cnt = [0]

        def epi(m0, msz, n0, nsz, ps, tok):
            b = st[cnt[0] % nst]
            use_dve = alt and func == AF.Copy and bias_fn is None and (cnt[0] % 2 == 1)
            cnt[0] += 1
            o = b.ap[0:msz, 0:nsz]
            deps = [tok] + list(b.readers)
            if use_dve:
                t = P.op(DVE, lambda e: e.tensor_copy(out=o, in_=ps), deps=deps)
            elif bias_fn is not None:
                bcol = bias_fn(m0, msz)
                t = P.op(ACT, lambda e: e.activation(out=o, in_=ps, func=func, bias=bcol), deps=deps)
            else:
                t = P.op(ACT, lambda e: e.activation(out=o, in_=ps, func=func), deps=deps)
            b.wrote(t)
            b.store(dst_fn(m0, msz, n0, nsz), src=o)
            return t
        return epi

    def load_const(self, src, shape, dt, eng=POOL):
        b = self.buf(shape, dt)
        b.load(src, eng=eng)
        return b

    def silu_cols(self, cc, scT):
        P = self.P
        a = self.load_const(cc.rearrange("(c p) x -> p c x", p=128), [16, 2], F32)
        o = self.buf([16, 2], BF16)
        t = P.op(ACT, lambda e: e.activation(out=o.ap, in_=a.ap, func=AF.Silu), deps=[a.ready])
        o.wrote(t)
        o.store(scT.rearrange("(c p) x -> p c x", p=128))

    def mod(self, ada_w, ada_b48, scT, modD):
        bias = self.load_const(ada_b48, [48], F32)
        epi = self.epi_act(lambda m0, ms, n0, ns: modD[m0:m0 + ms, n0:n0 + ns], func=AF.Identity, dt=F32,
                           bias_fn=lambda m0, ms: bias.ap[0:ms, m0 // 128:m0 // 128 + 1])
        self.gemm(2048, 6144, 2, lambda a, b: ada_w[:, a:b], lambda a, b: scT[:, a:b], 'R', epi, deps=[bias.ready])

    def mod_cols(self, modD, g16, col):
        P = self.P
        m = self.load_const(modD.rearrange("(j p) x -> p j x", p=128), [48, 2], F32)
        g = self.load_const(g16, [16], F32)
        A = self.buf([16], F32)
        t = P.op(DVE, lambda e: e.tensor_scalar(A.ap, m.ap[:, 16:32, col], 1.0, 1.0, ALU.add, ALU.mult),
                 deps=[m.ready])
        t = P.op(DVE, lambda e: e.tensor_tensor(A.ap, A.ap, g.ap, ALU.mult), deps=[t, g.ready])
        A.wrote(t)
        sh = self.buf([16], F32)
        t2 = P.op(DVE, lambda e: e.tensor_copy(out=sh.ap, in_=m.ap[:, 0:16, col]), deps=[m.ready])
        sh.wrote(t2)
        gt = self.buf([16], F32)
        t3 = P.op(DVE, lambda e: e.tensor_copy(out=gt.ap, in_=m.ap[:, 32:48, col]), deps=[m.ready])
        gt.wrote(t3)
        return A, sh, gt

    def norm(self, xT, Tn, A, sh, hT=None, outF=None, tile=512):
        P = self.P
        ones = self.buf([128], BF16)
        t1 = P.op(DVE, lambda e: e.memset(ones.ap, 1.0))
        ones.wrote(t1)
        epsb = self.buf([1], F32)
        t1 = P.op(DVE, lambda e: e.memset(epsb.ap, EPS))
        epsb.wrote(t1)
        tile = min(tile, Tn)
        xb = [self.buf([16, tile], F32) for _ in range(2)]
        sq = self.buf([16, tile], BF16)
        rs = self.buf([tile], F32)
        tmp = [self.buf([tile], F32) for _ in range(2)]
        odt = F32 if outF is not None else BF16
        ob = [self.buf([16, tile], odt) for _ in range(1 if outF is not None else 2)]
        dst = outF if outF is not None else hT
        assert Tn % tile == 0
        nt = Tn // tile
        xv = xT.rearrange("(c p) t -> p c t", p=128)
        dv = dst.rearrange("(c p) t -> p c t", p=128)
        xb[0].load(xv[:, :, 0:tile])
        for i in range(nt):
            if i + 1 < nt:
                xb[(i + 1) % 2].load(xv[:, :, (i + 1) * tile:(i + 2) * tile])
            x = xb[i % 2]
            o = ob[i % len(ob)]
            t = P.op(ACT, lambda e, x=x: e.activation(out=sq.ap, in_=x.ap, func=AF.Square),
                     deps=[x.ready] + sq.readers)
            sq.wrote(t)
            bi = self.bank()
            ps = self.banks[bi][:, 0:tile]
            for c in range(16):
                tk = P.op(PE, lambda e, c=c, ps=ps: e.matmul(ps, ones.ap, sq.ap[:, c, :], start=(c == 0), stop=(c == 15)),
                          deps=[sq.ready, ones.ready, self.bank_free[bi]] if c == 0 else [], signal=(c == 15))
            sq.read(tk)
            t0_ = P.op(ACT, lambda e, ps=ps: e.activation(out=rs.ap, in_=ps, func=AF.Sqrt, scale=1.0 / 2048.0, bias=epsb.ap),
                       deps=[tk, epsb.ready] + rs.readers)
            t = P.op(DVE, lambda e: e.reciprocal(rs.ap, rs.ap), deps=[t0_])
            rs.wrote(t)
            self.bank_free[bi] = t
            last = []
            for c in range(16):
                tb = tmp[c % 2]
                t = P.op(DVE, lambda e, c=c, x=x, tb=tb: e.scalar_tensor_tensor(
                    out=tb.ap, in0=x.ap[:, c, :], scalar=A.ap[:, c:c + 1], in1=rs.ap, op0=ALU.mult, op1=ALU.mult),
                    deps=[rs.ready, A.ready, x.ready] + tb.readers)
                tb.wrote(t)
                if sh is not None:
                    t2 = P.op(ACT, lambda e, c=c, tb=tb, o=o: e.activation(
                        out=o.ap[:, c, :], in_=tb.ap, func=AF.Identity, bias=sh.ap[:, c:c + 1]),
                        deps=[t, sh.ready] + (o.readers if c == 0 else []))
                else:
                    t2 = P.op(ACT, lambda e, c=c, tb=tb, o=o: e.activation(out=o.ap[:, c, :], in_=tb.ap, func=AF.Copy),
                              deps=[t] + (o.readers if c == 0 else []))
                tb.read(t2)
                last.append(t2)
            x.read(last[-1])
            x.read(t)
            rs.read(t)
            o.wrote(last[-1])
            o.readers = []
            o.store(dv[:, :, i * tile:(i + 1) * tile])

    def ew_mul(self, a, b, out, rows, cols, dt_out=BF16):
        P = self.P
        R = rows // 128
        ct = min(cols, 512)
        rb = min(R, 8)
        av = a.rearrange("(c p) t -> p c t", p=128)
        bv = b.rearrange("(c p) t -> p c t", p=128)
        ov = out.rearrange("(c p) t -> p c t", p=128)
        ab = [self.buf([rb, ct], BF16) for _ in range(2)]
        bb = [self.buf([rb, ct], BF16) for _ in range(2)]
        ob = [self.buf([rb, ct], dt_out) for _ in range(2)]
        i = 0
        for r0 in range(0, R, rb):
            for c0 in range(0, cols, ct):
                A_, B_, O_ = ab[i % 2], bb[i % 2], ob[i % 2]
                A_.load(av[:, r0:r0 + rb, c0:c0 + ct])
                B_.load(bv[:, r0:r0 + rb, c0:c0 + ct], eng=SP)
                t = P.op(DVE if i % 2 == 0 else POOL, lambda e, A_=A_, B_=B_, O_=O_: e.tensor_tensor(O_.ap, A_.ap, B_.ap, ALU.mult),
                         deps=[A_.ready, B_.ready] + O_.readers)
                A_.read(t)
                B_.read(t)
                O_.wrote(t)
                O_.store(ov[:, r0:r0 + rb, c0:c0 + ct])
                i += 1

    def epi_resid(self, xT_in, xT_out, gate, nst=3):
        P = self.P
        xs = [self.buf([512], F32) for _ in range(nst)]
        os_ = [self.buf([512], F32) for _ in range(nst)]
        cnt = [0]

        def epi(m0, msz, n0, nsz, ps, tok):
            xb = xs[cnt[0] % nst]
            ob = os_[cnt[0] % nst]
            cnt[0] += 1
            xb.load(xT_in[m0:m0 + msz, n0:n0 + nsz], dst=xb.ap[0:msz, 0:nsz], eng=SP)
            j = m0 // 128
            t = P.op(DVE, lambda e: e.scalar_tensor_tensor(out=ob.ap[0:msz, 0:nsz], in0=ps, scalar=gate.ap[0:msz, j:j + 1],
                                                           in1=xb.ap[0:msz, 0:nsz], op0=ALU.mult, op1=ALU.add),
                     deps=[tok, xb.ready, gate.ready] + ob.readers)
            xb.read(t)
            ob.wrote(t)
            ob.store(xT_out[m0:m0 + msz, n0:n0 + nsz], src=ob.ap[0:msz, 0:nsz])
            return t
        return epi

    def outproj(self, w_out, yT, xT_in, xT_out, gate, Tn):
        step = min(Tn, 2048)
        for n0 in range(0, Tn, step):
            off0 = self.off
            epi = self.epi_resid(xT_in[:, n0:n0 + step], xT_out[:, n0:n0 + step], gate)
            self.gemm(4096, 2048, step, lambda a, b: w_out[:, a:b], lambda a, b, n0=n0: yT[:, n0 + a:n0 + b], 'R', epi,
                      sblk=(256 if step > 1024 else 512))
            if n0 + step < Tn:
                self.P.barrier()
                self.off = off0
                self.bank_free = [None] * 8


def run_launch(build_fn, in_maps, out_names):
    nc = bass.Bass("TRN2", target_bir_lowering=False)
    with contextlib.ExitStack() as st:
        kb = KB(nc, st, ext_in=list(in_maps[0].keys()), ext_out=out_names)
        kb.in_shapes = {k: (v.shape, v.dtype) for k, v in in_maps[0].items()}
        build_fn(kb)
        toks = kb.P.all_tokens()
        kb.P.wait(SP, toks)
        kb.P.wait(POOL, toks)
        kb.P.emit()
    res = run_bass_kernel_spmd(nc, in_maps, core_ids=list(range(8)))
    return res.results


def IN(kb, name):
    shape, dt = kb.in_shapes[name]
    return kb.D(name, list(shape), BF16 if dt == NPBF else F32)


def col48(v):
    return np.ascontiguousarray(v.reshape(-1, 128).T)


def build_MOD(kb):
    cc = IN(kb, "cc")
    scT = kb.D("scT", [2048, 2], BF16)
    kb.silu_cols(cc, scT)
    kb.new_phase()
    modD = kb.D("modD", [6144, 2], F32)
    kb.mod(IN(kb, "ada_w"), IN(kb, "ada_b48"), scT, modD)


def front(kb, Tn, has_ctx, x_name="xT", h_name="hT"):
    modD = IN(kb, "modD")
    A, sh, gt = kb.mod_cols(modD, IN(kb, "g16"), 0)
    hT = kb.D(h_name, [2048, Tn], BF16)
    kb.norm(IN(kb, x_name), Tn, A, sh, hT=hT, tile=(384 if Tn == 2304 else 512))
    if has_ctx:
        kb.new_phase()
        A, sh, gt = kb.mod_cols(modD, IN(kb, "g16"), 1)
        hcT = kb.D("hcT", [2048, 256], BF16)
        kb.norm(IN(kb, "xcT"), 256, A, sh, hT=hcT, tile=256)
    kb.new_phase()


def build_A(has_ctx):
    def f(kb):
        front(kb, 2048, has_ctx)
        hT = kb.D("hT")
        w = IN(kb, "w_in")
        U = kb.D("U", [2048, 4096], BF16)
        SZT = kb.D("SZT", [4096, 2048], BF16)
        epi = kb.epi_act(lambda m0, ms, n0, ns: U[m0:m0 + ms, n0:n0 + ns])
        kb.gemm(2048, 2048, 4096, lambda a, b: hT[:, a:b], lambda a, b: w[:, a:b], 'L', epi)
        kb.new_phase()
        epi = kb.epi_act(lambda m0, ms, n0, ns: SZT[m0:m0 + ms, n0:n0 + ns], func=AF.Silu)
        kb.gemm(2048, 4096, 2048, lambda a, b: w[:, 4096 + a:4096 + b], lambda a, b: hT[:, a:b], 'R', epi)
        if has_ctx:
            kb.new_phase()
            hcT = kb.D("hcT")
            UC = kb.D("UC", [256, 4096], BF16)
            SZCT = kb.D("SZCT", [4096, 256], BF16)
            epi = kb.epi_act(lambda m0, ms, n0, ns: UC[m0:m0 + ms, n0:n0 + ns])
            kb.gemm(2048, 256, 4096, lambda a, b: hcT[:, a:b], lambda a, b: w[:, a:b], 'L', epi)
            kb.new_phase()
            epi = kb.epi_act(lambda m0, ms, n0, ns: SZCT[m0:m0 + ms, n0:n0 + ns], func=AF.Silu)
            kb.gemm(2048, 4096, 256, lambda a, b: w[:, 4096 + a:4096 + b], lambda a, b: hcT[:, a:b], 'R', epi)
    return f


def dft_consts():
    n2 = np.arange(128)
    ang = 2 * np.pi * np.outer(n2, n2) / 128.0
    sA = 1.0 / math.sqrt(128.0)
    CSa = np.concatenate([np.cos(ang), -np.sin(ang)], 1) * sA
    c = np.arange(256)
    angc = 2 * np.pi * np.outer(c, c) / 256.0
    Cc = np.cos(angc) / 16.0
    Sc = np.sin(angc) / 16.0
    CS256 = np.concatenate([np.cos(angc), -np.sin(angc)], 1) / 16.0
    n1 = np.arange(64)
    k1 = np.arange(64)
    TW = np.zeros((64, 256, 128), np.float32)
    for j in range(64):
        for e in range(2):
            k2 = 2 * j + e
            th = 2 * np.pi * (n1[:, None] * k2 / 8192.0 + np.outer(n1, k1) / 64.0)
            TW[j, e * 64:(e + 1) * 64, e * 64:(e + 1) * 64] = np.cos(th) / 8.0
            TW[j, 128 + e * 64:128 + (e + 1) * 64, e * 64:(e + 1) * 64] = np.sin(th) / 8.0
    bf = lambda a: np.ascontiguousarray(a.astype(np.float32)).astype(NPBF)
    return dict(CSa=bf(CSa), Cc=bf(Cc), Sc=bf(Sc), Scn=bf(-Sc), CS256=bf(CS256), TW=bf(TW))


def build_B(ngroups, has_ctx):
    def f(kb):
        UQ = IN(kb, "UQ")
        CSa = IN(kb, "CSa")
        A1 = kb.D("A1", [256, 65536], BF16)
        epi = kb.epi_act(lambda m0, ms, n0, ns: A1[m0:m0 + ms, n0:n0 + ns])
        kb.gemm(128, 256, 65536, lambda a, b: CSa[:, a:b], lambda a, b: UQ[:, a:b], 'L', epi)
        kb.new_phase()
        WmT = IN(kb, "WmT")
        MM = kb.D("MM", [3, 256, ngroups * 256], BF16)
        for t, nm in enumerate(("Cc", "Sc", "Scn")):
            Cm = IN(kb, nm)
            epi = kb.epi_act(lambda m0, ms, n0, ns, t=t: MM[t, m0:m0 + ms, n0:n0 + ns], nst=3)
            kb.gemm(256, 256, ngroups * 256, lambda a, b, Cm=Cm: Cm[:, a:b], lambda a, b: WmT[:, a:b], 'L', epi)
            kb.new_phase()
        if has_ctx:
            UC = IN(kb, "UC")
            CS256 = IN(kb, "CS256")
            AC = kb.D("AC", [4096, 512], BF16)
            epi = kb.epi_act(lambda m0, ms, n0, ns: AC[m0:m0 + ms, n0:n0 + ns])
            kb.gemm(256, 4096, 512, lambda a, b: UC[:, a:b], lambda a, b: CS256[:, a:b], 'R', epi)
    return f


def build_C(has_ctx):
    def f(kb):
        LB = IN(kb, "LB")
        RB = IN(kb, "RB")
        B1 = kb.D("B1", [4, 512, 8192], BF16)
        for g in range(4):
            epi = kb.epi_act(lambda m0, ms, n0, ns, g=g: B1[g, m0:m0 + ms, n0:n0 + ns])
            kb.gemm(512, 512, 8192, lambda a, b, g=g: RB[g, :, a:b], lambda a, b, g=g: LB[g, :, a:b], 'L', epi)
            kb.new_phase()
        if has_ctx:
            LCc = IN(kb, "LCc")
            RCc = IN(kb, "RCc")
            FCT = kb.D("FCT", [4096, 256], BF16)
            for g in range(16):
                epi = kb.epi_act(lambda m0, ms, n0, ns, g=g: FCT[g * 256 + m0:g * 256 + m0 + ms, n0:n0 + ns], nst=2)
                kb.gemm(512, 256, 256, lambda a, b, g=g: LCc[g, :, a:b], lambda a, b, g=g: RCc[g, :, a:b], 'R', epi)
                kb.new_phase()
    return f


def build_Dd(kb):
    P = kb.P
    LC = IN(kb, "LC")
    TW = IN(kb, "TW")
    FQ = kb.D("FQ", [64, 128, 1024], BF16)
    tw = kb.buf([64, 2, 128], BF16)
    twv = TW.rearrange("j (c p) n -> p j c n", p=128)
    tw_toks = []
    for j0 in range(0, 64, 16):
        tw_toks.append(P.dma(POOL, tw.sem, tw.ap[:, j0:j0 + 16], twv[:, j0:j0 + 16]))
    lc = [kb.buf([2, 1024], BF16) for _ in range(4)]
    ob = [kb.buf([1024], BF16) for _ in range(4)]
    n = 0
    for j in range(64):
        b = lc[j % 4]
        o = ob[j % 4]
        b.load(LC[j].rearrange("(c p) n -> p c n", p=128), eng=(SP if j % 2 else POOL))
        etoks = []
        tok = None
        for nt in range(2):
            bi = kb.bank()
            ps = kb.banks[bi][:, 0:512]
            for c in range(2):
                tok = P.op(PE, lambda e, ps=ps, j=j, c=c, b=b, nt=nt: e.matmul(ps, tw.ap[:, j, c, :], b.ap[:, c, nt * 512:(nt + 1) * 512],
                                                                               start=(c == 0), stop=(c == 1)),
                           deps=([kb.bank_free[bi], b.ready] + tw_toks) if c == 0 else [], signal=(c == 1))
            dst = o.ap[:, nt * 512:(nt + 1) * 512]
            deps = [tok] + list(o.readers)
            if n % 2 == 0:
                t = P.op(ACT, lambda e, dst=dst, ps=ps: e.activation(out=dst, in_=ps, func=AF.Copy), deps=deps)
            else:
                t = P.op(DVE, lambda e, dst=dst, ps=ps: e.tensor_copy(out=dst, in_=ps), deps=deps)
            n += 1
            etoks.append(t)
            kb.bank_free[bi] = t
        b.read(tok)
        o.readers = []
        t_st = P.dma(SP, o.sem, FQ[j], o.ap, deps=etoks)
        o.readers.append(t_st)


def build_E(has_ctx, final):
    def f(kb):
        FT = IN(kb, "FT")
        SZT = IN(kb, "SZT")
        YT = kb.D("YT", [4096, 2048], BF16)
        kb.ew_mul(FT, SZT, YT, 4096, 2048)
        kb.new_phase()
        modD = IN(kb, "modD")
        A, sh, gt = kb.mod_cols(modD, IN(kb, "g16"), 0)
        xo = kb.D("xoT", [2048, 2048], F32)
        kb.outproj(IN(kb, "w_out"), YT, IN(kb, "xT"), xo, gt, 2048)
        if has_ctx:
            kb.new_phase()
            YCT = kb.D("YCT", [4096, 256], BF16)
            kb.ew_mul(IN(kb, "FCT"), IN(kb, "SZCT"), YCT, 4096, 256)
            kb.new_phase()
            A, sh, gtc = kb.mod_cols(modD, IN(kb, "g16"), 1)
            xco = kb.D("xcoT", [2048, 256], F32)
            kb.outproj(IN(kb, "w_out"), YCT, IN(kb, "xcT"), xco, gtc, 256)
        if final:
            kb.new_phase()
            fg = kb.load_const(IN(kb, "fg16"), [16], F32)
            outT = kb.D("outT", [2048, 2048], F32)
            kb.norm(xo, 2048, fg, None, outF=outT)
    return f


def T(a):
    return np.ascontiguousarray(np.asarray(a).T)


_CONSTS = {}


def consts():
    if not _CONSTS:
        _CONSTS.update(dft_consts())
    return _CONSTS


def run_mods(c, c_ctx, ada_w, ada_b):
    in_maps = []
    for core in range(8):
        b, i = core // 4, core % 4
        in_maps.append({"cc": np.ascontiguousarray(np.stack([c[b], c_ctx], 1)).astype(np.float32),
                        "ada_w": np.ascontiguousarray(ada_w[i]), "ada_b48": col48(ada_b[i])})
    res = run_launch(build_MOD, in_maps, ["modD"])
    return [[np.asarray(res[b * 4 + i]["modD"]) for b in range(2)] for i in range(4)]


def fnet_layer(xT, xcT, mods_i, g16, w_in, w_mix, w_out_i, has_ctx, final_g16=None):
    C = consts()
    in_maps = []
    for core in range(8):
        b = core // 4
        m = {"modD": mods_i[b], "g16": g16, "xT": xT[core], "w_in": w_in}
        if has_ctx:
            m["xcT"] = xcT[b]
        in_maps.append(m)
    outs = ["U", "SZT"] + (["UC", "SZCT"] if has_ctx else [])
    rA = run_launch(build_A(has_ctx), in_maps, outs)
    in_maps = []
    for core in range(8):
        b, q = core // 4, core % 4
        Ufull = np.concatenate([np.asarray(rA[b * 4 + s]["U"]) for s in range(4)], 0)
        UQ = np.ascontiguousarray(Ufull[:, q * 1024:(q + 1) * 1024]).reshape(128, 65536)
        m = {"UQ": UQ, "CSa": C["CSa"], "Cc": C["Cc"], "Sc": C["Sc"], "Scn": C["Scn"]}
        if has_ctx:
            m["WmT"] = np.ascontiguousarray(w_mix.transpose(1, 0, 2).reshape(256, 16 * 256))
            m["UC"] = np.asarray(rA[core]["UC"])
            m["CS256"] = C["CS256"]
        else:
            m["WmT"] = np.ascontiguousarray(w_mix[q * 4:(q + 1) * 4].transpose(1, 0, 2).reshape(256, 4 * 256))
        in_maps.append(m)
    ng = 16 if has_ctx else 4
    rB = run_launch(build_B(ng, has_ctx), in_maps, ["A1", "MM"] + (["AC"] if has_ctx else []))
    in_maps = []
    for core in range(8):
        q = core % 4
        A1 = np.asarray(rB[core]["A1"]).reshape(2, 128, 64, 4, 256)
        LB = np.ascontiguousarray(A1.transpose(3, 0, 4, 1, 2)).reshape(4, 512, 8192)
        MM = np.ascontiguousarray(np.asarray(rB[core]["MM"]).reshape(3, 256, ng, 256).transpose(0, 2, 1, 3))
        g0 = q * 4 if has_ctx else 0
        RB = np.zeros((4, 512, 512), NPBF)
        for gl in range(4):
            M1, M2, M2n = MM[0, g0 + gl], MM[1, g0 + gl], MM[2, g0 + gl]
            RB[gl, 0:256, 0:256] = M1
            RB[gl, 0:256, 256:512] = M2n
            RB[gl, 256:512, 0:256] = M2
            RB[gl, 256:512, 256:512] = M1
        m = {"LB": LB, "RB": RB}
        if has_ctx:
            AC = np.asarray(rB[core]["AC"]).reshape(16, 256, 2, 256)
            m["RCc"] = np.ascontiguousarray(AC.transpose(0, 2, 1, 3)).reshape(16, 512, 256)
            m["LCc"] = np.ascontiguousarray(np.concatenate([MM[0], MM[1]], 1))
        in_maps.append(m)
    rC = run_launch(build_C(has_ctx), in_maps, ["B1"] + (["FCT"] if has_ctx else []))
    in_maps = []
    for core in range(8):
        B1 = np.asarray(rC[core]["B1"]).reshape(4, 2, 256, 64, 2, 64)
        LC = np.ascontiguousarray(B1.transpose(3, 1, 4, 5, 0, 2)).reshape(64, 256, 1024)
        in_maps.append({"LC": LC, "TW": C["TW"]})
    rD = run_launch(build_Dd, in_maps, ["FQ"])
    Fq = []
    for core in range(8):
        FQ = np.asarray(rD[core]["FQ"]).reshape(64, 2, 64, 1024)
        Fq.append(np.ascontiguousarray(FQ.transpose(2, 0, 1, 3)).reshape(8192, 1024))
    in_maps = []
    for core in range(8):
        b, q = core // 4, core % 4
        FT = np.ascontiguousarray(np.concatenate([Fq[b * 4 + s][q * 2048:(q + 1) * 2048] for s in range(4)], 1).T)
        m = {"FT": FT, "SZT": np.asarray(rA[core]["SZT"]), "modD": mods_i[b], "g16": g16, "w_out": w_out_i,
             "xT": xT[core]}
        if has_ctx:
            m.update({"FCT": np.asarray(rC[core]["FCT"]), "SZCT": np.asarray(rA[core]["SZCT"]), "xcT": xcT[b]})
        if final_g16 is not None:
            m["fg16"] = final_g16
        in_maps.append(m)
    outs = ["xoT"] + (["xcoT"] if has_ctx else []) + (["outT"] if final_g16 is not None else [])
    rE = run_launch(build_E(has_ctx, final_g16 is not None), in_maps, outs)
    if final_g16 is not None:
        return [np.asarray(rE[k]["outT"]) for k in range(8)]
    new_x = [np.asarray(rE[k]["xoT"]) for k in range(8)]
    new_xc = [np.asarray(rE[b * 4]["xcoT"]) for b in range(2)] if has_ctx else xcT
    return new_x, new_xc


def build_F1(kb):
    front(kb, 2304, True)
    hT = kb.D("hT")
    hcT = kb.D("hcT")
    w = IN(kb, "w_in")
    QT0 = kb.D("QT0", [4096, 2048], BF16)
    KT0 = kb.D("KT0", [512, 2304], BF16)
    V = kb.D("V", [2304, 512], BF16)
    SZT = kb.D("SZT", [4096, 2048], BF16)
    KCT = kb.D("KCT", [512, 256], BF16)
    VC = kb.D("VC", [256, 512], BF16)
    epi = kb.epi_act(lambda m0, ms, n0, ns: QT0[m0:m0 + ms, n0:n0 + ns])
    kb.gemm(2048, 4096, 2048, lambda a, b: w[:, a:b], lambda a, b: hT[:, 128 + a:128 + b], 'R', epi)
    kb.new_phase()
    epi = kb.epi_act(lambda m0, ms, n0, ns: SZT[m0:m0 + ms, n0:n0 + ns], func=AF.Silu)
    kb.gemm(2048, 4096, 2048, lambda a, b: w[:, 5120 + a:5120 + b], lambda a, b: hT[:, 128 + a:128 + b], 'R', epi)
    kb.new_phase()
    epi = kb.epi_act(lambda m0, ms, n0, ns: KT0[m0:m0 + ms, n0:n0 + ns])
    kb.gemm(2048, 512, 2304, lambda a, b: w[:, 4096 + a:4096 + b], lambda a, b: hT[:, a:b], 'R', epi)
    kb.new_phase()
    epi = kb.epi_act(lambda m0, ms, n0, ns: V[m0:m0 + ms, n0:n0 + ns])
    kb.gemm(2048, 2304, 512, lambda a, b: hT[:, a:b], lambda a, b: w[:, 4608 + a:4608 + b], 'L', epi)
    kb.new_phase()
    epi = kb.epi_act(lambda m0, ms, n0, ns: KCT[m0:m0 + ms, n0:n0 + ns])
    kb.gemm(2048, 512, 256, lambda a, b: w[:, 4096 + a:4096 + b], lambda a, b: hcT[:, a:b], 'R', epi)
    kb.new_phase()
    epi = kb.epi_act(lambda m0, ms, n0, ns: VC[m0:m0 + ms, n0:n0 + ns])
    kb.gemm(2048, 256, 512, lambda a, b: hcT[:, a:b], lambda a, b: w[:, 4608 + a:4608 + b], 'L', epi)


def rope_tables(pos):
    rows = (pos // 64).astype(np.float64)
    cols = (pos % 64).astype(np.float64)
    inv = 10000.0 ** (-np.arange(16) / 16.0)
    cosT = np.zeros((64, len(pos)), np.float32)
    ssin = np.zeros((64, len(pos)), np.float32)
    for d in range(64):
        half, wv = d // 32, d % 32
        f, part = wv % 16, wv // 16
        ang = (rows if half == 0 else cols) * inv[f]
        cosT[d] = np.cos(ang)
        ssin[d] = np.sin(ang) * (-1.0 if part == 0 else 1.0)
    return np.ascontiguousarray(np.tile(cosT, (2, 1))), np.ascontiguousarray(np.tile(ssin, (2, 1)))


def rope_perm_rows(nheads):
    idx = np.arange(nheads * 64).reshape(nheads, 64)
    d = np.arange(64)
    wv = d % 32
    partner = np.where(wv // 16 == 0, d + 16, d - 16)
    return idx[:, partner].reshape(-1)


def build_G1a(kb):
    P = kb.P
    for (nm, rows, Tn) in (("Q", 4096, 2048), ("K", 512, 2304)):
        a = IN(kb, nm + "T0")
        bp = IN(kb, nm + "T0p")
        cosD = IN(kb, "cos" + nm)
        sinD = IN(kb, "ssin" + nm)
        out = kb.D(nm + "Tr", [rows, Tn], BF16)
        cs = kb.load_const(cosD, [Tn], F32)
        sn = kb.load_const(sinD, [Tn], F32)
        ab = [kb.buf([Tn], BF16) for _ in range(2)]
        bb = [kb.buf([Tn], BF16) for _ in range(2)]
        t1 = [kb.buf([Tn], F32) for _ in range(2)]
        t2 = [kb.buf([Tn], F32) for _ in range(2)]
        ob = [kb.buf([Tn], BF16) for _ in range(2)]
        for ch in range(rows // 128):
            i = ch % 2
            A_, B_, T1, T2, O_ = ab[i], bb[i], t1[i], t2[i], ob[i]
            A_.load(a[ch * 128:(ch + 1) * 128, :])
            B_.load(bp[ch * 128:(ch + 1) * 128, :], eng=SP)
            ta = P.op(DVE, lambda e, A_=A_, T1=T1, cs=cs: e.tensor_tensor(T1.ap, A_.ap, cs.ap, ALU.mult),
                      deps=[A_.ready, cs.ready] + T1.readers)
            T1.wrote(ta)
            A_.read(ta)
            tb = P.op(POOL, lambda e, B_=B_, T2=T2, sn=sn: e.tensor_tensor(T2.ap, B_.ap, sn.ap, ALU.mult),
                      deps=[B_.ready, sn.ready] + T2.readers)
            T2.wrote(tb)
            B_.read(tb)
            tc = P.op(DVE, lambda e, T1=T1, T2=T2, O_=O_: e.tensor_tensor(O_.ap, T1.ap, T2.ap, ALU.add),
                      deps=[ta, tb] + O_.readers)
            T1.read(tc)
            T2.read(tc)
            O_.wrote(tc)
            O_.store(out[ch * 128:(ch + 1) * 128, :])
        kb.new_phase()


def attn_core(kb, QTr, KTr, V, KCT, VC, SZT, YT, maskD, sinkD):
    P = kb.P
    Qv = QTr.rearrange("(h g d) t -> h d g t", g=8, d=64)
    Sv = SZT.rearrange("(h g d) t -> h d g t", g=8, d=64)
    Yv = YT.rearrange("(h g d) t -> h d g t", g=8, d=64)
    masks = kb.load_const(maskD.rearrange("k p n -> p k n"), [4, 512], BF16)
    srow = kb.buf([8192], F32)
    srow.load(sinkD, dst=srow.ap[0:1])
    esrow = kb.buf([8192], BF16)
    t = P.op(ACT, lambda e: e.activation(out=esrow.ap[0:1], in_=srow.ap[0:1], func=AF.Exp), deps=[srow.ready])
    esrow.wrote(t)
    ones = kb.buf([64], BF16)
    t = P.op(DVE, lambda e: e.memset(ones.ap, 1.0))
    ones.wrote(t)
    Qh = [kb.buf([8, 2048], BF16) for _ in range(2)]
    Kh = [kb.buf([2560], BF16) for _ in range(2)]
    Vh = [kb.buf([20, 64], BF16) for _ in range(2)]
    Pt = [kb.buf([5, 512], BF16) for _ in range(3)]
    rD = [kb.buf([512], F32) for _ in range(2)]
    yt = [kb.buf([512], F32) for _ in range(2)]
    szb = [kb.buf([8, 128], BF16) for _ in range(3)]
    yb = [kb.buf([8, 128], BF16) for _ in range(3)]
    kv = {}

    def load_hk(hk):
        Q_, K_, V_ = Qh[hk % 2], Kh[hk % 2], Vh[hk % 2]
        Q_.load(Qv[hk], dst=Q_.ap[0:64])
        K_.load(KTr[hk * 64:(hk + 1) * 64, :], dst=K_.ap[0:64, 0:2304], eng=SP)
        tk2 = K_.ready
        K_.readers = []
        t2 = P.dma(SP, K_.sem, K_.ap[0:64, 2304:2560], KCT[hk * 64:(hk + 1) * 64, :])
        K_.ready = t2
        V_.load(V[:, hk * 64:(hk + 1) * 64].rearrange("(b p) d -> p b d", p=128), dst=V_.ap[:, 0:18, :], eng=SP)
        tv1 = V_.ready
        V_.readers = []
        tv2 = P.dma(SP, V_.sem, V_.ap[:, 18:20, :], VC[:, hk * 64:(hk + 1) * 64].rearrange("(b p) d -> p b d", p=128))
        V_.ready = tv2
        kv[hk] = ([tk2, t2], [tv1, tv2])

    steps = [(hk, b, half) for hk in range(8) for b in range(16) for half in range(2)]
    st = {}

    def emit_S(i):
        hk, b, half = steps[i]
        if b == 0 and half == 0:
            if hk == 0:
                load_hk(0)
            if hk + 1 < 8:
                load_hk(hk + 1)
        Q_, K_ = Qh[hk % 2], Kh[hk % 2]
        kdeps = kv[hk][0]
        P_ = Pt[i % 3]
        rhs = Q_.ap[0:64, 4 * half:4 * half + 4, b * 128:(b + 1) * 128]
        blks = [b, b + 1, b + 2, 18, 19]
        pt_toks = []
        tm = None
        for j, blk in enumerate(blks):
            bi = kb.bank()
            ps = kb.banks[bi][:, 0:512]
            kcols = K_.ap[0:64, blk * 128:(blk + 1) * 128]
            tm = P.op(PE, lambda e, ps=ps, kcols=kcols, rhs=rhs: e.matmul(ps, kcols, rhs, start=True, stop=True),
                      deps=[kb.bank_free[bi], Q_.ready] + kdeps)
            te = P.op(ACT, lambda e, ps=ps, P_=P_, j=j: e.activation(out=P_.ap[:, j, :], in_=ps, func=AF.Exp, scale=0.125),
                      deps=[tm] + (P_.readers if j == 0 else []))
            kb.bank_free[bi] = te
            if j in (0, 2):
                mi = (0 if j == 0 else 1)
                if j == 0 and b == 0:
                    mi = 2
                if j == 2 and b == 15:
                    mi = 3
                te = P.op(DVE, lambda e, P_=P_, j=j, mi=mi: e.tensor_tensor(P_.ap[:, j, :], P_.ap[:, j, :], masks.ap[:, mi, :], ALU.mult),
                          deps=[te, masks.ready])
            pt_toks.append(te)
        Q_.read(tm)
        K_.read(tm)
        P_.wrote(pt_toks[-1])
        st[i] = pt_toks

    def emit_PV(i):
        hk, b, half = steps[i]
        V_ = Vh[hk % 2]
        vdeps = kv[hk][1]
        P_ = Pt[i % 3]
        pt_toks = st.pop(i)
        blks = [b, b + 1, b + 2, 18, 19]
        it = (hk * 16 + b) % 3
        SZ_, Y_ = szb[it], yb[it]
        if half == 0:
            SZ_.load(Sv[hk][:, :, b * 128:(b + 1) * 128], dst=SZ_.ap[0:64], eng=SP)
        bo, bd = kb.bank(), kb.bank()
        pso = kb.banks[bo][0:64, 0:512]
        psd = kb.banks[bd][0:64, 0:512]
        to = None
        for j, blk in enumerate(blks):
            to = P.op(PE, lambda e, pso=pso, V_=V_, blk=blk, P_=P_, j=j: e.matmul(pso, V_.ap[:, blk, :], P_.ap[:, j, :], start=(j == 0), stop=(j == 4)),
                      deps=(pt_toks + vdeps + [kb.bank_free[bo]]) if j == 0 else [], signal=(j == 4))
        for j in range(5):
            P.op(PE, lambda e, psd=psd, P_=P_, j=j: e.matmul(psd, ones.ap, P_.ap[:, j, :], start=(j == 0), stop=False),
                 deps=[kb.bank_free[bd], ones.ready] if j == 0 else [], signal=False)
        h0 = hk * 8 + 4 * half
        td = P.op(PE, lambda e, psd=psd, h0=h0: e.matmul(psd, ones.ap[0:1, :], esrow.ap[0:1, h0 * 128:h0 * 128 + 512], start=False, stop=True),
                  deps=[esrow.ready])
        P_.read(td)
        V_.read(td)
        R_, T_ = rD[half], yt[half]
        tr = P.op(DVE, lambda e, psd=psd, R_=R_: e.reciprocal(R_.ap[0:64], psd), deps=[td] + R_.readers)
        R_.wrote(tr)
        kb.bank_free[bd] = tr
        ty = P.op(DVE, lambda e, pso=pso, R_=R_, T_=T_: e.tensor_tensor(T_.ap[0:64], pso, R_.ap[0:64], ALU.mult),
                  deps=[to, tr] + T_.readers)
        T_.wrote(ty)
        R_.read(ty)
        kb.bank_free[bo] = ty
        tz = P.op(POOL, lambda e, T_=T_, Y_=Y_, SZ_=SZ_, half=half: e.tensor_tensor(
            Y_.ap[0:64, 4 * half:4 * half + 4, :], T_.ap[0:64].rearrange("p (g q) -> p g q", g=4),
            SZ_.ap[0:64, 4 * half:4 * half + 4, :], ALU.mult),
            deps=[ty, SZ_.ready] + (Y_.readers if half == 0 else []))
        T_.read(tz)
        if half == 1:
            SZ_.read(tz)
            Y_.wrote(tz)
            Y_.store(Yv[hk][:, :, b * 128:(b + 1) * 128], src=Y_.ap[0:64])

    n = len(steps)
    emit_S(0)
    for i in range(n):
        if i + 1 < n:
            emit_S(i + 1)
        emit_PV(i)


def build_G1b(kb):
    build_G1a(kb)
    YT = kb.D("YT", [4096, 2048], BF16)
    attn_core(kb, kb.D("QTr"), kb.D("KTr"), IN(kb, "V"), IN(kb, "KCT"), IN(kb, "VC"), IN(kb, "SZT"), YT,
              IN(kb, "maskx"), IN(kb, "sinkrep"))
    kb.new_phase()
    A, sh, gt = kb.mod_cols(IN(kb, "modD"), IN(kb, "g16"), 0)
    xo = kb.D("xoT", [2048, 2048], F32)
    kb.outproj(IN(kb, "w_out"), YT, IN(kb, "xT"), xo, gt, 2048)


def attn_layer(xT, xcT, mods_i, g16, w_in, sink, w_out_i):
    in_maps = []
    for core in range(8):
        b, q = core // 4, core % 4
        left = xT[core - 1][:, -128:] if q > 0 else np.zeros((2048, 128), np.float32)
        right = xT[core + 1][:, :128] if q < 3 else np.zeros((2048, 128), np.float32)
        xh = np.ascontiguousarray(np.concatenate([left, xT[core], right], 1))
        in_maps.append({"modD": mods_i[b], "g16": g16, "xT": xh, "xcT": xcT[b], "w_in": w_in})
    rF = run_launch(build_F1, in_maps, ["QT0", "KT0", "V", "SZT", "KCT", "VC"])
    pq, pk = rope_perm_rows(64), rope_perm_rows(8)
    in_maps = []
    for core in range(8):
        q = core % 4
        cq, sq = rope_tables(q * 2048 + np.arange(2048))
        ck, sk = rope_tables(np.abs(q * 2048 - 128 + np.arange(2304)))
        QT0 = np.asarray(rF[core]["QT0"])
        KT0 = np.asarray(rF[core]["KT0"])
        in_maps.append({"QT0": QT0, "QT0p": np.ascontiguousarray(QT0[pq]), "KT0": KT0, "KT0p": np.ascontiguousarray(KT0[pk]),
                        "cosQ": cq, "ssinQ": sq, "cosK": ck, "ssinK": sk})
    rope_maps = in_maps
    kj = np.arange(128)[:, None]
    qi = np.arange(128)[None, :]
    mprev = np.tile((kj >= qi).astype(np.float32), (1, 4))
    mnext = np.tile((kj <= qi).astype(np.float32), (1, 4))
    zero = np.zeros_like(mprev)
    in_maps = []
    for core in range(8):
        b, q = core // 4, core % 4
        maskx = np.stack([mprev, mnext, zero if q == 0 else mprev, zero if q == 3 else mnext], 0).astype(NPBF)
        in_maps.append({**rope_maps[core], "V": np.asarray(rF[core]["V"]), "KCT": np.asarray(rF[core]["KCT"]), "VC": np.asarray(rF[core]["VC"]),
                        "SZT": np.asarray(rF[core]["SZT"]), "maskx": maskx,
                        "sinkrep": np.ascontiguousarray(np.repeat(sink.astype(np.float32), 128)[None, :]),
                        "modD": mods_i[b], "g16": g16, "w_out": w_out_i, "xT": xT[core]})
    rH = run_launch(build_G1b, in_maps, ["xoT"])
    return [np.asarray(rH[k]["xoT"]) for k in range(8)]


AXX = mybir.AxisListType.X


def build_H2(kb):
    P = kb.P
    front(kb, 2048, False)
    hT = kb.D("hT")
    w = IN(kb, "w_in")
    U = kb.D("U", [2048, 4096], BF16)
    Vf = kb.D("Vf", [2048, 4096], F32)
    SZ = kb.D("SZ", [2048, 4096], BF16)
    e_u = kb.epi_act(lambda m0, ms, n0, ns: U[m0:m0 + ms, n0:n0 + ns], func=AF.Gelu_apprx_tanh, nst=3)
    e_v = kb.epi_act(lambda m0, ms, n0, ns: Vf[m0:m0 + ms, n0 - 4096:n0 - 4096 + ns], func=AF.Gelu_apprx_tanh, dt=F32, nst=3)
    e_z = kb.epi_act(lambda m0, ms, n0, ns: SZ[m0:m0 + ms, n0 - 8192:n0 - 8192 + ns], func=AF.Silu, nst=3)

    def epi(m0, ms, n0, ns, ps, tok):
        return (e_u if n0 < 4096 else (e_v if n0 < 8192 else e_z))(m0, ms, n0, ns, ps, tok)
    kb.gemm(2048, 2048, 12288, lambda a, b: hT[:, a:b], lambda a, b: w[:, a:b], 'L', epi)
    kb.new_phase()
    VN = kb.D("VN", [2048, 4096], BF16)
    G = kb.load_const(IN(kb, "lnG"), [4096], F32)
    Bt = kb.load_const(IN(kb, "lnB"), [4096], F32, eng=SP)
    epsb = kb.buf([1], F32)
    t = P.op(DVE, lambda e: e.memset(epsb.ap, EPS))
    epsb.wrote(t)
    vb = [kb.buf([4096], F32) for _ in range(2)]
    sq = kb.buf([4096], F32)
    vh = kb.buf([4096], F32)
    ob = [kb.buf([4096], BF16) for _ in range(2)]
    st = [kb.buf([8], F32) for _ in range(2)]
    vb[0].load(Vf[0:128, :])
    for i in range(16):
        if i + 1 < 16:
            vb[(i + 1) % 2].load(Vf[(i + 1) * 128:(i + 2) * 128, :])
        v, o, s = vb[i % 2], ob[i % 2], st[i % 2]
        t_sq = P.op(ACT, lambda e, v=v: e.activation(out=sq.ap, in_=v.ap, func=AF.Square), deps=[v.ready] + sq.readers)
        sq.wrote(t_sq)
        t1 = P.op(DVE, lambda e, v=v, s=s: e.reduce_sum(out=s.ap[:, 0:1], in_=v.ap, axis=AXX), deps=[v.ready] + s.readers)
        t2 = P.op(DVE, lambda e, s=s: e.reduce_sum(out=s.ap[:, 1:2], in_=sq.ap, axis=AXX), deps=[t_sq, t1])
        sq.read(t2)
        t3 = P.op(DVE, lambda e, s=s: e.tensor_scalar(s.ap[:, 2:3], s.ap[:, 0:1], 1.0 / 4096.0, None, ALU.mult), deps=[t2])
        t4 = P.op(DVE, lambda e, s=s: e.tensor_tensor(s.ap[:, 3:4], s.ap[:, 2:3], s.ap[:, 2:3], ALU.mult), deps=[t3])
        t5 = P.op(DVE, lambda e, s=s: e.scalar_tensor_tensor(out=s.ap[:, 4:5], in0=s.ap[:, 1:2], scalar=1.0 / 4096.0,
                                                             in1=s.ap[:, 3:4], op0=ALU.mult, op1=ALU.subtract), deps=[t4])
        t6 = P.op(ACT, lambda e, s=s: e.activation(out=s.ap[:, 5:6], in_=s.ap[:, 4:5], func=AF.Sqrt, bias=epsb.ap), deps=[t5, epsb.ready])
        t7 = P.op(DVE, lambda e, s=s: e.reciprocal(s.ap[:, 5:6], s.ap[:, 5:6]), deps=[t6])
        t8 = P.op(DVE, lambda e, s=s: e.scalar_tensor_tensor(out=s.ap[:, 6:7], in0=s.ap[:, 2:3], scalar=-1.0,
                                                             in1=s.ap[:, 5:6], op0=ALU.mult, op1=ALU.mult), deps=[t7])
        t9 = P.op(ACT, lambda e, v=v, s=s: e.activation(out=vh.ap, in_=v.ap, func=AF.Identity, scale=s.ap[:, 5:6], bias=s.ap[:, 6:7]),
                  deps=[t8] + vh.readers)
        vh.wrote(t9)
        v.read(t9)
        v.read(t2)
        t10 = P.op(DVE, lambda e: e.tensor_tensor(vh.ap, vh.ap, G.ap, ALU.mult), deps=[t9, G.ready])
        t11 = P.op(DVE, lambda e, o=o: e.tensor_tensor(o.ap, vh.ap, Bt.ap, ALU.add), deps=[t10, Bt.ready] + o.readers)
        vh.read(t11)
        s.read(t11)
        o.wrote(t11)
        o.store(VN[i * 128:(i + 1) * 128, :])
    kb.new_phase()
    Y = kb.D("Y", [2048, 4096], BF16)
    WsT = IN(kb, "WsT")
    bs = kb.load_const(IN(kb, "bsT"), [16], F32)

    def view(D_, g, kk):
        return D_.rearrange("(k t) (g c) -> g t k c", t=128, c=256)[g][:, kk:kk + 2, :]
    wt = [kb.buf([128], BF16) for _ in range(2)]
    rb = [kb.buf([2, 256], BF16) for _ in range(2)]
    ub = [kb.buf([2, 256], BF16) for _ in range(2)]
    zb = [kb.buf([2, 256], BF16) for _ in range(2)]
    sb_ = [kb.buf([512], F32) for _ in range(2)]
    yb = [kb.buf([2, 256], BF16) for _ in range(2)]
    it = 0
    for g in range(16):
        W_ = wt[g % 2]
        W_.load(WsT[g])
        for kk in range(0, 16, 2):
            i = it % 2
            it += 1
            R_, U_, Z_, S_, Y_ = rb[i], ub[i], zb[i], sb_[i], yb[i]
            R_.load(view(VN, g, kk))
            U_.load(view(U, g, kk), eng=SP)
            Z_.load(view(SZ, g, kk), eng=SP)
            bi = kb.bank()
            ps = kb.banks[bi][:, 0:512]
            tm = P.op(PE, lambda e, ps=ps, W_=W_, R_=R_: e.matmul(ps, W_.ap, R_.ap.rearrange("p k c -> p (k c)"), start=True, stop=True),
                      deps=[W_.ready, R_.ready, kb.bank_free[bi]])
            W_.read(tm)
            R_.read(tm)
            ts = P.op(ACT, lambda e, ps=ps, S_=S_, g=g: e.activation(out=S_.ap, in_=ps, func=AF.Identity, bias=bs.ap[:, g:g + 1]),
                      deps=[tm, bs.ready] + S_.readers)
            kb.bank_free[bi] = ts
            S_.wrote(ts)
            ta = P.op(DVE, lambda e, S_=S_, U_=U_: e.tensor_tensor(S_.ap, S_.ap, U_.ap.rearrange("p k c -> p (k c)"), ALU.mult),
                      deps=[ts, U_.ready])
            U_.read(ta)
            tb = P.op(DVE, lambda e, S_=S_, Z_=Z_, Y_=Y_: e.tensor_tensor(Y_.ap.rearrange("p k c -> p (k c)"), S_.ap,
                                                                         Z_.ap.rearrange("p k c -> p (k c)"), ALU.mult),
                      deps=[ta, Z_.ready] + Y_.readers)
            Z_.read(tb)
            S_.read(tb)
            Y_.wrote(tb)
            Y_.store(view(Y, g, kk))


def build_I(final):
    def f(kb):
        A, sh, gt = kb.mod_cols(IN(kb, "modD"), IN(kb, "g16"), 0)
        xo = kb.D("xoT", [2048, 2048], F32)
        kb.outproj(IN(kb, "w_out"), IN(kb, "YT"), IN(kb, "xT"), xo, gt, 2048)
    return f


def gmlp_layer(xT, mods_i, g16, w_in, w_s, b_s, ln_g, ln_b, w_out_i):
    in_maps = []
    for core in range(8):
        b = core // 4
        in_maps.append({"modD": mods_i[b], "g16": g16, "xT": xT[core], "w_in": w_in,
                        "lnG": np.ascontiguousarray(np.broadcast_to(ln_g[None, :], (128, 4096))).astype(np.float32),
                        "lnB": np.ascontiguousarray(np.broadcast_to(ln_b[None, :], (128, 4096))).astype(np.float32),
                        "WsT": np.ascontiguousarray(w_s.transpose(0, 2, 1)), "bsT": T(b_s)})
    rH = run_launch(build_H2, in_maps, ["Y"])
    in_maps = []
    for core in range(8):
        b = core // 4
        in_maps.append({"YT": T(rH[core]["Y"]), "modD": mods_i[b], "g16": g16, "w_out": w_out_i, "xT": xT[core]})
    rI = run_launch(build_I(False), in_maps, ["xoT"])
    return [np.asarray(rI[k]["xoT"]) for k in range(8)]


def kernel(x, c, ctx, c_ctx, norm_g, ada_w, ada_b, w_out, fnet_w_in, fnet_w_mix, attn_w_in, attn_sink,
           gmlp_w_in, gmlp_w_s, gmlp_b_s, gmlp_ln_g, gmlp_ln_b, final_g):
    f32 = lambda a: np.asarray(a, np.float32)
    x, c, ctx, c_ctx, norm_g, ada_w, ada_b, w_out = map(f32, (x, c, ctx, c_ctx, norm_g, ada_w, ada_b, w_out))
    fnet_w_in, fnet_w_mix, attn_w_in, attn_sink = map(f32, (fnet_w_in, fnet_w_mix, attn_w_in, attn_sink))
    gmlp_w_in, gmlp_w_s, gmlp_b_s, gmlp_ln_g, gmlp_ln_b, final_g = map(
        f32, (gmlp_w_in, gmlp_w_s, gmlp_b_s, gmlp_ln_g, gmlp_ln_b, final_g))
    mods = run_mods(c, c_ctx, ada_w, ada_b)
    xT = [T(x[k // 4, (k % 4) * 2048:(k % 4 + 1) * 2048]) for k in range(8)]
    xcT = [T(ctx[b]) for b in range(2)]
    xT, xcT = fnet_layer(xT, xcT, mods[0], col48(norm_g[0]), fnet_w_in[0], fnet_w_mix[0], w_out[0], True)
    xT = attn_layer(xT, xcT, mods[1], col48(norm_g[1]), attn_w_in[0], attn_sink[0], w_out[1])
    xT = gmlp_layer(xT, mods[2], col48(norm_g[2]), gmlp_w_in[0], gmlp_w_s[0], gmlp_b_s[0], gmlp_ln_g[0], gmlp_ln_b[0], w_out[2])
    oT = fnet_layer(xT, None, mods[3], col48(norm_g[3]), fnet_w_in[1], fnet_w_mix[1], w_out[3], False,
                    final_g16=col48(final_g))
    out = np.empty((2, 8192, 2048), np.float32)
    for k in range(8):
        out[k // 4, (k % 4) * 2048:(k % 4 + 1) * 2048] = oT[k].T
    return out
```
